# Optimizing a Trainium2 kernel written in Bass

```python
import math
import jax, jax.numpy as jnp
from jax import lax
import numpy as np

D_MODEL = 1024
BATCH = 32
SEQ = 2048
DEPTH = 4

HEAD_DIM = 64
A_GROUPS = 12
A_WIDTH = A_GROUPS * HEAD_DIM
A_CHUNK = 128
B_BLOCKS = 12
B_WIDTH = 768
B_BLOCK_DIM = B_WIDTH // B_BLOCKS
B_CONV = 4
RG_C = 8.0
C_HEADS = 12
C_WIDTH = C_HEADS * HEAD_DIM
C_CONFIGS = ((128, 1), (512, 4), (2048, 16))
ATT_BLOCK = 128
N_BUCKETS = 32
MAX_DISTANCE = 2048
N_BRANCH = 3
D_FF = 2816
FFN_CONV = 3
EPS = 1e-6
NEG_INF = -1e30
IN_COLS = 2 * A_WIDTH + 2 * B_WIDTH + 3 * C_WIDTH + N_BRANCH * D_MODEL

kernel_name = "hybrid_gated_parallel_mixers"


def _rmsnorm(x, g):
    x32 = x.astype(jnp.float32)
    y = x32 * lax.rsqrt(jnp.mean(x32 * x32, axis=-1, keepdims=True) + EPS)
    return (y * g.astype(jnp.float32)).astype(x.dtype)


def _layernorm(x, g):
    x32 = x.astype(jnp.float32)
    mu = jnp.mean(x32, axis=-1, keepdims=True)
    var = jnp.mean(jnp.square(x32 - mu), axis=-1, keepdims=True)
    return ((x32 - mu) * lax.rsqrt(var + EPS) * g.astype(jnp.float32)).astype(x.dtype)


def _modulate(x, shift, scale):
    return x * (1.0 + scale[:, None, :]) + shift[:, None, :]


def _causal_dwconv(x, w, b):
    k, c = w.shape
    y = lax.conv_general_dilated(
        x, w[:, None, :].astype(x.dtype), window_strides=(1,), padding=[(k - 1, 0)],
        dimension_numbers=("NWC", "WIO", "NWC"), feature_group_count=c)
    return y + b.astype(x.dtype)


def _t5_bucket(dist):
    max_exact = N_BUCKETS // 2
    d = np.maximum(dist, 1).astype(np.float32)
    large = max_exact + (np.log(d / max_exact) / np.log(MAX_DISTANCE / max_exact)
                         * (N_BUCKETS - max_exact)).astype(np.int32)
    large = np.minimum(large, N_BUCKETS - 1)
    return np.where(dist < max_exact, dist, large).astype(np.int32)


def _mixer_a(u, v, ln_g, w_s, b_s):
    bsz, s, _ = v.shape
    u = jax.nn.gelu(u)
    v = _layernorm(jax.nn.gelu(v), ln_g)
    vr = v.reshape(bsz, s // A_CHUNK, A_CHUNK, A_GROUPS, HEAD_DIM)
    mask = jnp.tril(jnp.ones((A_CHUNK, A_CHUNK), v.dtype))
    sv = jnp.einsum('gts,bnsgc->bntgc', w_s * mask, vr) + b_s.T[None, None, :, :, None]
    return u * sv.reshape(bsz, s, A_WIDTH)


def _lru_combine(left, right):
    a1, b1 = left
    a2, b2 = right
    return a1 * a2, a2 * b1 + b2


def _mixer_b(xb, gb, conv_w, conv_b, wa, ba, wx, bx, lam):
    bsz, s, _ = xb.shape
    xb = _causal_dwconv(xb, conv_w, conv_b)
    xr = xb.reshape(bsz, s, B_BLOCKS, B_BLOCK_DIM)
    r = jax.nn.sigmoid(jnp.einsum('bshi,hij->bshj', xr, wa).reshape(bsz, s, B_WIDTH) + ba)
    i = jax.nn.sigmoid(jnp.einsum('bshi,hij->bshj', xr, wx).reshape(bsz, s, B_WIDTH) + bx)
    log_a = -RG_C * r.astype(jnp.float32) * jax.nn.softplus(-lam.astype(jnp.float32))
    a = jnp.exp(log_a)
    mult = jnp.sqrt(-jnp.expm1(2.0 * log_a))
    mult = mult.at[:, 0].set(1.0)
    xin = xb.astype(jnp.float32) * i.astype(jnp.float32) * mult
    _, h = lax.associative_scan(_lru_combine, (a, xin), axis=1)
    return h.astype(xb.dtype) * jax.nn.gelu(gb)


def _dilated_attn(q, k, v, window, dil, rel_table):
    bsz, s, h, e = q.shape
    L = s // dil
    nb = -(-L // ATT_BLOCK)
    lp = nb * ATT_BLOCK
    nw = window // dil

    def to_blocks(t):
        t = t.reshape(bsz, L, dil, h, e)
        t = jnp.pad(t, ((0, 0), (0, lp - L), (0, 0), (0, 0), (0, 0)))
        return t.reshape(bsz, nb, ATT_BLOCK, dil, h, e)

    qb, kb, vb = to_blocks(q), to_blocks(k), to_blocks(v)

    def with_prev(t):
        prev = jnp.concatenate([jnp.zeros_like(t[:, :1]), t[:, :-1]], axis=1)
        return jnp.concatenate([prev, t], axis=2)

    kc, vc = with_prev(kb), with_prev(vb)
    qi = np.arange(ATT_BLOCK)[:, None]
    kk = np.arange(2 * ATT_BLOCK)[None, :]
    dist = qi + ATT_BLOCK - kk
    band = (dist >= 0) & (dist <= nw)
    blk = np.arange(nb)[:, None, None]
    valid = band[None] & ((blk > 0) | (kk[None] >= ATT_BLOCK))
    bucket = _t5_bucket(np.maximum(dist, 0) * dil)
    bias = jnp.transpose(rel_table[jnp.asarray(bucket)], (2, 0, 1)).astype(jnp.float32)

    logits = jnp.einsum('bnidhe,bnkdhe->bndhik', qb, kc).astype(jnp.float32) + bias[None, None, None]
    logits = jnp.where(jnp.asarray(valid)[None, :, None, None], logits, NEG_INF)
    m = jnp.max(logits, axis=-1, keepdims=True)
    p = jnp.exp(logits - m)
    den = jnp.sum(p, axis=-1)
    o = jnp.einsum('bndhik,bnkdhe->bnidhe', p, vc.astype(jnp.float32))
    o = o / jnp.transpose(den, (0, 1, 4, 2, 3))[..., None]
    lse = jnp.transpose(m[..., 0] + jnp.log(den), (0, 1, 4, 2, 3))
    o = o.reshape(bsz, lp, dil, h, e)[:, :L].reshape(bsz, s, h, e)
    lse = lse.reshape(bsz, lp, dil, h)[:, :L].reshape(bsz, s, h)
    return o, lse


def _mixer_c(q, k, v, rel_table):
    bsz, s, _ = q.shape
    q = q.reshape(bsz, s, C_HEADS, HEAD_DIM) * (HEAD_DIM ** -0.5)
    k = k.reshape(bsz, s, C_HEADS, HEAD_DIM)
    v = v.reshape(bsz, s, C_HEADS, HEAD_DIM)
    outs, lses = [], []
    for window, dil in C_CONFIGS:
        o, l = _dilated_attn(q, k, v, window, dil, rel_table)
        outs.append(o)
        lses.append(l)
    wts = jax.nn.softmax(jnp.stack(lses, axis=0), axis=0)
    o = jnp.einsum('nbsh,nbshe->bshe', wts, jnp.stack(outs, axis=0))
    return o.reshape(bsz, s, C_WIDTH).astype(q.dtype)


def _ffn(x, w_gate, w_up, conv_w, conv_b, w_down):
    g = _causal_dwconv(x @ w_gate, conv_w, conv_b)
    return (jax.nn.gelu(g) * (x @ w_up)) @ w_down


def setup_inputs(seed: int = 0) -> dict:
    key = jax.random.key(seed)
    ks = jax.random.split(key, 32)
    f32 = jnp.float32

    def nrm(k, shape, fan_in):
        return jax.random.normal(k, shape, f32) * (fan_in ** -0.5)

    def small(k, shape, s=0.02):
        return jax.random.normal(k, shape, f32) * s

    a8 = jax.random.uniform(ks[15], (DEPTH, B_WIDTH), f32, 0.9, 0.999)
    a_base = a8 ** (1.0 / RG_C)
    lam = jnp.log(a_base) - jnp.log1p(-a_base)
    return {
        "x": jax.random.normal(ks[0], (BATCH, SEQ, D_MODEL), f32),
        "c": jax.random.normal(ks[1], (BATCH, D_MODEL), f32),
        "w_ada": nrm(ks[2], (DEPTH, D_MODEL, 6 * D_MODEL), D_MODEL) * 0.5,
        "b_ada": small(ks[3], (DEPTH, 6 * D_MODEL)),
        "norm1": 1.0 + small(ks[4], (DEPTH, D_MODEL)),
        "w_in": nrm(ks[5], (DEPTH, D_MODEL, IN_COLS), D_MODEL),
        "a_ln": 1.0 + small(ks[6], (DEPTH, A_WIDTH)),
        "a_ws": nrm(ks[7], (DEPTH, A_GROUPS, A_CHUNK, A_CHUNK), A_CHUNK),
        "a_bs": small(ks[8], (DEPTH, A_GROUPS, A_CHUNK), 0.1),
        "b_conv_w": nrm(ks[9], (DEPTH, B_CONV, B_WIDTH), B_CONV),
        "b_conv_b": small(ks[10], (DEPTH, B_WIDTH)),
        "b_wa": nrm(ks[11], (DEPTH, B_BLOCKS, B_BLOCK_DIM, B_BLOCK_DIM), B_BLOCK_DIM),
        "b_ba": small(ks[12], (DEPTH, B_WIDTH)),
        "b_wx": nrm(ks[13], (DEPTH, B_BLOCKS, B_BLOCK_DIM, B_BLOCK_DIM), B_BLOCK_DIM),
        "b_bx": small(ks[14], (DEPTH, B_WIDTH)),
        "b_lam": lam,
        "rel_table": small(ks[16], (N_BUCKETS, C_HEADS), 0.5),
        "p_a": nrm(ks[17], (DEPTH, A_WIDTH, D_MODEL), A_WIDTH),
        "p_b": nrm(ks[18], (DEPTH, B_WIDTH, D_MODEL), B_WIDTH),
        "p_c": nrm(ks[19], (DEPTH, C_WIDTH, D_MODEL), C_WIDTH),
        "w_out": nrm(ks[20], (DEPTH, D_MODEL, D_MODEL), D_MODEL),
        "norm2": 1.0 + small(ks[21], (DEPTH, D_MODEL)),
        "f_wgate": nrm(ks[22], (DEPTH, D_MODEL, D_FF), D_MODEL),
        "f_wup": nrm(ks[23], (DEPTH, D_MODEL, D_FF), D_MODEL),
        "f_conv_w": nrm(ks[24], (DEPTH, FFN_CONV, D_FF), FFN_CONV),
        "f_conv_b": small(ks[25], (DEPTH, D_FF)),
        "f_wdown": nrm(ks[26], (DEPTH, D_FF, D_MODEL), D_FF),
        "final_norm": 1.0 + small(ks[27], (D_MODEL,)),
    }


def reference(x, c, w_ada, b_ada, norm1, w_in, a_ln, a_ws, a_bs, b_conv_w, b_conv_b, b_wa, b_ba,
              b_wx, b_bx, b_lam, rel_table, p_a, p_b, p_c, w_out, norm2, f_wgate, f_wup,
              f_conv_w, f_conv_b, f_wdown, final_norm):
    bsz, s, _ = x.shape
    splits = [A_WIDTH, A_WIDTH, B_WIDTH, B_WIDTH, C_WIDTH, C_WIDTH, C_WIDTH]
    offsets = [int(o) for o in np.cumsum(splits)]
    cond = jax.nn.silu(c)
    for l in range(DEPTH):
        ada = cond @ w_ada[l] + b_ada[l]
        sh1, sc1, g1, sh2, sc2, g2 = jnp.split(ada, 6, axis=-1)

        xn = _modulate(_rmsnorm(x, norm1[l]), sh1, sc1)
        z = xn @ w_in[l]
        a_u, a_v, b_x, b_g, c_q, c_k, c_v, z_gate = jnp.split(z, offsets, axis=-1)
        out_a = _mixer_a(a_u, a_v, a_ln[l], a_ws[l], a_bs[l])
        out_b = _mixer_b(b_x, b_g, b_conv_w[l], b_conv_b[l], b_wa[l], b_ba[l], b_wx[l], b_bx[l], b_lam[l])
        out_c = _mixer_c(c_q, c_k, c_v, rel_table)
        gates = jax.nn.sigmoid(z_gate.reshape(bsz, s, N_BRANCH, D_MODEL))
        merged = (gates[:, :, 0] * (out_a @ p_a[l]) + gates[:, :, 1] * (out_b @ p_b[l])
                  + gates[:, :, 2] * (out_c @ p_c[l]))
        x = x + g1[:, None, :] * (merged @ w_out[l])

        xn = _modulate(_rmsnorm(x, norm2[l]), sh2, sc2)
        x = x + g2[:, None, :] * _ffn(xn, f_wgate[l], f_wup[l], f_conv_w[l], f_conv_b[l], f_wdown[l])
    return _rmsnorm(x, final_norm)
```

```python
import numpy as np
from contextlib import ExitStack, contextmanager
import concourse.bass as bass
import concourse.mybir as mybir
from concourse.bass_utils import run_bass_kernel_spmd

F32 = mybir.dt.float32
BF16 = mybir.dt.bfloat16
AF = mybir.ActivationFunctionType
ALU = mybir.AluOpType

D = 1024
KC = 8
NT = 2048
TG = 512
NTG = 4
DFF = 2816
NF = 22
AW = 768
IN_COLS = 8448
EPS = 1e-6
DEPTH = 4
BATCH = 32
NCORES = 8
SLOT = 3072
DILS = (1, 4, 16)
NGW = 385


def piece_table():
    names = []
    for hp in range(6):
        names.append((f"qkv{hp}", 3 * 8 * 128))
    for i in range(2):
        names.append((f"au{i}", 8 * 384))
    for i in range(2):
        names.append((f"av{i}", 4 * 768))
    for c in range(6):
        names.append((f"b{c}", 2 * 8 * 128))
    for c in range(8):
        names.append((f"gt{c}", 3 * 8 * 128))
        names.append((f"pp{c}", 3 * 6 * 128))
    for i, n in enumerate((384, 384, 256)):
        names.append((f"wo{i}", 8 * n))
    for f in range(NF):
        names.append((f"gu{f}", 2 * 8 * 128))
    for c in range(8):
        names.append((f"dn{c}", NF * 128))
    tab = {}
    off = 0
    for n, e in names:
        tab[n] = (off, e)
        off += 128 * e
    return tab, off


def _kpn(w, kc):
    n = w.shape[1]
    return w.reshape(kc, 128, n).transpose(1, 0, 2)


def host_wflat(l, w_in, p_a, p_b, p_c, w_out, f_wgate, f_wup, f_wdown):
    tab, tot = piece_table()
    out = np.empty((tot,), np.float32)

    def put(name, arr):
        off, e = tab[name]
        out[off:off + 128 * e] = np.ascontiguousarray(arr, dtype=np.float32).reshape(-1)

    wi = w_in[l]
    for hp in range(6):
        a = np.stack([_kpn(wi[:, 3072 + wch * 768 + hp * 128: 3072 + wch * 768 + (hp + 1) * 128], 8)
                      for wch in range(3)], axis=1)
        put(f"qkv{hp}", a)
    for i in range(2):
        put(f"au{i}", _kpn(wi[:, i * 384:(i + 1) * 384], 8))
    for i in range(2):
        put(f"av{i}", _kpn(wi[i * 512:(i + 1) * 512, 768:1536], 4))
    for c in range(6):
        a = np.stack([_kpn(wi[:, 1536 + c * 128:1536 + (c + 1) * 128], 8),
                      _kpn(wi[:, 2304 + c * 128:2304 + (c + 1) * 128], 8)], axis=1)
        put(f"b{c}", a)
    pp = (p_a[l], p_b[l], p_c[l])
    for c in range(8):
        a = np.stack([_kpn(wi[:, 5376 + X * 1024 + c * 128:5376 + X * 1024 + (c + 1) * 128], 8)
                      for X in range(3)], axis=1)
        put(f"gt{c}", a)
        a = np.stack([_kpn(pp[X][:, c * 128:(c + 1) * 128], 6) for X in range(3)], axis=1)
        put(f"pp{c}", a)
    c0 = 0
    for i, n in enumerate((384, 384, 256)):
        put(f"wo{i}", _kpn(w_out[l][:, c0:c0 + n], 8))
        c0 += n
    for f in range(NF):
        a = np.stack([_kpn(f_wgate[l][:, f * 128:(f + 1) * 128], 8),
                      _kpn(f_wup[l][:, f * 128:(f + 1) * 128], 8)], axis=1)
        put(f"gu{f}", a)
    for c in range(8):
        put(f"dn{c}", _kpn(f_wdown[l][:, c * 128:(c + 1) * 128], NF))
    return out


def vec_table(L):
    tab = {}
    off = 0
    for name, n in (("b_ada", L * 48), ("norm1", L * 8), ("norm2", L * 8), ("final", 8),
                    ("bcw", L * 4 * 6), ("bcb", L * 6), ("bba", L * 6), ("bbx", L * 6), ("blam", L * 6),
                    ("fcw", L * 3 * NF), ("fcb", L * NF)):
        tab[name] = off
        off += n
    return tab, off


def host_vecs(L, b_ada, norm1, norm2, final_norm, b_conv_w, b_conv_b, b_ba, b_bx, b_lam, f_conv_w, f_conv_b):
    tab, nv = vec_table(L)
    v = np.zeros((128, nv), np.float32)

    def fm(a):
        a = np.asarray(a, np.float32)
        lead = a.shape[:-1]
        c = a.shape[-1] // 128
        return a.reshape(lead + (c, 128)).reshape(-1, 128).T

    def put(name, a):
        m = fm(a)
        v[:, tab[name]:tab[name] + m.shape[1]] = m

    put("b_ada", b_ada[:L])
    put("norm1", norm1[:L])
    put("norm2", norm2[:L])
    put("final", final_norm)
    put("bcw", b_conv_w[:L])
    put("bcb", b_conv_b[:L])
    put("bba", b_ba[:L])
    put("bbx", b_bx[:L])
    put("blam", b_lam[:L])
    put("fcw", f_conv_w[:L])
    put("fcb", f_conv_b[:L])
    return v


def t5_bucket(dist):
    n_buckets, max_distance = 32, 2048
    max_exact = n_buckets // 2
    d = np.maximum(dist, 1).astype(np.float32)
    large = max_exact + (np.log(d / max_exact) / np.log(max_distance / max_exact)
                         * (n_buckets - max_exact)).astype(np.int32)
    large = np.minimum(large, n_buckets - 1)
    return np.where(dist < max_exact, dist, large).astype(np.int32)


def host_consts():
    oh = np.zeros((32, 3 * 129), np.float32)
    for ci, dil in enumerate(DILS):
        b = t5_bucket(np.arange(129) * dil)
        oh[b, ci * 129 + np.arange(129)] = 1.0
    triu = np.triu(np.ones((128, 128), np.float32))
    return oh, triu


class T:
    __slots__ = ("wr", "rd", "excl")

    def __init__(self, floor=None, excl=False):
        self.wr = {}
        self.rd = dict(floor) if floor else {}
        self.excl = excl


class Chan:
    def __init__(self, key, sem):
        self.key = key
        self.sem = sem
        self.count = 0


class Ctx:
    def __init__(self, nc, es):
        self.nc = nc
        self.es = es
        self.E = {"pe": nc.tensor, "act": nc.scalar, "dve": nc.vector, "pool": nc.gpsimd, "sp": nc.sync}
        self.semh = {}
        self.cnt = {}
        self.seen = {}
        for k in self.E:
            self.semh[k] = es.enter_context(nc.semaphore("s_" + k))
            self.cnt[k] = 0
            self.seen[k] = {}
        self.floor = {}
        self.uid = 0
        self.nwait = 0

    def newT(self):
        return T(self.floor)

    def chan(self, name):
        sem = self.es.enter_context(self.nc.semaphore("c_" + name))
        key = "c_" + name
        self.semh[key] = sem
        return Chan(key, sem)

    def _need(self, eng, key, val):
        if val <= 0:
            return
        if self.seen[eng].get(key, 0) < val:
            self.E[eng].wait_ge(self.semh[key], val)
            self.seen[eng][key] = val
            self.nwait += 1

    def _deps(self, eng, w, r):
        for t in r:
            for k, v in t.wr.items():
                self._need(eng, k, v)
            if t.excl:
                for k, v in t.rd.items():
                    if k != eng:
                        self._need(eng, k, v)
        for t in w:
            for k, v in t.wr.items():
                if k != eng:
                    self._need(eng, k, v)
            for k, v in t.rd.items():
                if k != eng:
                    self._need(eng, k, v)

    def op(self, eng, fn, w=(), r=()):
        self._deps(eng, w, r)
        inst = fn()
        self.cnt[eng] += 1
        c = self.cnt[eng]
        inst.then_inc(self.semh[eng], 1)
        for t in r:
            t.rd[eng] = c
        for t in w:
            t.wr[eng] = c
        return inst

    def dma(self, q, ch, out, in_, w=(), r=()):
        self._deps(q, w, r)
        inst = self.E[q].dma_start(out=out, in_=in_)
        ch.count += 16
        inst.then_inc(ch.sem, 16)
        for t in r:
            t.rd[ch.key] = ch.count
        for t in w:
            t.wr[ch.key] = ch.count
        return inst

    def snapshot_floor(self):
        self.floor = dict(self.cnt)

    def fence(self, eng, ts_, dummy):
        self._deps(eng, ts_, ts_)
        self.op(eng, lambda: self.E[eng].memset(dummy, 0.0), (), ())


def bc_last(ap, n):
    return bass.AP(ap.tensor, ap.offset, [list(x) for x in ap.ap] + [[0, n]])


def build_program(NS, L, dbg=False):
    nc = bass.Bass("TRN2", target_bir_lowering=False)
    ptab, WTOT = piece_table()
    vtab, NV = vec_table(L)

    xfm = nc.dram_tensor("xfm", [NS, 128, KC, NT], F32, kind="ExternalInput").ap()
    cT = nc.dram_tensor("cT", [128, KC, NS], F32, kind="ExternalInput").ap()
    wada = nc.dram_tensor("wada", [L, 24, 128, KC, 256], F32, kind="ExternalInput").ap()
    ident_d = nc.dram_tensor("ident", [8, 8], F32, kind="ExternalInput").ap()
    vecs = nc.dram_tensor("vecs", [128, NV], F32, kind="ExternalInput").ap()
    wflat = nc.dram_tensor("wflat", [L, WTOT], F32, kind="ExternalInput").ap()
    wsT_d = nc.dram_tensor("wsT", [L, 128, 12, 128], F32, kind="ExternalInput").ap()
    wbd_d = nc.dram_tensor("wbd", [L, 128, 2, 6, 128], F32, kind="ExternalInput").ap()
    bsb_d = nc.dram_tensor("bsb", [L, 128, 6, 128], F32, kind="ExternalInput").ap()
    lnb_d = nc.dram_tensor("lnb", [L, 128, AW], F32, kind="ExternalInput").ap()
    rel_d = nc.dram_tensor("rel", [32, 12], F32, kind="ExternalInput").ap()
    oh_d = nc.dram_tensor("oh", [32, 387], F32, kind="ExternalInput").ap()
    triu_d = nc.dram_tensor("triu", [128, 128], F32, kind="ExternalInput").ap()
    ofm = nc.dram_tensor("ofm", [NS, 128, KC, NT], F32, kind="ExternalOutput").ap()
    wbf_t = nc.dram_tensor("wbf", [L, WTOT], BF16, kind="Internal")
    wbf = wbf_t.ap()
    GROW = 36 * NGW
    gscr_t = nc.dram_tensor("gscr", [128, GROW], BF16, kind="Internal")
    gscr = gscr_t.ap()
    if dbg:
        dbg_d = nc.dram_tensor("dbg", [8, 128, KC, NT], F32, kind="ExternalOutput").ap()
        dbg2_d = nc.dram_tensor("dbg2", [128, 4096], F32, kind="ExternalOutput").ap()

    with ExitStack() as es:
        C = Ctx(nc, es)
        uid = [0]

        def sb(stack, shape, dt, name="t"):
            uid[0] += 1
            return stack.enter_context(nc.sbuf_tensor(f"{name}_{uid[0]}", list(shape), dt))

        @contextmanager
        def scope():
            with ExitStack() as s2:
                yield s2
            C.snapshot_floor()

        def mm(out, lhsT, rhs, start, stop, w, r):
            return C.op("pe", lambda: nc.tensor.matmul(out, lhsT, rhs, start=start, stop=stop), w, r)

        def act(out, in_, func, w, r, **kw):
            return C.op("act", lambda: nc.scalar.activation(out=out, in_=in_, func=func, **kw), w, r)

        def tt(eng, out, in0, in1, op, w, r):
            return C.op(eng, lambda: C.E[eng].tensor_tensor(out=out, in0=in0, in1=in1, op=op), w, r)

        def ts(eng, out, in0, s1, s2, op0, op1, w, r):
            if op1 is None:
                return C.op(eng, lambda: C.E[eng].tensor_scalar(out=out, in0=in0, scalar1=s1, scalar2=None, op0=op0), w, r)
            return C.op(eng, lambda: C.E[eng].tensor_scalar(out=out, in0=in0, scalar1=s1, scalar2=s2, op0=op0, op1=op1), w, r)

        def stt(out, in0, scalar, in1, op0, op1, w, r):
            return C.op("dve", lambda: nc.vector.scalar_tensor_tensor(out=out, in0=in0, scalar=scalar, in1=in1, op0=op0, op1=op1), w, r)

        def cp(eng, out, in_, w, r):
            return C.op(eng, lambda: C.E[eng].tensor_copy(out, in_), w, r)

        def mset(eng, ap, val, w):
            return C.op(eng, lambda: C.E[eng].memset(ap, val), w, ())

        psum = es.enter_context(nc.psum_tensor("psum", [128, 8, 512], F32))
        psT = [T(excl=True) for _ in range(8)]
        bank_rr = [0]
        bank_set = [list(range(8))]

        def bank():
            single_banks = bank_set[0]
            b = single_banks[bank_rr[0] % len(single_banks)]
            bank_rr[0] += 1
            return psum[:, b, :], psT[b]

        xn_sb = sb(es, [128, KC, NT], BF16, "xn")
        xnT = [C.newT() for _ in range(NTG)]
        oc_sb = sb(es, [128, 6, NT], BF16, "oc")
        ocT = C.newT()
        NSLOT = 3
        ring = [sb(es, [128, SLOT], BF16, "ring") for _ in range(NSLOT)]
        ringT = [C.newT() for _ in range(NSLOT)]
        ringC = [C.chan(f"ring{i}") for i in range(NSLOT)]
        ring_rr = [0]
        mskC = [C.chan(f"msk{i}") for i in range(2)]
        vec_sb = sb(es, [128, NV], F32, "vecs")
        vecT = C.newT()
        ada_sb = sb(es, [128, L, 48, NS], F32, "ada")
        adaT = C.newT()
        A_sb = sb(es, [128, L, 2, 8, NS], F32, "Asb")
        AT = C.newT()
        cl_sb = sb(es, [128, 2, L * 6], F32, "cl")
        clT = C.newT()
        ones_bf = sb(es, [128, 128], BF16, "ones")
        onesm_bf = sb(es, [128, 128], BF16, "onesm")
        onesT = C.newT()
        triu_sb = sb(es, [128, 128], F32, "triu")
        triuT = C.newT()
        sq_sb = [sb(es, [128, TG], BF16, "sq") for _ in range(2)]
        sqT = [C.newT() for _ in range(2)]
        rstd_sb = sb(es, [128, TG], F32, "rstd")
        rstdT = C.newT()
        ntmp_sb = [sb(es, [128, TG], F32, "ntmp") for _ in range(2)]
        ntmpT = [C.newT() for _ in range(2)]

        dummy_sb = sb(es, [128, 1], F32, "dummy")
        ROWS = WTOT // 2048
        RCH = 2048
        NCH = (ROWS + RCH - 1) // RCH
        wbfT = [[C.newT() for _ in range(NCH)] for _ in range(L)]
        gscrT = C.newT()
        ofmT = C.newT()
        constC = C.chan("const")
        precastC = [[C.chan(f"pc{i}_{j}") for j in range(NCH)] for i in range(L)]
        outC = C.chan("out")
        gscrC = C.chan("gscr")
        smallC = [C.chan(f"small{i}") for i in range(4)]
        wadaC = [C.chan(f"wada{i}") for i in range(2)]

        def vcol(name, idx):
            o = vtab[name] + idx
            return vec_sb[:, o:o + 1]

        def wpiece(l, name):
            off, e = ptab[name]
            i = ring_rr[0] % NSLOT
            ring_rr[0] += 1
            src = bass.AP(wbf_t, l * WTOT + off, [[e, 128], [1, e]])
            c0_, c1_ = off // (RCH * 2048), (off + 128 * e - 1) // (RCH * 2048)
            C.dma("sp", ringC[i], ring[i][:, 0:e], src, w=[ringT[i]], r=[wbfT[l][ci] for ci in range(c0_, c1_ + 1)])
            return ring[i], ringT[i]

        def precast_chunk(l, ci):
            if True:
                r0 = ci * RCH
                rn = min(RCH, ROWS - r0)
                src = bass.AP(wflat.tensor, l * WTOT + r0 * 2048, [[2048, rn], [1, 2048]])
                dst = bass.AP(wbf_t, l * WTOT + r0 * 2048, [[2048, rn], [1, 2048]])
                C.dma("pool", precastC[l][ci], dst, src, w=[wbfT[l][ci]])

        x_sb = sb(es, [128, KC, NT], F32, "x")
        xT = [C.newT() for _ in range(NTG)]
        xloadC = C.chan("xload")
        for c in range(KC):
            C.dma("pool", xloadC, x_sb[:, c, :], xfm[0, :, c, :], w=xT)
        for ci0 in range(NCH):
            precast_chunk(0, ci0)

        C.dma("sp", constC, vec_sb[:], vecs[:, :], w=[vecT])
        C.dma("sp", constC, triu_sb[:], triu_d[:, :], w=[triuT])
        mset("dve", ones_bf[:], 1.0, [onesT])
        mset("dve", onesm_bf[:], 1.0 / 1024.0, [onesT])

        with scope() as s0:
            ct_sb = sb(s0, [128, KC, NS], F32, "ct")
            ctT = C.newT()
            cond_sb = sb(s0, [128, KC, NS], F32, "cond")
            condT = C.newT()
            rel_sb = sb(s0, [32, 12], F32, "rel")
            relb_sb = sb(s0, [32, 12, 128], F32, "relb")
            oh_sb = sb(s0, [32, 387], F32, "oh")
            relT = C.newT()
            Gb = sb(s0, [128, 36, NGW], BF16, "Gb")
            GbT = C.newT()
            wa_sl = [sb(s0, [128, KC, 256], F32, "wasl") for _ in range(2)]
            ident_sb = sb(s0, [8, 8], F32, "ident")
            atm = [sb(s0, [8, 256], F32, "atm") for _ in range(2)]
            atmT = [C.newT() for _ in range(2)]
            waT = [C.newT() for _ in range(2)]
            tmpv = sb(s0, [128, L * 6], F32, "tmpv")
            tmpvT = C.newT()
            tmpa = sb(s0, [128, 8, NS], F32, "tmpa")
            tmpaT = C.newT()

            C.dma("sp", constC, ct_sb[:], cT[:, :, :], w=[ctT])
            C.dma("sp", constC, rel_sb[:], rel_d[:, :], w=[relT])
            C.dma("sp", constC, oh_sb[:], oh_d[:, :], w=[relT])
            C.dma("sp", constC, ident_sb[:], ident_d[:, :], w=[relT])
            for t_ in (vecT, triuT, ctT, relT):
                t_.wr[constC.key] = constC.count

            act(cond_sb[:], ct_sb[:], AF.Silu, [condT], [ctT])
            for l in range(L):
                for pc in range(24):
                    i = (l * 24 + pc) % 2
                    C.dma("sp", wadaC[i], wa_sl[i][:], wada[l, pc], w=[waT[i]])
                    pb, pt = bank()
                    for kc in range(KC):
                        mm(pb[0:NS, 0:256], cond_sb[:, kc, :], wa_sl[i][:, kc, :],
                           kc == 0, kc == KC - 1, [pt], [waT[i], condT])
                    cp("dve", atm[i][0:NS, :], pb[0:NS, 0:256], [atmT[i]], [pt])
                    for jj in range(2):
                        j = pc * 2 + jj
                        pb2, pt2 = bank()
                        mm(pb2[:, 0:NS], atm[i][0:NS, jj * 128:(jj + 1) * 128], ident_sb[0:NS, 0:NS], True, True,
                           [pt2], [atmT[i], relT])
                        ts("dve", ada_sb[:, l, j, :], pb2[:, 0:NS], vcol("b_ada", l * 48 + j), None, ALU.add, None,
                           [adaT], [pt2, vecT])
            for l in range(L):
                for wh, (part, nm) in enumerate(((1, "norm1"), (4, "norm2"))):
                    ts("dve", tmpa[:], ada_sb[:, l, part * 8:(part + 1) * 8, :], 1.0, None, ALU.add, None,
                       [tmpaT], [adaT])
                    o = vtab[nm] + l * 8
                    tt("dve", A_sb[:, l, wh, :, :], tmpa[:], bc_last(vec_sb[:, o:o + 8], NS), ALU.mult,
                       [AT], [tmpaT, vecT])
            o = vtab["blam"]
            act(tmpv[:], vec_sb[:, o:o + L * 6], AF.Exp, [tmpvT], [vecT], scale=-1.0)
            ts("dve", tmpv[:], tmpv[:], 1.0, None, ALU.add, None, [tmpvT], [tmpvT])
            act(tmpv[:], tmpv[:], AF.Ln, [tmpvT], [tmpvT])
            ts("dve", cl_sb[:, 0, :], tmpv[:], -8.0, None, ALU.mult, None, [clT], [tmpvT])
            ts("dve", cl_sb[:, 1, :], tmpv[:], -16.0, None, ALU.mult, None, [clT], [tmpvT])

            mset("pool", Gb[:], 0.0, [GbT])
            cp("dve", relb_sb[:], bc_last(rel_sb[:, :], 128), [relT], [relT])
            for h in range(12):
                pb, pt = bank()
                mm(pb[:, 0:387], relb_sb[:, h, :], oh_sb[:, :], True, True, [pt], [relT])
                act(Gb[:, h * 3:(h + 1) * 3, 128:257], pb[:, 0:387].rearrange("p (c d) -> p c d", c=3), AF.Exp,
                    [GbT], [pt])
            C.dma("sp", gscrC, gscr[:, :], Gb[:].rearrange("p a b -> p (a b)"), w=[gscrT], r=[GbT])
            C.fence("dve", [GbT], dummy_sb[:])

        def norm_tg(tg, Avec, shvec, dst_fn, dstT, final=False):
            tsl = slice(tg * TG, (tg + 1) * TG)
            pb, pt = bank()
            for c in range(KC):
                i = c % 2
                act(sq_sb[i][:], x_sb[:, c, tsl], AF.Square, [sqT[i]], [xT[tg]])
                mm(pb, onesm_bf[:], sq_sb[i][:], c == 0, c == KC - 1, [pt], [sqT[i], onesT])
            act(rstd_sb[:], pb, AF.Sqrt, [rstdT], [pt], bias=EPS)
            C.op("dve", lambda: nc.vector.reciprocal(rstd_sb[:], rstd_sb[:]), [rstdT], [rstdT])
            if dbg and not dbg_r[0]:
                dbg_r[0] = 1
                C.dma("sp", outC, dbg2_d[:, 3584:4096], rstd_sb[:], w=[ofmT], r=[rstdT])
            for c in range(KC):
                if final:
                    stt(x_sb[:, c, tsl], x_sb[:, c, tsl], Avec(c), rstd_sb[:], ALU.mult, ALU.mult,
                        [xT[tg]], [xT[tg], rstdT, vecT, AT])
                else:
                    i = c % 2
                    stt(ntmp_sb[i][:], x_sb[:, c, tsl], Avec(c), rstd_sb[:], ALU.mult, ALU.mult,
                        [ntmpT[i]], [xT[tg], rstdT, vecT, AT])
                    act(dst_fn(c), ntmp_sb[i][:], AF.Identity, [dstT], [ntmpT[i], adaT], bias=shvec(c), scale=1.0)


        def norm_steps(tg, Avec, shvec, dst_fn, dstT, final=False, nbank=7):
            tsl = slice(tg * TG, (tg + 1) * TG)
            pb, pt = psum[:, nbank, :], psT[nbank]

            def sqr(c):
                i = c % 2
                act(sq_sb[i][:], x_sb[:, c, tsl], AF.Square, [sqT[i]], [xT[tg]])

            def mmc(c):
                i = c % 2
                mm(pb, onesm_bf[:], sq_sb[i][:], c == 0, c == KC - 1, [pt], [sqT[i], onesT])

            def sttc(c):
                if final:
                    stt(x_sb[:, c, tsl], x_sb[:, c, tsl], Avec(c), rstd_sb[:], ALU.mult, ALU.mult,
                        [xT[tg]], [xT[tg], rstdT, vecT, AT])
                else:
                    i = c % 2
                    stt(ntmp_sb[i][:], x_sb[:, c, tsl], Avec(c), rstd_sb[:], ALU.mult, ALU.mult,
                        [ntmpT[i]], [xT[tg], rstdT, vecT, AT])

            def idc(c):
                if not final:
                    i = c % 2
                    act(dst_fn(c), ntmp_sb[i][:], AF.Identity, [dstT], [ntmpT[i], adaT], bias=shvec(c), scale=1.0)

            steps = []
            for j in range(5):
                def st_(j=j):
                    if j >= 1:
                        mmc(2 * j - 2)
                        mmc(2 * j - 1)
                    if j < 4:
                        sqr(2 * j)
                        sqr(2 * j + 1)
                steps.append(st_)
            steps.append(lambda: act(rstd_sb[:], pb, AF.Sqrt, [rstdT], [pt], bias=EPS))
            steps.append(lambda: C.op("dve", lambda: nc.vector.reciprocal(rstd_sb[:], rstd_sb[:]), [rstdT], [rstdT]))
            for j in range(5):
                def st2_(j=j):
                    if j >= 1:
                        idc(2 * j - 2)
                        idc(2 * j - 1)
                    if j < 4:
                        sttc(2 * j)
                        sttc(2 * j + 1)
                steps.append(st2_)
            return steps

        def tok(cfg, tile):
            dil = DILS[cfg]
            if cfg == 0:
                return slice(tile * 128, (tile + 1) * 128)
            if cfg == 1:
                r, n = tile // 4, tile % 4
                s0_ = n * 512 + r
                return slice(s0_, s0_ + 127 * 4 + 1, 4)
            r = tile
            return slice(r, r + 127 * 16 + 1, 16)

        dbg_i = [0]
        dbg_r = [0]
        marks = []

        def mark(name):
            marks.append((name, C.cnt["pe"]))

        def dump_x(s):
            if dbg and s == 0 and dbg_i[0] < 8:
                C.dma("sp", outC, dbg_d[dbg_i[0]], x_sb[:], w=[ofmT], r=xT)
                dbg_i[0] += 1

        def dump_bf(src, nchunk, rT):
            if dbg and dbg_i[0] < 8:
                with scope() as sd:
                    tmp = sb(sd, [128, NT], F32, "dbgt")
                    tT = C.newT()
                    for ch in range(nchunk):
                        cp("dve", tmp[:], src[:, ch, :], [tT], rT)
                        C.dma("sp", outC, dbg_d[dbg_i[0], :, ch, :], tmp[:], w=[ofmT], r=[tT])
                    C.fence("dve", [tT], dummy_sb[:])
                dbg_i[0] += 1

        if dbg:
            n1 = L * 48 * NS
            C.dma("sp", outC, dbg2_d[:, 0:n1], ada_sb[:].rearrange("p a b c -> p (a b c)"), w=[ofmT], r=[adaT])
            n2 = L * 16 * NS
            C.dma("sp", outC, dbg2_d[:, 1024:1024 + n2], A_sb[:].rearrange("p a b c d -> p (a b c d)"), w=[ofmT], r=[AT])
            C.dma("sp", outC, dbg2_d[:, 2048:2048 + 12 * L], cl_sb[:].rearrange("p a b -> p (a b)"), w=[ofmT], r=[clT])
            C.dma("sp", outC, dbg2_d[:, 2560:2560 + NV], vec_sb[:], w=[ofmT], r=[vecT])
        for s in range(NS):
            if s > 0:
                for c in range(KC):
                    C.dma("pool", xloadC, x_sb[:, c, :], xfm[s, :, c, :], w=xT)

            if dbg and s == 0:
                C.dma("sp", outC, dbg_d[7], x_sb[:], w=[ofmT], r=xT)
            for l in range(L):
                def adac(part, c):
                    return ada_sb[:, l, part * 8 + c, s:s + 1]

                if l == 0:
                    for tg in range(NTG):
                        norm_tg(tg, lambda c: A_sb[:, l, 0, c, s:s + 1], lambda c: adac(0, c),
                                lambda c, tg=tg: xn_sb[:, c, tg * TG:(tg + 1) * TG], xnT[tg])
                if s == 0 and l == 0:
                    dump_bf(xn_sb[:], 8, xnT)

                mark(f"s{s}l{l}.P2")
                def maybe_precast(k):
                    if s == 0 and l + 1 < L and k < NCH:
                        precast_chunk(l + 1, k)

                maybe_precast(0)
                bank_set[0] = [5, 6, 7]
                with scope() as s2:
                    q_sb = sb(s2, [128, 2, NT], BF16, "q")
                    k_sb = sb(s2, [128, NT], BF16, "k")
                    qT_, kT_ = C.newT(), C.newT()
                    mset("pool", q_sb[64:128, 0, :], 0.0, [qT_])
                    mset("pool", q_sb[0:64, 1, :], 0.0, [qT_])
                    V_sb = sb(s2, [128, 3, 16, 128], BF16, "V")
                    VT = C.newT()
                    acc = sb(s2, [128, 2, NT], F32, "acc")
                    accT = C.newT()
                    msk = [sb(s2, [128, 3, 2, 2, 128], BF16, "msk") for _ in range(2)]
                    mskT = [C.newT() for _ in range(2)]
                    pT = [sb(s2, [128, 2, 2, 128], BF16, "pT") for _ in range(3)]
                    pTT = [C.newT() for _ in range(3)]
                    for hp in range(6):
                        if hp == 2:
                            maybe_precast(1)
                        if hp == 4:
                            maybe_precast(2)
                        wq, wqT = wpiece(l, f"qkv{hp}")
                        wqv = wq[:, 0:3072].rearrange("p (a k n) -> p a k n", a=3, k=8)
                        mi = hp % 2
                        mo = (2 * hp) * 3 * NGW + 128
                        for h2m in range(2):
                            msrc = bass.AP(gscr_t, mo + h2m * 3 * NGW, [[GROW - 1, 128], [NGW, 3], [1, 256]])
                            C.dma("sp", mskC[mi], msk[mi][:, :, h2m, :, :].rearrange("p c j q -> p c (j q)"), msrc,
                                  w=[mskT[mi]], r=[gscrT])
                        for tg in range(NTG):
                            tsl = slice(tg * TG, (tg + 1) * TG)
                            pb, pt = bank()
                            for kc in range(KC):
                                mm(pb, wqv[:, 0, kc, :], xn_sb[:, kc, tsl], kc == 0, kc == KC - 1, [pt], [wqT, xnT[tg]])
                            act(q_sb[0:64, 0, tsl], pb[0:64, :], AF.Copy, [qT_], [pt], scale=0.125)
                            act(q_sb[64:128, 1, tsl], pb[64:128, :], AF.Copy, [qT_], [pt], scale=0.125)
                            pb, pt = bank()
                            for kc in range(KC):
                                mm(pb, wqv[:, 1, kc, :], xn_sb[:, kc, tsl], kc == 0, kc == KC - 1, [pt], [wqT, xnT[tg]])
                            act(k_sb[:, tsl], pb, AF.Copy, [kT_], [pt])
                        for cfg in range(3):
                            for grp in range(4):
                                pb, pt = bank()
                                pv = pb.rearrange("p (a n) -> p a n", a=4)
                                for ti in range(4):
                                    tile = grp * 4 + ti
                                    for kc in range(KC):
                                        mm(pv[:, ti, :], xn_sb[:, kc, tok(cfg, tile)], wqv[:, 2, kc, :],
                                           kc == 0, kc == KC - 1, [pt], [wqT] + xnT)
                                act(V_sb[:, cfg, grp * 4:(grp + 1) * 4, :], pv, AF.Copy, [VT], [pt])
                        blocks = []
                        for cfg in range(3):
                            for tile in range(16):
                                if cfg == 0:
                                    prev = tile - 1 if tile > 0 else None
                                elif cfg == 1:
                                    prev = tile - 1 if tile % 4 > 0 else None
                                else:
                                    prev = None
                                blocks.append((cfg, tile, prev))

                        def emit_S(bi):
                            cfg, tile, prev = blocks[bi]
                            nj = 2 if prev is not None else 1
                            sbk = bi % 3
                            pp2 = psum[:, sbk, :].rearrange("p (h j q) -> p h j q", h=2, j=2)
                            for j in range(nj):
                                kt = tile if j == 0 else prev
                                mm(pp2[:, :, j, :], k_sb[:, tok(cfg, kt)], q_sb[:, :, tok(cfg, tile)], True, True,
                                   [psT[sbk]], [kT_, qT_])
                            pi = bi % 3
                            act(pT[pi][:, :, 0:nj, :], pp2[:, :, 0:nj, :], AF.Exp, [pTT[pi]], [psT[sbk]])
                            tt("pool" if bi % 2 == 1 else "dve", pT[pi][:, :, 0:nj, :], pT[pi][:, :, 0:nj, :],
                               msk[mi][:, cfg, :, 0:nj, :], ALU.mult, [pTT[pi]], [pTT[pi], mskT[mi]])

                        def emit_PV(bi):
                            cfg, tile, prev = blocks[bi]
                            nj = 2 if prev is not None else 1
                            pi = bi % 3
                            pbk = 3 + bi % 2
                            po = psum[:, pbk, :].rearrange("p (a q) -> p a q", a=4)
                            pt = psT[pbk]
                            for h2 in range(2):
                                for j in range(nj):
                                    kt = tile if j == 0 else prev
                                    mm(po[:, h2, :], V_sb[:, cfg, kt, :], pT[pi][:, h2, j, :], j == 0, j == nj - 1,
                                       [pt], [VT, pTT[pi]])
                                for j in range(nj):
                                    mm(po[:, 2 + h2, :], ones_bf[:, :], pT[pi][:, h2, j, :], j == 0, j == nj - 1,
                                       [pt], [onesT, pTT[pi]])
                            for h2 in range(2):
                                hs = slice(h2 * 64, (h2 + 1) * 64)
                                dst = acc[hs, :, tok(cfg, tile)]
                                src = po[hs, h2:4:2, :]
                                if cfg == 0:
                                    cp("dve", dst, src, [accT], [pt])
                                else:
                                    tt("dve", dst, dst, src, ALU.add, [accT], [accT, pt])

                        emit_S(0)
                        emit_S(1)
                        for bi in range(len(blocks)):
                            if bi + 2 < len(blocks):
                                emit_S(bi + 2)
                            emit_PV(bi)
                        C.op("dve", lambda: nc.vector.reciprocal(acc[:, 1, :], acc[:, 1, :]), [accT], [accT])
                        tt("dve", oc_sb[:, hp, :], acc[:, 0, :], acc[:, 1, :], ALU.mult, [ocT], [accT])
                bank_set[0] = list(range(8))
                if s == 0 and l == 0:
                    dump_bf(oc_sb[:], 6, [ocT])

                mark(f"s{s}l{l}.P3")
                with scope() as s3:
                    oa_sb = sb(s3, [128, 6, TG], BF16, "oa")
                    ob_sb = sb(s3, [128, 6, TG], BF16, "ob")
                    oaT, obT = C.newT(), C.newT()
                    WmT = sb(s3, [128, 12, 128], BF16, "WmT")
                    WmTT = C.newT()
                    wbd = sb(s3, [128, 2, 6, 128], BF16, "wbd")
                    wbdT = C.newT()
                    bsb = sb(s3, [128, 6, 128], F32, "bsb")
                    bsbT = C.newT()
                    lnb = sb(s3, [128, AW], F32, "lnb")
                    lnbT = C.newT()
                    xh = sb(s3, [128, 6, 3], F32, "xh")
                    hc = sb(s3, [128, 6], F32, "hc")
                    carT = C.newT()
                    C.dma("pool", smallC[0], WmT[:], wsT_d[l], w=[WmTT])
                    C.dma("pool", smallC[1], wbd[:], wbd_d[l], w=[wbdT])
                    C.dma("sp", smallC[2], bsb[:], bsb_d[l], w=[bsbT])
                    C.dma("sp", smallC[3], lnb[:], lnb_d[l], w=[lnbT])
                    tt("pool", WmT[:], WmT[:], bass.AP(triu_sb[:].tensor, triu_sb[:].offset,
                                                       [list(triu_sb[:].ap[0]), [0, 12], [1, 128]]),
                       ALU.mult, [WmTT], [WmTT, triuT])
                    for tg in range(NTG):
                        tsl = slice(tg * TG, (tg + 1) * TG)
                        if tg == 0:
                            maybe_precast(3)
                        if tg == 2:
                            maybe_precast(4)
                        mark(f"s{s}l{l}.A{tg}")
                        with scope() as sa:
                            u_sb = sb(sa, [128, 6, TG], BF16, "u")
                            uT = C.newT()
                            gv = [sb(sa, [128, AW], F32, "gv") for _ in range(4)]
                            gvT = [C.newT() for _ in range(4)]
                            v_sb = [sb(sa, [128, AW], BF16, "v") for _ in range(2)]
                            vT = [C.newT() for _ in range(2)]
                            st = sb(sa, [128, 4, 2, 6], F32, "st")
                            mv = sb(sa, [128, 4, 2], F32, "mv")
                            rs = sb(sa, [128, 4], F32, "rs")
                            stT = C.newT()
                            atmp = [sb(sa, [128, 3, 128], F32, "atmp")] * 2
                            atmpT = [C.newT()] * 2
                            for i in range(2):
                                au, auT = wpiece(l, f"au{i}")
                                auv = au[:, 0:3072].rearrange("p (k n) -> p k n", k=8)
                                for cc in range(3):
                                    c = i * 3 + cc
                                    pb, pt = bank()
                                    for kc in range(KC):
                                        mm(pb, auv[:, kc, cc * 128:(cc + 1) * 128], xn_sb[:, kc, tsl], kc == 0, kc == KC - 1,
                                           [pt], [auT, xnT[tg]])
                                    act(u_sb[:, c, :], pb, AF.Gelu_apprx_tanh, [uT], [pt])
                            av0, av0T = wpiece(l, "av0")
                            av1, av1T = wpiece(l, "av1")
                            avv = [av0[:, 0:3072].rearrange("p (k n) -> p k n", k=4),
                                   av1[:, 0:3072].rearrange("p (k n) -> p k n", k=4)]
                            for ti in range(4):
                                tile = tg * 4 + ti
                                tks = slice(tile * 128, (tile + 1) * 128)
                                for half in range(2):
                                    pb, pt = bank()
                                    for kc in range(KC):
                                        mm(pb[:, 0:384], xn_sb[:, kc, tks], avv[kc // 4][:, kc % 4, half * 384:(half + 1) * 384],
                                           kc == 0, kc == KC - 1, [pt], [av0T, av1T, xnT[tg]])
                                    act(gv[ti][:, half * 384:(half + 1) * 384], pb[:, 0:384], AF.Gelu_apprx_tanh, [gvT[ti]], [pt])
                                    C.op("dve", lambda half=half, ti=ti: nc.vector.bn_stats(st[:, ti, half, :], gv[ti][:, half * 384:(half + 1) * 384]),
                                         [stT], [gvT[ti]])
                                C.op("dve", lambda ti=ti: nc.vector.bn_aggr(mv[:, ti, :], st[:, ti, :, :].rearrange("p a b -> p (a b)")), [stT], [stT])
                            act(rs[:], mv[:, :, 1], AF.Sqrt, [stT], [stT], bias=EPS)
                            C.op("dve", lambda: nc.vector.reciprocal(rs[:], rs[:]), [stT], [stT])
                            for ti in range(4):
                                b2 = ti % 2
                                ts("dve", gv[ti][:], gv[ti][:], mv[:, ti, 0:1], rs[:, ti:ti + 1], ALU.subtract, ALU.mult,
                                   [gvT[ti]], [gvT[ti], stT])
                                tt("dve", v_sb[b2][:], gv[ti][:], lnb[:], ALU.mult, [vT[b2]], [gvT[ti], lnbT])
                                for half in range(2):
                                    pb, pt = bank()
                                    for cc in range(3):
                                        c = half * 3 + cc
                                        for g2 in range(2):
                                            g = 2 * c + g2
                                            mm(pb[g2 * 64:(g2 + 1) * 64, cc * 128:(cc + 1) * 128], v_sb[b2][:, g * 64:(g + 1) * 64],
                                               WmT[:, g, :], True, True, [pt], [vT[b2], WmTT])
                                    tt("dve", atmp[half][:], pb[:, 0:384].rearrange("p (a t) -> p a t", a=3),
                                       bsb[:, half * 3:(half + 1) * 3, :], ALU.add, [atmpT[half]], [pt, bsbT])
                                    tt("pool", oa_sb[:, half * 3:(half + 1) * 3, ti * 128:(ti + 1) * 128], atmp[half][:],
                                       u_sb[:, half * 3:(half + 1) * 3, ti * 128:(ti + 1) * 128], ALU.mult,
                                       [oaT], [atmpT[half], uT])
                        mark(f"s{s}l{l}.B{tg}")
                        with scope() as sbb:
                            xbuf = [sb(sbb, [128, TG + 3], F32, "xbuf") for _ in range(2)]
                            xb = [sb(sbb, [128, TG], F32, "xb") for _ in range(2)]
                            xbb = [sb(sbb, [128, TG], BF16, "xbb") for _ in range(2)]
                            rbuf = [sb(sbb, [128, TG], F32, "rbuf") for _ in range(2)]
                            ibuf = [sb(sbb, [128, TG], F32, "ibuf") for _ in range(2)]
                            mbuf = [sb(sbb, [128, TG], F32, "mbuf") for _ in range(2)]
                            xbufT = [C.newT() for _ in range(2)]
                            xbT = [C.newT() for _ in range(2)]
                            xbbT = [C.newT() for _ in range(2)]
                            rT_ = [C.newT() for _ in range(2)]
                            iT_ = [C.newT() for _ in range(2)]
                            mT_ = [C.newT() for _ in range(2)]
                            ocw = vtab["bcw"] + l * 24
                            for cp_ in range(3):
                                pair = (2 * cp_, 2 * cp_ + 1)
                                st = {}
                                for c in pair:
                                    i = c % 2
                                    bw, bwT = wpiece(l, f"b{c}")
                                    bwv = bw[:, 0:2048].rearrange("p (a k n) -> p a k n", a=2, k=8)
                                    psx, psxT = bank()
                                    for kc in range(KC):
                                        mm(psx, bwv[:, 0, kc, :], xn_sb[:, kc, tsl], kc == 0, kc == KC - 1, [psxT], [bwT, xnT[tg]])
                                    psg, psgT = bank()
                                    for kc in range(KC):
                                        mm(psg, bwv[:, 1, kc, :], xn_sb[:, kc, tsl], kc == 0, kc == KC - 1, [psgT], [bwT, xnT[tg]])
                                    if tg == 0:
                                        mset("dve", xbuf[i][:, 0:3], 0.0, [xbufT[i]])
                                    else:
                                        cp("dve", xbuf[i][:, 0:3], xh[:, c, :], [xbufT[i]], [carT])
                                    act(xbuf[i][:, 3:TG + 3], psx, AF.Copy, [xbufT[i]], [psxT])
                                    act(xb[i][:], psx, AF.Identity, [xbT[i]], [psxT, vecT],
                                        bias=vcol("bcb", l * 6 + c), scale=vec_sb[:, ocw + 18 + c:ocw + 18 + c + 1])
                                    cp("dve", xh[:, c, :], xbuf[i][:, TG:TG + 3], [carT], [xbufT[i]])
                                    for k in range(0, 3):
                                        stt(xb[i][:], xbuf[i][:, k:k + TG], vec_sb[:, ocw + k * 6 + c:ocw + k * 6 + c + 1], xb[i][:],
                                            ALU.mult, ALU.add, [xbT[i]], [xbufT[i], xbT[i], vecT])
                                    act(xbb[i][:], xb[i][:], AF.Copy, [xbbT[i]], [xbT[i]])
                                    psr, psrT = bank()
                                    mm(psr, wbd[:, 0, c, :], xbb[i][:], True, True, [psrT], [wbdT, xbbT[i]])
                                    psi, psiT = bank()
                                    mm(psi, wbd[:, 1, c, :], xbb[i][:], True, True, [psiT], [wbdT, xbbT[i]])
                                    st[c] = (psg, psgT, psr, psrT, psi, psiT)
                                for c in pair:
                                    i = c % 2
                                    psg, psgT, psr, psrT, psi, psiT = st[c]
                                    act(rbuf[i][:], psr, AF.Sigmoid, [rT_[i]], [psrT, vecT], bias=vcol("bba", l * 6 + c), scale=1.0)
                                    act(ibuf[i][:], psi, AF.Sigmoid, [iT_[i]], [psiT, vecT], bias=vcol("bbx", l * 6 + c), scale=1.0)
                                for c in pair:
                                    i = c % 2
                                    act(mbuf[i][:], rbuf[i][:], AF.Exp, [mT_[i]], [rT_[i], clT], scale=cl_sb[:, 1, l * 6 + c:l * 6 + c + 1])
                                    act(rbuf[i][:], rbuf[i][:], AF.Exp, [rT_[i]], [rT_[i], clT], scale=cl_sb[:, 0, l * 6 + c:l * 6 + c + 1])
                                    ts("pool", mbuf[i][:], mbuf[i][:], -1.0, 1.0, ALU.mult, ALU.add, [mT_[i]], [mT_[i]])
                                    tt("pool", ibuf[i][:], ibuf[i][:], xb[i][:], ALU.mult, [iT_[i]], [iT_[i], xbT[i]])
                                for c in pair:
                                    i = c % 2
                                    act(mbuf[i][:], mbuf[i][:], AF.Sqrt, [mT_[i]], [mT_[i]])
                                    if tg == 0:
                                        mset("dve", mbuf[i][:, 0:1], 1.0, [mT_[i]])
                                    tt("dve", ibuf[i][:], ibuf[i][:], mbuf[i][:], ALU.mult, [iT_[i]], [iT_[i], mT_[i]])
                                    init = 0.0 if tg == 0 else hc[:, c:c + 1]
                                    C.op("dve", lambda init=init, i=i: nc.vector.tensor_tensor_scan(mbuf[i][:], rbuf[i][:], ibuf[i][:], init, ALU.mult, ALU.add),
                                         [mT_[i]], [mT_[i], rT_[i], iT_[i], carT])
                                    cp("dve", hc[:, c:c + 1], mbuf[i][:, TG - 1:TG], [carT], [mT_[i]])
                                for c in pair:
                                    i = c % 2
                                    psg, psgT, psr, psrT, psi, psiT = st[c]
                                    act(xb[i][:], psg, AF.Gelu_apprx_tanh, [xbT[i]], [psgT])
                                    tt("pool", ob_sb[:, c, :], mbuf[i][:], xb[i][:], ALU.mult, [obT], [mT_[i], xbT[i]])
                        mark(f"s{s}l{l}.M{tg}")
                        with scope() as sm:
                            gate = sb(sm, [128, 3, TG], F32, "gate")
                            gateT = C.newT()
                            mg = sb(sm, [128, 8, TG], BF16, "mg")
                            mgT = C.newT()
                            t0 = sb(sm, [128, TG], F32, "t0")
                            t1 = sb(sm, [128, TG], F32, "t1")
                            t0T, t1T = C.newT(), C.newT()
                            for c in range(8):
                                gt, gtT = wpiece(l, f"gt{c}")
                                gtv = gt[:, 0:3072].rearrange("p (a k n) -> p a k n", a=3, k=8)
                                for X in range(3):
                                    pb, pt = bank()
                                    for kc in range(KC):
                                        mm(pb, gtv[:, X, kc, :], xn_sb[:, kc, tsl], kc == 0, kc == KC - 1, [pt], [gtT, xnT[tg]])
                                    act(gate[:, X, :], pb, AF.Sigmoid, [gateT], [pt])
                                pw, pwT = wpiece(l, f"pp{c}")
                                pwv = pw[:, 0:2304].rearrange("p (a k n) -> p a k n", a=3, k=6)
                                srcs = ((lambda k: oa_sb[:, k, :], oaT), (lambda k: ob_sb[:, k, :], obT),
                                        (lambda k: oc_sb[:, k, tsl], ocT))
                                for X in range(3):
                                    pb, pt = bank()
                                    fsrc, sT_ = srcs[X]
                                    for k in range(6):
                                        mm(pb, pwv[:, X, k, :], fsrc(k), k == 0, k == 5, [pt], [pwT, sT_])
                                    if X == 0:
                                        tt("dve", t0[:], pb, gate[:, 0, :], ALU.mult, [t0T], [pt, gateT])
                                    elif X == 1:
                                        tt("dve", t1[:], pb, gate[:, 1, :], ALU.mult, [t1T], [pt, gateT])
                                        tt("pool", t0[:], t0[:], t1[:], ALU.add, [t0T], [t0T, t1T])
                                    else:
                                        tt("dve", t1[:], pb, gate[:, 2, :], ALU.mult, [t1T], [pt, gateT])
                                        tt("pool", mg[:, c, :], t0[:], t1[:], ALU.add, [mgT], [t0T, t1T])
                            c = 0
                            for i, n in enumerate((384, 384, 256)):
                                wo, woT = wpiece(l, f"wo{i}")
                                wov = wo[:, 0:8 * n].rearrange("p (k n) -> p k n", k=8)
                                for cc in range(n // 128):
                                    pb, pt = bank()
                                    for k in range(8):
                                        mm(pb, wov[:, k, cc * 128:(cc + 1) * 128], mg[:, k, :], k == 0, k == 7, [pt], [woT, mgT])
                                    stt(x_sb[:, c, tsl], pb, adac(2, c), x_sb[:, c, tsl], ALU.mult, ALU.add,
                                        [xT[tg]], [pt, xT[tg], adaT])
                                    c += 1
                dump_x(s) if l == 0 else None

                maybe_precast(5)
                maybe_precast(6)
                mark(f"s{s}l{l}.P4")
                with scope() as s4:
                    xn2b = [sb(s4, [128, KC, TG], BF16, "xn2") for _ in range(2)]
                    xn2Tb = [C.newT() for _ in range(2)]
                    gbuf = [sb(s4, [128, TG + 2], F32, "gbuf") for _ in range(2)]
                    gbufT = [C.newT() for _ in range(2)]
                    tb = [sb(s4, [128, TG], F32, "tb") for _ in range(2)]
                    tbT = [C.newT() for _ in range(2)]
                    ub = [sb(s4, [128, TG], BF16, "ub") for _ in range(2)]
                    ubT = [C.newT() for _ in range(2)]
                    hbuf = sb(s4, [128, NF, TG], BF16, "hbuf")
                    hbufT = [C.newT() for _ in range(NF)]
                    fh = sb(s4, [128, NF, 2], F32, "fh")
                    fhT = C.newT()
                    def next_norm(tgn, stepped):
                        fn = norm_steps if stepped else norm_tg
                        if l + 1 < L:
                            return fn(tgn, lambda c: A_sb[:, l + 1, 0, c, s:s + 1],
                                      lambda c: ada_sb[:, l + 1, c, s:s + 1],
                                      lambda c: xn_sb[:, c, tgn * TG:(tgn + 1) * TG], xnT[tgn])
                        return fn(tgn, lambda c: vcol("final", c), None, None, None, final=True)

                    bank_set[0] = list(range(7))

                    for tg in range(NTG):
                        tsl = slice(tg * TG, (tg + 1) * TG)
                        xn2, xn2T = xn2b[tg % 2], xn2Tb[tg % 2]
                        if tg == 0:
                            norm_tg(0, lambda c: A_sb[:, l, 1, c, s:s + 1], lambda c: adac(3, c),
                                    lambda c: xn2b[0][:, c, :], xn2Tb[0])
                        sched = {}
                        if tg + 1 < NTG:
                            n2 = norm_steps(tg + 1, lambda c: A_sb[:, l, 1, c, s:s + 1], lambda c: adac(3, c),
                                            lambda c, nb=(tg + 1) % 2: xn2b[nb][:, c, :], xn2Tb[(tg + 1) % 2])
                            for k_, st_ in enumerate(n2):
                                sched.setdefault(k_, []).append(st_)
                        if tg > 0:
                            nn = next_norm(tg - 1, True)
                            for k_, st_ in enumerate(nn):
                                sched.setdefault(10 + k_, []).append(st_)
                        for f in range(NF):
                            i = f % 2
                            for st_ in sched.get(f, []):
                                st_()
                            gu, guT = wpiece(l, f"gu{f}")
                            guv = gu[:, 0:2048].rearrange("p (a k n) -> p a k n", a=2, k=8)
                            psg, psgT = bank()
                            for kc in range(KC):
                                mm(psg, guv[:, 0, kc, :], xn2[:, kc, :], kc == 0, kc == KC - 1, [psgT], [guT, xn2T])
                            psu, psuT = bank()
                            for kc in range(KC):
                                mm(psu, guv[:, 1, kc, :], xn2[:, kc, :], kc == 0, kc == KC - 1, [psuT], [guT, xn2T])
                            if tg == 0:
                                mset("dve", gbuf[i][:, 0:2], 0.0, [gbufT[i]])
                            else:
                                cp("dve", gbuf[i][:, 0:2], fh[:, f, :], [gbufT[i]], [fhT])
                            o = vtab["fcw"] + l * 3 * NF
                            act(gbuf[i][:, 2:TG + 2], psg, AF.Copy, [gbufT[i]], [psgT])
                            act(tb[i][:], psg, AF.Identity, [tbT[i]], [psgT, vecT],
                                bias=vcol("fcb", l * NF + f), scale=vec_sb[:, o + 2 * NF + f:o + 2 * NF + f + 1])
                            cp("dve", fh[:, f, :], gbuf[i][:, TG:TG + 2], [fhT], [gbufT[i]])
                            for k in range(0, 2):
                                stt(tb[i][:], gbuf[i][:, k:k + TG], vec_sb[:, o + k * NF + f:o + k * NF + f + 1], tb[i][:],
                                    ALU.mult, ALU.add, [tbT[i]], [gbufT[i], tbT[i], vecT])
                            act(tb[i][:], tb[i][:], AF.Gelu_apprx_tanh, [tbT[i]], [tbT[i]])
                            act(ub[i][:], psu, AF.Copy, [ubT[i]], [psuT])
                            tt("pool", hbuf[:, f, :], tb[i][:], ub[i][:], ALU.mult, [hbufT[f]], [ubT[i], tbT[i]])
                        for c in range(8):
                            dn, dnT = wpiece(l, f"dn{c}")
                            dnv = dn[:, 0:NF * 128].rearrange("p (k n) -> p k n", k=NF)
                            pb, pt = bank()
                            for f in range(NF):
                                mm(pb, dnv[:, f, :], hbuf[:, f, :], f == 0, f == NF - 1, [pt], [dnT, hbufT[f]])
                            stt(x_sb[:, c, tsl], pb, adac(5, c), x_sb[:, c, tsl], ALU.mult, ALU.add,
                                [xT[tg]], [pt, xT[tg], adaT])
                        if tg == NTG - 1:
                            next_norm(tg, False)
                    bank_set[0] = list(range(8))
                dump_x(s) if l == 0 else None

            C.dma("sp", outC, ofm[s], x_sb[:], w=[ofmT], r=xT)

        C._need("sp", outC.key, outC.count)
        mark("end")
        build_program.stats = dict(C.cnt, nwait=C.nwait)
        build_program.marks = marks
    return nc


def prepare_inputs(NS, L, ncores, x, c, w_ada, b_ada, norm1, w_in, a_ln, a_ws, a_bs, b_conv_w, b_conv_b, b_wa, b_ba,
                   b_wx, b_bx, b_lam, rel_table, p_a, p_b, p_c, w_out, norm2, f_wgate, f_wup,
                   f_conv_w, f_conv_b, f_wdown, final_norm):
    f32 = np.float32
    x = np.asarray(x, f32)
    c = np.asarray(c, f32)
    args = [np.asarray(a, f32) for a in (w_in, p_a, p_b, p_c, w_out, f_wgate, f_wup, f_wdown)]
    wflat = np.stack([host_wflat(l, *args) for l in range(L)], axis=0)
    w_ada = np.asarray(w_ada, f32)
    wada = np.ascontiguousarray(w_ada[:L].reshape(L, KC, 128, 24, 256).transpose(0, 3, 2, 1, 4))
    vecs = host_vecs(L, np.asarray(b_ada, f32), np.asarray(norm1, f32), np.asarray(norm2, f32),
                     np.asarray(final_norm, f32), np.asarray(b_conv_w, f32), np.asarray(b_conv_b, f32),
                     np.asarray(b_ba, f32), np.asarray(b_bx, f32), np.asarray(b_lam, f32),
                     np.asarray(f_conv_w, f32), np.asarray(f_conv_b, f32))
    a_ws = np.asarray(a_ws, f32)
    wsT = np.ascontiguousarray(a_ws[:L].transpose(0, 3, 1, 2))
    b_wa = np.asarray(b_wa, f32)
    b_wx = np.asarray(b_wx, f32)
    wbd = np.zeros((L, 128, 2, 6, 128), f32)
    for wi_, wsrc in enumerate((b_wa, b_wx)):
        for cch in range(6):
            for g2 in range(2):
                wbd[:, g2 * 64:(g2 + 1) * 64, wi_, cch, g2 * 64:(g2 + 1) * 64] = wsrc[:L, 2 * cch + g2]
    a_bs = np.asarray(a_bs, f32)
    bsb = np.ascontiguousarray(np.repeat(a_bs[:L].reshape(L, 6, 2, 1, 128), 64, axis=3).reshape(L, 6, 128, 128)
                               .transpose(0, 2, 1, 3))
    lnb = np.ascontiguousarray(np.broadcast_to(np.asarray(a_ln, f32)[:L, None, :], (L, 128, AW)))
    oh, triu = host_consts()
    rel = np.ascontiguousarray(np.asarray(rel_table, f32))
    in_maps = []
    for k in range(ncores):
        xs = x[k * NS:(k + 1) * NS]
        xfm = np.ascontiguousarray(xs.reshape(NS, NT, KC, 128).transpose(0, 3, 2, 1))
        cs = c[k * NS:(k + 1) * NS]
        cTk = np.ascontiguousarray(cs.reshape(NS, KC, 128).transpose(2, 1, 0))
        in_maps.append({"xfm": xfm, "cT": cTk, "wada": wada, "vecs": vecs, "wflat": wflat, "wsT": wsT,
                        "wbd": wbd, "bsb": bsb, "lnb": lnb, "rel": rel, "oh": oh, "triu": triu,
                        "ident": np.eye(8, dtype=np.float32)})
    return in_maps


def gather_output(res, NS, ncores):
    outs = []
    for k in range(ncores):
        o = np.asarray(res.results[k]["ofm"])
        outs.append(o.transpose(0, 3, 2, 1).reshape(NS, NT, D))
    return np.concatenate(outs, axis=0).astype(np.float32)


def kernel(**inputs):
    NS = BATCH // NCORES
    in_maps = prepare_inputs(NS, DEPTH, NCORES, **inputs)
    nc = build_program(NS, DEPTH)
    res = run_bass_kernel_spmd(nc, in_maps, core_ids=list(range(NCORES)))
    return gather_output(res, NS, NCORES)
```

```python
import numpy as np
from contextlib import ExitStack, contextmanager
import concourse.bass as bass
import concourse.mybir as mybir
from concourse.bass_utils import run_bass_kernel_spmd

F32 = mybir.dt.float32
BF16 = mybir.dt.bfloat16
AF = mybir.ActivationFunctionType
ALU = mybir.AluOpType

D = 1024
KC = 8
NT = 2048
TG = 512
NTG = 4
DFF = 2816
NF = 22
AW = 768
IN_COLS = 8448
EPS = 1e-6
DEPTH = 4
BATCH = 32
NCORES = 8
SLOT = 3072
DILS = (1, 4, 16)
NGW = 385


def piece_table():
    names = []
    for hp in range(6):
        names.append((f"qkv{hp}", 3 * 8 * 128))
    for i in range(2):
        names.append((f"au{i}", 8 * 384))
    for i in range(2):
        names.append((f"av{i}", 4 * 768))
    for c in range(6):
        names.append((f"b{c}", 2 * 8 * 128))
    for c in range(8):
        names.append((f"gt{c}", 3 * 8 * 128))
        names.append((f"pp{c}", 3 * 6 * 128))
    for i, n in enumerate((384, 384, 256)):
        names.append((f"wo{i}", 8 * n))
    for f in range(NF):
        names.append((f"gu{f}", 2 * 8 * 128))
    for c in range(8):
        names.append((f"dn{c}", NF * 128))
    tab = {}
    off = 0
    for n, e in names:
        tab[n] = (off, e)
        off += 128 * e
    return tab, off


def _kpn(w, kc):
    n = w.shape[1]
    return w.reshape(kc, 128, n).transpose(1, 0, 2)


def host_wflat(l, w_in, p_a, p_b, p_c, w_out, f_wgate, f_wup, f_wdown):
    tab, tot = piece_table()
    out = np.empty((tot,), np.float32)

    def put(name, arr):
        off, e = tab[name]
        out[off:off + 128 * e] = np.ascontiguousarray(arr, dtype=np.float32).reshape(-1)

    wi = w_in[l]
    for hp in range(6):
        a = np.stack([_kpn(wi[:, 3072 + wch * 768 + hp * 128: 3072 + wch * 768 + (hp + 1) * 128], 8)
                      for wch in range(3)], axis=1)
        put(f"qkv{hp}", a)
    for i in range(2):
        put(f"au{i}", _kpn(wi[:, i * 384:(i + 1) * 384], 8))
    for i in range(2):
        put(f"av{i}", _kpn(wi[i * 512:(i + 1) * 512, 768:1536], 4))
    for c in range(6):
        a = np.stack([_kpn(wi[:, 1536 + c * 128:1536 + (c + 1) * 128], 8),
                      _kpn(wi[:, 2304 + c * 128:2304 + (c + 1) * 128], 8)], axis=1)
        put(f"b{c}", a)
    pp = (p_a[l], p_b[l], p_c[l])
    for c in range(8):
        a = np.stack([_kpn(wi[:, 5376 + X * 1024 + c * 128:5376 + X * 1024 + (c + 1) * 128], 8)
                      for X in range(3)], axis=1)
        put(f"gt{c}", a)
        a = np.stack([_kpn(pp[X][:, c * 128:(c + 1) * 128], 6) for X in range(3)], axis=1)
        put(f"pp{c}", a)
    c0 = 0
    for i, n in enumerate((384, 384, 256)):
        put(f"wo{i}", _kpn(w_out[l][:, c0:c0 + n], 8))
        c0 += n
    for f in range(NF):
        a = np.stack([_kpn(f_wgate[l][:, f * 128:(f + 1) * 128], 8),
                      _kpn(f_wup[l][:, f * 128:(f + 1) * 128], 8)], axis=1)
        put(f"gu{f}", a)
    for c in range(8):
        put(f"dn{c}", _kpn(f_wdown[l][:, c * 128:(c + 1) * 128], NF))
    return out


def vec_table(L):
    tab = {}
    off = 0
    for name, n in (("b_ada", L * 48), ("norm1", L * 8), ("norm2", L * 8), ("final", 8),
                    ("bcw", L * 4 * 6), ("bcb", L * 6), ("bba", L * 6), ("bbx", L * 6), ("blam", L * 6),
                    ("fcw", L * 3 * NF), ("fcb", L * NF)):
        tab[name] = off
        off += n
    return tab, off


def host_vecs(L, b_ada, norm1, norm2, final_norm, b_conv_w, b_conv_b, b_ba, b_bx, b_lam, f_conv_w, f_conv_b):
    tab, nv = vec_table(L)
    v = np.zeros((128, nv), np.float32)

    def fm(a):
        a = np.asarray(a, np.float32)
        lead = a.shape[:-1]
        c = a.shape[-1] // 128
        return a.reshape(lead + (c, 128)).reshape(-1, 128).T

    def put(name, a):
        m = fm(a)
        v[:, tab[name]:tab[name] + m.shape[1]] = m

    put("b_ada", b_ada[:L])
    put("norm1", norm1[:L])
    put("norm2", norm2[:L])
    put("final", final_norm)
    put("bcw", b_conv_w[:L])
    put("bcb", b_conv_b[:L])
    put("bba", b_ba[:L])
    put("bbx", b_bx[:L])
    put("blam", b_lam[:L])
    put("fcw", f_conv_w[:L])
    put("fcb", f_conv_b[:L])
    return v


def t5_bucket(dist):
    n_buckets, max_distance = 32, 2048
    max_exact = n_buckets // 2
    d = np.maximum(dist, 1).astype(np.float32)
    large = max_exact + (np.log(d / max_exact) / np.log(max_distance / max_exact)
                         * (n_buckets - max_exact)).astype(np.int32)
    large = np.minimum(large, n_buckets - 1)
    return np.where(dist < max_exact, dist, large).astype(np.int32)


def host_consts():
    oh = np.zeros((32, 3 * 129), np.float32)
    for ci, dil in enumerate(DILS):
        b = t5_bucket(np.arange(129) * dil)
        oh[b, ci * 129 + np.arange(129)] = 1.0
    triu = np.triu(np.ones((128, 128), np.float32))
    return oh, triu


class T:
    __slots__ = ("wr", "rd", "excl")

    def __init__(self, floor=None, excl=False):
        self.wr = {}
        self.rd = dict(floor) if floor else {}
        self.excl = excl


class Chan:
    def __init__(self, key, sem):
        self.key = key
        self.sem = sem
        self.count = 0


class Ctx:
    def __init__(self, nc, es):
        self.nc = nc
        self.es = es
        self.E = {"pe": nc.tensor, "act": nc.scalar, "dve": nc.vector, "pool": nc.gpsimd, "sp": nc.sync}
        self.semh = {}
        self.cnt = {}
        self.seen = {}
        for k in self.E:
            self.semh[k] = es.enter_context(nc.semaphore("s_" + k))
            self.cnt[k] = 0
            self.seen[k] = {}
        self.floor = {}
        self.uid = 0
        self.nwait = 0

    def newT(self):
        return T(self.floor)

    def chan(self, name):
        sem = self.es.enter_context(self.nc.semaphore("c_" + name))
        key = "c_" + name
        self.semh[key] = sem
        return Chan(key, sem)

    def _need(self, eng, key, val):
        if val <= 0:
            return
        if self.seen[eng].get(key, 0) < val:
            self.E[eng].wait_ge(self.semh[key], val)
            self.seen[eng][key] = val
            self.nwait += 1

    def _deps(self, eng, w, r):
        for t in r:
            for k, v in t.wr.items():
                self._need(eng, k, v)
            if t.excl:
                for k, v in t.rd.items():
                    if k != eng:
                        self._need(eng, k, v)
        for t in w:
            for k, v in t.wr.items():
                if k != eng:
                    self._need(eng, k, v)
            for k, v in t.rd.items():
                if k != eng:
                    self._need(eng, k, v)

    def op(self, eng, fn, w=(), r=()):
        self._deps(eng, w, r)
        inst = fn()
        self.cnt[eng] += 1
        c = self.cnt[eng]
        inst.then_inc(self.semh[eng], 1)
        for t in r:
            t.rd[eng] = c
        for t in w:
            t.wr[eng] = c
        return inst

    def dma(self, q, ch, out, in_, w=(), r=()):
        self._deps(q, w, r)
        inst = self.E[q].dma_start(out=out, in_=in_)
        ch.count += 16
        inst.then_inc(ch.sem, 16)
        for t in r:
            t.rd[ch.key] = ch.count
        for t in w:
            t.wr[ch.key] = ch.count
        return inst

    def snapshot_floor(self):
        self.floor = dict(self.cnt)

    def fence(self, eng, ts_, dummy):
        self._deps(eng, ts_, ts_)
        self.op(eng, lambda: self.E[eng].memset(dummy, 0.0), (), ())


def bc_last(ap, n):
    return bass.AP(ap.tensor, ap.offset, [list(x) for x in ap.ap] + [[0, n]])


def build_program(NS, L, dbg=False):
    nc = bass.Bass("TRN2", target_bir_lowering=False)
    ptab, WTOT = piece_table()
    vtab, NV = vec_table(L)

    xfm = nc.dram_tensor("xfm", [NS, 128, KC, NT], F32, kind="ExternalInput").ap()
    cT = nc.dram_tensor("cT", [128, KC, NS], F32, kind="ExternalInput").ap()
    wada = nc.dram_tensor("wada", [L, 24, 128, KC, 256], F32, kind="ExternalInput").ap()
    ident_d = nc.dram_tensor("ident", [8, 8], F32, kind="ExternalInput").ap()
    vecs = nc.dram_tensor("vecs", [128, NV], F32, kind="ExternalInput").ap()
    wflat = nc.dram_tensor("wflat", [L, WTOT], F32, kind="ExternalInput").ap()
    wsT_d = nc.dram_tensor("wsT", [L, 128, 12, 128], F32, kind="ExternalInput").ap()
    wbd_d = nc.dram_tensor("wbd", [L, 128, 2, 6, 128], F32, kind="ExternalInput").ap()
    bsb_d = nc.dram_tensor("bsb", [L, 128, 6, 128], F32, kind="ExternalInput").ap()
    lnb_d = nc.dram_tensor("lnb", [L, 128, AW], F32, kind="ExternalInput").ap()
    rel_d = nc.dram_tensor("rel", [32, 12], F32, kind="ExternalInput").ap()
    oh_d = nc.dram_tensor("oh", [32, 387], F32, kind="ExternalInput").ap()
    triu_d = nc.dram_tensor("triu", [128, 128], F32, kind="ExternalInput").ap()
    ofm = nc.dram_tensor("ofm", [NS, 128, KC, NT], F32, kind="ExternalOutput").ap()
    wbf_t = nc.dram_tensor("wbf", [L, WTOT], BF16, kind="Internal")
    wbf = wbf_t.ap()
    GROW = 36 * NGW
    gscr_t = nc.dram_tensor("gscr", [128, GROW], BF16, kind="Internal")
    gscr = gscr_t.ap()
    if dbg:
        dbg_d = nc.dram_tensor("dbg", [8, 128, KC, NT], F32, kind="ExternalOutput").ap()
        dbg2_d = nc.dram_tensor("dbg2", [128, 4096], F32, kind="ExternalOutput").ap()

    with ExitStack() as es:
        C = Ctx(nc, es)
        uid = [0]

        def sb(stack, shape, dt, name="t"):
            uid[0] += 1
            return stack.enter_context(nc.sbuf_tensor(f"{name}_{uid[0]}", list(shape), dt))

        @contextmanager
        def scope():
            with ExitStack() as s2:
                yield s2
            C.snapshot_floor()

        def mm(out, lhsT, rhs, start, stop, w, r):
            return C.op("pe", lambda: nc.tensor.matmul(out, lhsT, rhs, start=start, stop=stop), w, r)

        def act(out, in_, func, w, r, **kw):
            return C.op("act", lambda: nc.scalar.activation(out=out, in_=in_, func=func, **kw), w, r)

        def tt(eng, out, in0, in1, op, w, r):
            return C.op(eng, lambda: C.E[eng].tensor_tensor(out=out, in0=in0, in1=in1, op=op), w, r)

        def ts(eng, out, in0, s1, s2, op0, op1, w, r):
            if op1 is None:
                return C.op(eng, lambda: C.E[eng].tensor_scalar(out=out, in0=in0, scalar1=s1, scalar2=None, op0=op0), w, r)
            return C.op(eng, lambda: C.E[eng].tensor_scalar(out=out, in0=in0, scalar1=s1, scalar2=s2, op0=op0, op1=op1), w, r)

        def stt(out, in0, scalar, in1, op0, op1, w, r):
            return C.op("dve", lambda: nc.vector.scalar_tensor_tensor(out=out, in0=in0, scalar=scalar, in1=in1, op0=op0, op1=op1), w, r)

        def cp(eng, out, in_, w, r):
            return C.op(eng, lambda: C.E[eng].tensor_copy(out, in_), w, r)

        def mset(eng, ap, val, w):
            return C.op(eng, lambda: C.E[eng].memset(ap, val), w, ())

        psum = es.enter_context(nc.psum_tensor("psum", [128, 8, 512], F32))
        psT = [T(excl=True) for _ in range(8)]
        bank_rr = [0]
        bank_set = [list(range(8))]

        def bank():
            single_banks = bank_set[0]
            b = single_banks[bank_rr[0] % len(single_banks)]
            bank_rr[0] += 1
            return psum[:, b, :], psT[b]

        xn_sb = sb(es, [128, KC, NT], BF16, "xn")
        xnT = [C.newT() for _ in range(NTG)]
        oc_sb = sb(es, [128, 6, NT], BF16, "oc")
        ocT = C.newT()
        NSLOT = 3
        ring = [sb(es, [128, SLOT], BF16, "ring") for _ in range(NSLOT)]
        ringT = [C.newT() for _ in range(NSLOT)]
        ringC = [C.chan(f"ring{i}") for i in range(NSLOT)]
        ring_rr = [0]
        mskC = [C.chan(f"msk{i}") for i in range(2)]
        vec_sb = sb(es, [128, NV], F32, "vecs")
        vecT = C.newT()
        ada_sb = sb(es, [128, L, 48, NS], F32, "ada")
        adaT = C.newT()
        A_sb = sb(es, [128, L, 2, 8, NS], F32, "Asb")
        AT = C.newT()
        cl_sb = sb(es, [128, 2, L * 6], F32, "cl")
        clT = C.newT()
        ones_bf = sb(es, [128, 128], BF16, "ones")
        onesm_bf = sb(es, [128, 128], BF16, "onesm")
        onesT = C.newT()
        triu_sb = sb(es, [128, 128], F32, "triu")
        triuT = C.newT()
        sq_sb = [sb(es, [128, TG], BF16, "sq") for _ in range(2)]
        sqT = [C.newT() for _ in range(2)]
        rstd_sb = sb(es, [128, TG], F32, "rstd")
        rstdT = C.newT()
        ntmp_sb = [sb(es, [128, TG], F32, "ntmp") for _ in range(2)]
        ntmpT = [C.newT() for _ in range(2)]

        dummy_sb = sb(es, [128, 1], F32, "dummy")
        ROWS = WTOT // 2048
        RCH = 2048
        NCH = (ROWS + RCH - 1) // RCH
        wbfT = [[C.newT() for _ in range(NCH)] for _ in range(L)]
        gscrT = C.newT()
        ofmT = C.newT()
        constC = C.chan("const")
        precastC = [[C.chan(f"pc{i}_{j}") for j in range(NCH)] for i in range(L)]
        outC = C.chan("out")
        gscrC = C.chan("gscr")
        smallC = [C.chan(f"small{i}") for i in range(4)]
        wadaC = [C.chan(f"wada{i}") for i in range(2)]

        def vcol(name, idx):
            o = vtab[name] + idx
            return vec_sb[:, o:o + 1]

        def wpiece(l, name):
            off, e = ptab[name]
            i = ring_rr[0] % NSLOT
            ring_rr[0] += 1
            src = bass.AP(wbf_t, l * WTOT + off, [[e, 128], [1, e]])
            c0_, c1_ = off // (RCH * 2048), (off + 128 * e - 1) // (RCH * 2048)
            C.dma("sp", ringC[i], ring[i][:, 0:e], src, w=[ringT[i]], r=[wbfT[l][ci] for ci in range(c0_, c1_ + 1)])
            return ring[i], ringT[i]

        def precast_chunk(l, ci):
            if True:
                r0 = ci * RCH
                rn = min(RCH, ROWS - r0)
                src = bass.AP(wflat.tensor, l * WTOT + r0 * 2048, [[2048, rn], [1, 2048]])
                dst = bass.AP(wbf_t, l * WTOT + r0 * 2048, [[2048, rn], [1, 2048]])
                C.dma("pool", precastC[l][ci], dst, src, w=[wbfT[l][ci]])

        x_sb = sb(es, [128, KC, NT], F32, "x")
        xT = [C.newT() for _ in range(NTG)]
        xloadC = C.chan("xload")
        for c in range(KC):
            C.dma("pool", xloadC, x_sb[:, c, :], xfm[0, :, c, :], w=xT)
        for ci0 in range(NCH):
            precast_chunk(0, ci0)

        C.dma("sp", constC, vec_sb[:], vecs[:, :], w=[vecT])
        C.dma("sp", constC, triu_sb[:], triu_d[:, :], w=[triuT])
        mset("dve", ones_bf[:], 1.0, [onesT])
        mset("dve", onesm_bf[:], 1.0 / 1024.0, [onesT])

        with scope() as s0:
            ct_sb = sb(s0, [128, KC, NS], F32, "ct")
            ctT = C.newT()
            cond_sb = sb(s0, [128, KC, NS], F32, "cond")
            condT = C.newT()
            rel_sb = sb(s0, [32, 12], F32, "rel")
            relb_sb = sb(s0, [32, 12, 128], F32, "relb")
            oh_sb = sb(s0, [32, 387], F32, "oh")
            relT = C.newT()
            Gb = sb(s0, [128, 36, NGW], BF16, "Gb")
            GbT = C.newT()
            wa_sl = [sb(s0, [128, KC, 256], F32, "wasl") for _ in range(2)]
            ident_sb = sb(s0, [8, 8], F32, "ident")
            atm = [sb(s0, [8, 256], F32, "atm") for _ in range(2)]
            atmT = [C.newT() for _ in range(2)]
            waT = [C.newT() for _ in range(2)]
            tmpv = sb(s0, [128, L * 6], F32, "tmpv")
            tmpvT = C.newT()
            tmpa = sb(s0, [128, 8, NS], F32, "tmpa")
            tmpaT = C.newT()

            C.dma("sp", constC, ct_sb[:], cT[:, :, :], w=[ctT])
            C.dma("sp", constC, rel_sb[:], rel_d[:, :], w=[relT])
            C.dma("sp", constC, oh_sb[:], oh_d[:, :], w=[relT])
            C.dma("sp", constC, ident_sb[:], ident_d[:, :], w=[relT])
            for t_ in (vecT, triuT, ctT, relT):
                t_.wr[constC.key] = constC.count

            act(cond_sb[:], ct_sb[:], AF.Silu, [condT], [ctT])
            for l in range(L):
                for pc in range(24):
                    i = (l * 24 + pc) % 2
                    C.dma("sp", wadaC[i], wa_sl[i][:], wada[l, pc], w=[waT[i]])
                    pb, pt = bank()
                    for kc in range(KC):
                        mm(pb[0:NS, 0:256], cond_sb[:, kc, :], wa_sl[i][:, kc, :],
                           kc == 0, kc == KC - 1, [pt], [waT[i], condT])
                    cp("dve", atm[i][0:NS, :], pb[0:NS, 0:256], [atmT[i]], [pt])
                    for jj in range(2):
                        j = pc * 2 + jj
                        pb2, pt2 = bank()
                        mm(pb2[:, 0:NS], atm[i][0:NS, jj * 128:(jj + 1) * 128], ident_sb[0:NS, 0:NS], True, True,
                           [pt2], [atmT[i], relT])
                        ts("dve", ada_sb[:, l, j, :], pb2[:, 0:NS], vcol("b_ada", l * 48 + j), None, ALU.add, None,
                           [adaT], [pt2, vecT])
            for l in range(L):
                for wh, (part, nm) in enumerate(((1, "norm1"), (4, "norm2"))):
                    ts("dve", tmpa[:], ada_sb[:, l, part * 8:(part + 1) * 8, :], 1.0, None, ALU.add, None,
                       [tmpaT], [adaT])
                    o = vtab[nm] + l * 8
                    tt("dve", A_sb[:, l, wh, :, :], tmpa[:], bc_last(vec_sb[:, o:o + 8], NS), ALU.mult,
                       [AT], [tmpaT, vecT])
            o = vtab["blam"]
            act(tmpv[:], vec_sb[:, o:o + L * 6], AF.Exp, [tmpvT], [vecT], scale=-1.0)
            ts("dve", tmpv[:], tmpv[:], 1.0, None, ALU.add, None, [tmpvT], [tmpvT])
            act(tmpv[:], tmpv[:], AF.Ln, [tmpvT], [tmpvT])
            ts("dve", cl_sb[:, 0, :], tmpv[:], -8.0, None, ALU.mult, None, [clT], [tmpvT])
            ts("dve", cl_sb[:, 1, :], tmpv[:], -16.0, None, ALU.mult, None, [clT], [tmpvT])

            mset("pool", Gb[:], 0.0, [GbT])
            cp("dve", relb_sb[:], bc_last(rel_sb[:, :], 128), [relT], [relT])
            for h in range(12):
                pb, pt = bank()
                mm(pb[:, 0:387], relb_sb[:, h, :], oh_sb[:, :], True, True, [pt], [relT])
                act(Gb[:, h * 3:(h + 1) * 3, 128:257], pb[:, 0:387].rearrange("p (c d) -> p c d", c=3), AF.Exp,
                    [GbT], [pt])
            C.dma("sp", gscrC, gscr[:, :], Gb[:].rearrange("p a b -> p (a b)"), w=[gscrT], r=[GbT])
            C.fence("dve", [GbT], dummy_sb[:])

        def norm_tg(tg, Avec, shvec, dst_fn, dstT, final=False):
            tsl = slice(tg * TG, (tg + 1) * TG)
            pb, pt = bank()
            for c in range(KC):
                i = c % 2
                act(sq_sb[i][:], x_sb[:, c, tsl], AF.Square, [sqT[i]], [xT[tg]])
                mm(pb, onesm_bf[:], sq_sb[i][:], c == 0, c == KC - 1, [pt], [sqT[i], onesT])
            act(rstd_sb[:], pb, AF.Sqrt, [rstdT], [pt], bias=EPS)
            C.op("dve", lambda: nc.vector.reciprocal(rstd_sb[:], rstd_sb[:]), [rstdT], [rstdT])
            if dbg and not dbg_r[0]:
                dbg_r[0] = 1
                C.dma("sp", outC, dbg2_d[:, 3584:4096], rstd_sb[:], w=[ofmT], r=[rstdT])
            for c in range(KC):
                if final:
                    stt(x_sb[:, c, tsl], x_sb[:, c, tsl], Avec(c), rstd_sb[:], ALU.mult, ALU.mult,
                        [xT[tg]], [xT[tg], rstdT, vecT, AT])
                else:
                    i = c % 2
                    stt(ntmp_sb[i][:], x_sb[:, c, tsl], Avec(c), rstd_sb[:], ALU.mult, ALU.mult,
                        [ntmpT[i]], [xT[tg], rstdT, vecT, AT])
                    act(dst_fn(c), ntmp_sb[i][:], AF.Identity, [dstT], [ntmpT[i], adaT], bias=shvec(c), scale=1.0)


        def norm_steps(tg, Avec, shvec, dst_fn, dstT, final=False, nbank=7):
            tsl = slice(tg * TG, (tg + 1) * TG)
            pb, pt = psum[:, nbank, :], psT[nbank]

            def sqr(c):
                i = c % 2
                act(sq_sb[i][:], x_sb[:, c, tsl], AF.Square, [sqT[i]], [xT[tg]])

            def mmc(c):
                i = c % 2
                mm(pb, onesm_bf[:], sq_sb[i][:], c == 0, c == KC - 1, [pt], [sqT[i], onesT])

            def sttc(c):
                if final:
                    stt(x_sb[:, c, tsl], x_sb[:, c, tsl], Avec(c), rstd_sb[:], ALU.mult, ALU.mult,
                        [xT[tg]], [xT[tg], rstdT, vecT, AT])
                else:
                    i = c % 2
                    stt(ntmp_sb[i][:], x_sb[:, c, tsl], Avec(c), rstd_sb[:], ALU.mult, ALU.mult,
                        [ntmpT[i]], [xT[tg], rstdT, vecT, AT])

            def idc(c):
                if not final:
                    i = c % 2
                    act(dst_fn(c), ntmp_sb[i][:], AF.Identity, [dstT], [ntmpT[i], adaT], bias=shvec(c), scale=1.0)

            steps = []
            for j in range(5):
                def st_(j=j):
                    if j >= 1:
                        mmc(2 * j - 2)
                        mmc(2 * j - 1)
                    if j < 4:
                        sqr(2 * j)
                        sqr(2 * j + 1)
                steps.append(st_)
            steps.append(lambda: act(rstd_sb[:], pb, AF.Sqrt, [rstdT], [pt], bias=EPS))
            steps.append(lambda: C.op("dve", lambda: nc.vector.reciprocal(rstd_sb[:], rstd_sb[:]), [rstdT], [rstdT]))
            for j in range(5):
                def st2_(j=j):
                    if j >= 1:
                        idc(2 * j - 2)
                        idc(2 * j - 1)
                    if j < 4:
                        sttc(2 * j)
                        sttc(2 * j + 1)
                steps.append(st2_)
            return steps

        def tok(cfg, tile):
            dil = DILS[cfg]
            if cfg == 0:
                return slice(tile * 128, (tile + 1) * 128)
            if cfg == 1:
                r, n = tile // 4, tile % 4
                s0_ = n * 512 + r
                return slice(s0_, s0_ + 127 * 4 + 1, 4)
            r = tile
            return slice(r, r + 127 * 16 + 1, 16)

        dbg_i = [0]
        dbg_r = [0]
        marks = []

        def mark(name):
            marks.append((name, C.cnt["pe"]))

        def dump_x(s):
            if dbg and s == 0 and dbg_i[0] < 8:
                C.dma("sp", outC, dbg_d[dbg_i[0]], x_sb[:], w=[ofmT], r=xT)
                dbg_i[0] += 1

        def dump_bf(src, nchunk, rT):
            if dbg and dbg_i[0] < 8:
                with scope() as sd:
                    tmp = sb(sd, [128, NT], F32, "dbgt")
                    tT = C.newT()
                    for ch in range(nchunk):
                        cp("dve", tmp[:], src[:, ch, :], [tT], rT)
                        C.dma("sp", outC, dbg_d[dbg_i[0], :, ch, :], tmp[:], w=[ofmT], r=[tT])
                    C.fence("dve", [tT], dummy_sb[:])
                dbg_i[0] += 1

        if dbg:
            n1 = L * 48 * NS
            C.dma("sp", outC, dbg2_d[:, 0:n1], ada_sb[:].rearrange("p a b c -> p (a b c)"), w=[ofmT], r=[adaT])
            n2 = L * 16 * NS
            C.dma("sp", outC, dbg2_d[:, 1024:1024 + n2], A_sb[:].rearrange("p a b c d -> p (a b c d)"), w=[ofmT], r=[AT])
            C.dma("sp", outC, dbg2_d[:, 2048:2048 + 12 * L], cl_sb[:].rearrange("p a b -> p (a b)"), w=[ofmT], r=[clT])
            C.dma("sp", outC, dbg2_d[:, 2560:2560 + NV], vec_sb[:], w=[ofmT], r=[vecT])
        for s in range(NS):
            if s > 0:
                for c in range(KC):
                    C.dma("pool", xloadC, x_sb[:, c, :], xfm[s, :, c, :], w=xT)

            if dbg and s == 0:
                C.dma("sp", outC, dbg_d[7], x_sb[:], w=[ofmT], r=xT)
            for l in range(L):
                def adac(part, c):
                    return ada_sb[:, l, part * 8 + c, s:s + 1]

                if l == 0:
                    for tg in range(NTG):
                        norm_tg(tg, lambda c: A_sb[:, l, 0, c, s:s + 1], lambda c: adac(0, c),
                                lambda c, tg=tg: xn_sb[:, c, tg * TG:(tg + 1) * TG], xnT[tg])
                if s == 0 and l == 0:
                    dump_bf(xn_sb[:], 8, xnT)

                mark(f"s{s}l{l}.P2")
                def maybe_precast(k):
                    if s == 0 and l + 1 < L and k < NCH:
                        precast_chunk(l + 1, k)

                maybe_precast(0)
                bank_set[0] = [5, 6, 7]
                with scope() as s2:
                    q_sb = sb(s2, [128, 2, NT], BF16, "q")
                    k_sb = sb(s2, [128, NT], BF16, "k")
                    qT_, kT_ = C.newT(), C.newT()
                    mset("pool", q_sb[64:128, 0, :], 0.0, [qT_])
                    mset("pool", q_sb[0:64, 1, :], 0.0, [qT_])
                    V_sb = sb(s2, [128, 3, 16, 192], BF16, "V")
                    VT = C.newT()
                    mset("pool", V_sb[:, :, :, 64:128], 0.0, [VT])
                    ones_pad = sb(s2, [128, 192], BF16, "onesp")
                    mset("pool", ones_pad[:, 0:64], 1.0, [VT])
                    mset("pool", ones_pad[:, 64:128], 0.0, [VT])
                    mset("pool", ones_pad[:, 128:192], 1.0, [VT])
                    acc = sb(s2, [128, 2, NT], F32, "acc")
                    accT = C.newT()
                    msk = [sb(s2, [128, 3, 2, 2, 128], BF16, "msk")] * 2
                    mskT = [C.newT()] * 2
                    pT = [sb(s2, [128, 2, 2, 128], BF16, "pT") for _ in range(3)]
                    pTT = [C.newT() for _ in range(3)]
                    for hp in range(6):
                        if hp == 2:
                            maybe_precast(1)
                        if hp == 4:
                            maybe_precast(2)
                        wq, wqT = wpiece(l, f"qkv{hp}")
                        wqv = wq[:, 0:3072].rearrange("p (a k n) -> p a k n", a=3, k=8)
                        mi = hp % 2
                        mo = (2 * hp) * 3 * NGW + 128
                        for h2m in range(2):
                            msrc = bass.AP(gscr_t, mo + h2m * 3 * NGW, [[GROW - 1, 128], [NGW, 3], [1, 256]])
                            C.dma("sp", mskC[mi], msk[mi][:, :, h2m, :, :].rearrange("p c j q -> p c (j q)"), msrc,
                                  w=[mskT[mi]], r=[gscrT])
                        for tg in range(NTG):
                            tsl = slice(tg * TG, (tg + 1) * TG)
                            pb, pt = bank()
                            for kc in range(KC):
                                mm(pb, wqv[:, 0, kc, :], xn_sb[:, kc, tsl], kc == 0, kc == KC - 1, [pt], [wqT, xnT[tg]])
                            act(q_sb[0:64, 0, tsl], pb[0:64, :], AF.Copy, [qT_], [pt], scale=0.125)
                            act(q_sb[64:128, 1, tsl], pb[64:128, :], AF.Copy, [qT_], [pt], scale=0.125)
                            pb, pt = bank()
                            for kc in range(KC):
                                mm(pb, wqv[:, 1, kc, :], xn_sb[:, kc, tsl], kc == 0, kc == KC - 1, [pt], [wqT, xnT[tg]])
                            act(k_sb[:, tsl], pb, AF.Copy, [kT_], [pt])
                        for cfg in range(3):
                            for grp in range(4):
                                pb, pt = bank()
                                pv = pb.rearrange("p (a n) -> p a n", a=4)
                                for ti in range(4):
                                    tile = grp * 4 + ti
                                    for kc in range(KC):
                                        mm(pv[:, ti, :], xn_sb[:, kc, tok(cfg, tile)], wqv[:, 2, kc, :],
                                           kc == 0, kc == KC - 1, [pt], [wqT] + xnT)
                                act(V_sb[:, cfg, grp * 4:(grp + 1) * 4, :].rearrange("p t (h e) -> p t h e", h=3)[:, :, 0:3:2, :],
                                    pv.rearrange("p t (h e) -> p t h e", h=2), AF.Copy, [VT], [pt])
                        blocks = []
                        for cfg in range(3):
                            for tile in range(16):
                                if cfg == 0:
                                    prev = tile - 1 if tile > 0 else None
                                elif cfg == 1:
                                    prev = tile - 1 if tile % 4 > 0 else None
                                else:
                                    prev = None
                                blocks.append((cfg, tile, prev))

                        def emit_S(bi):
                            cfg, tile, prev = blocks[bi]
                            nj = 2 if prev is not None else 1
                            sbk = bi % 3
                            pp2 = psum[:, sbk, :].rearrange("p (h j q) -> p h j q", h=2, j=2)
                            for j in range(nj):
                                kt = tile if j == 0 else prev
                                mm(pp2[:, :, j, :], k_sb[:, tok(cfg, kt)], q_sb[:, :, tok(cfg, tile)], True, True,
                                   [psT[sbk]], [kT_, qT_])
                            pi = bi % 3
                            act(pT[pi][:, :, 0:nj, :], pp2[:, :, 0:nj, :], AF.Exp, [pTT[pi]], [psT[sbk]])
                            tt("pool" if bi % 2 == 1 else "dve", pT[pi][:, :, 0:nj, :], pT[pi][:, :, 0:nj, :],
                               msk[mi][:, cfg, :, 0:nj, :], ALU.mult, [pTT[pi]], [pTT[pi], mskT[mi]])

                        def emit_PV(bi):
                            cfg, tile, prev = blocks[bi]
                            nj = 2 if prev is not None else 1
                            pi = bi % 3
                            pbk = 3 + bi % 2
                            po = psum[:, pbk, 0:256].rearrange("p (a q) -> p a q", a=2)
                            pt = psT[pbk]
                            n_mm = 2 * nj
                            k_ = 0
                            for h2 in range(2):
                                for j in range(nj):
                                    kt = tile if j == 0 else prev
                                    mm(po[:, 0, :], V_sb[:, cfg, kt, h2 * 64:h2 * 64 + 128], pT[pi][:, h2, j, :],
                                       k_ == 0, k_ == n_mm - 1, [pt], [VT, pTT[pi]])
                                    k_ += 1
                            k_ = 0
                            for h2 in range(2):
                                for j in range(nj):
                                    mm(po[:, 1, :], ones_pad[:, h2 * 64:h2 * 64 + 128], pT[pi][:, h2, j, :],
                                       k_ == 0, k_ == n_mm - 1, [pt], [VT, pTT[pi]])
                                    k_ += 1
                            dst = acc[:, :, tok(cfg, tile)]
                            if cfg == 0:
                                cp("dve", dst, po, [accT], [pt])
                            else:
                                tt("dve", dst, dst, po, ALU.add, [accT], [accT, pt])

                        emit_S(0)
                        emit_S(1)
                        for bi in range(len(blocks)):
                            if bi + 2 < len(blocks):
                                emit_S(bi + 2)
                            emit_PV(bi)
                        C.op("dve", lambda: nc.vector.reciprocal(acc[:, 1, :], acc[:, 1, :]), [accT], [accT])
                        tt("dve", oc_sb[:, hp, :], acc[:, 0, :], acc[:, 1, :], ALU.mult, [ocT], [accT])
                bank_set[0] = list(range(8))
                if s == 0 and l == 0:
                    dump_bf(oc_sb[:], 6, [ocT])

                mark(f"s{s}l{l}.P3")
                with scope() as s3:
                    oa_sb = sb(s3, [128, 6, TG], BF16, "oa")
                    ob_sb = sb(s3, [128, 6, TG], BF16, "ob")
                    oaT, obT = C.newT(), C.newT()
                    WmT = sb(s3, [128, 12, 128], BF16, "WmT")
                    WmTT = C.newT()
                    wbd = sb(s3, [128, 2, 6, 128], BF16, "wbd")
                    wbdT = C.newT()
                    bsb = sb(s3, [128, 6, 128], F32, "bsb")
                    bsbT = C.newT()
                    lnb = sb(s3, [128, AW], F32, "lnb")
                    lnbT = C.newT()
                    xh = sb(s3, [128, 6, 3], F32, "xh")
                    hc = sb(s3, [128, 6], F32, "hc")
                    carT = C.newT()
                    C.dma("pool", smallC[0], WmT[:], wsT_d[l], w=[WmTT])
                    C.dma("pool", smallC[1], wbd[:], wbd_d[l], w=[wbdT])
                    C.dma("sp", smallC[2], bsb[:], bsb_d[l], w=[bsbT])
                    C.dma("sp", smallC[3], lnb[:], lnb_d[l], w=[lnbT])
                    tt("pool", WmT[:], WmT[:], bass.AP(triu_sb[:].tensor, triu_sb[:].offset,
                                                       [list(triu_sb[:].ap[0]), [0, 12], [1, 128]]),
                       ALU.mult, [WmTT], [WmTT, triuT])
                    for tg in range(NTG):
                        tsl = slice(tg * TG, (tg + 1) * TG)
                        if tg == 0:
                            maybe_precast(3)
                        if tg == 2:
                            maybe_precast(4)
                        mark(f"s{s}l{l}.A{tg}")
                        with scope() as sa:
                            u_sb = sb(sa, [128, 6, TG], BF16, "u")
                            uT = C.newT()
                            gv = [sb(sa, [128, AW], F32, "gv") for _ in range(4)]
                            gvT = [C.newT() for _ in range(4)]
                            v_sb = [sb(sa, [128, AW], BF16, "v") for _ in range(2)]
                            vT = [C.newT() for _ in range(2)]
                            st = sb(sa, [128, 4, 2, 6], F32, "st")
                            mv = sb(sa, [128, 4, 2], F32, "mv")
                            rs = sb(sa, [128, 4], F32, "rs")
                            stT = C.newT()
                            atmp = [sb(sa, [128, 3, 128], F32, "atmp")] * 2
                            atmpT = [C.newT()] * 2
                            for i in range(2):
                                au, auT = wpiece(l, f"au{i}")
                                auv = au[:, 0:3072].rearrange("p (k n) -> p k n", k=8)
                                for cc in range(3):
                                    c = i * 3 + cc
                                    pb, pt = bank()
                                    for kc in range(KC):
                                        mm(pb, auv[:, kc, cc * 128:(cc + 1) * 128], xn_sb[:, kc, tsl], kc == 0, kc == KC - 1,
                                           [pt], [auT, xnT[tg]])
                                    act(u_sb[:, c, :], pb, AF.Gelu_apprx_tanh, [uT], [pt])
                            av0, av0T = wpiece(l, "av0")
                            av1, av1T = wpiece(l, "av1")
                            avv = [av0[:, 0:3072].rearrange("p (k n) -> p k n", k=4),
                                   av1[:, 0:3072].rearrange("p (k n) -> p k n", k=4)]
                            for ti in range(4):
                                tile = tg * 4 + ti
                                tks = slice(tile * 128, (tile + 1) * 128)
                                for half in range(2):
                                    pb, pt = bank()
                                    for kc in range(KC):
                                        mm(pb[:, 0:384], xn_sb[:, kc, tks], avv[kc // 4][:, kc % 4, half * 384:(half + 1) * 384],
                                           kc == 0, kc == KC - 1, [pt], [av0T, av1T, xnT[tg]])
                                    act(gv[ti][:, half * 384:(half + 1) * 384], pb[:, 0:384], AF.Gelu_apprx_tanh, [gvT[ti]], [pt])
                                    C.op("dve", lambda half=half, ti=ti: nc.vector.bn_stats(st[:, ti, half, :], gv[ti][:, half * 384:(half + 1) * 384]),
                                         [stT], [gvT[ti]])
                                C.op("dve", lambda ti=ti: nc.vector.bn_aggr(mv[:, ti, :], st[:, ti, :, :].rearrange("p a b -> p (a b)")), [stT], [stT])
                            act(rs[:], mv[:, :, 1], AF.Sqrt, [stT], [stT], bias=EPS)
                            C.op("dve", lambda: nc.vector.reciprocal(rs[:], rs[:]), [stT], [stT])
                            for ti in range(4):
                                b2 = ti % 2
                                ts("dve", gv[ti][:], gv[ti][:], mv[:, ti, 0:1], rs[:, ti:ti + 1], ALU.subtract, ALU.mult,
                                   [gvT[ti]], [gvT[ti], stT])
                                tt("dve", v_sb[b2][:], gv[ti][:], lnb[:], ALU.mult, [vT[b2]], [gvT[ti], lnbT])
                                for half in range(2):
                                    pb, pt = bank()
                                    for cc in range(3):
                                        c = half * 3 + cc
                                        for g2 in range(2):
                                            g = 2 * c + g2
                                            mm(pb[g2 * 64:(g2 + 1) * 64, cc * 128:(cc + 1) * 128], v_sb[b2][:, g * 64:(g + 1) * 64],
                                               WmT[:, g, :], True, True, [pt], [vT[b2], WmTT])
                                    tt("dve", atmp[half][:], pb[:, 0:384].rearrange("p (a t) -> p a t", a=3),
                                       bsb[:, half * 3:(half + 1) * 3, :], ALU.add, [atmpT[half]], [pt, bsbT])
                                    tt("pool", oa_sb[:, half * 3:(half + 1) * 3, ti * 128:(ti + 1) * 128], atmp[half][:],
                                       u_sb[:, half * 3:(half + 1) * 3, ti * 128:(ti + 1) * 128], ALU.mult,
                                       [oaT], [atmpT[half], uT])
                        mark(f"s{s}l{l}.B{tg}")
                        with scope() as sbb:
                            xbuf = [sb(sbb, [128, TG + 3], F32, "xbuf") for _ in range(2)]
                            xb = [sb(sbb, [128, TG], F32, "xb") for _ in range(2)]
                            xbb = [sb(sbb, [128, TG], BF16, "xbb") for _ in range(2)]
                            rbuf = [sb(sbb, [128, TG], F32, "rbuf") for _ in range(2)]
                            ibuf = [sb(sbb, [128, TG], F32, "ibuf") for _ in range(2)]
                            mbuf = [sb(sbb, [128, TG], F32, "mbuf") for _ in range(2)]
                            xbufT = [C.newT() for _ in range(2)]
                            xbT = [C.newT() for _ in range(2)]
                            xbbT = [C.newT() for _ in range(2)]
                            rT_ = [C.newT() for _ in range(2)]
                            iT_ = [C.newT() for _ in range(2)]
                            mT_ = [C.newT() for _ in range(2)]
                            ocw = vtab["bcw"] + l * 24
                            for cp_ in range(3):
                                pair = (2 * cp_, 2 * cp_ + 1)
                                st = {}
                                for c in pair:
                                    i = c % 2
                                    bw, bwT = wpiece(l, f"b{c}")
                                    bwv = bw[:, 0:2048].rearrange("p (a k n) -> p a k n", a=2, k=8)
                                    psx, psxT = bank()
                                    for kc in range(KC):
                                        mm(psx, bwv[:, 0, kc, :], xn_sb[:, kc, tsl], kc == 0, kc == KC - 1, [psxT], [bwT, xnT[tg]])
                                    psg, psgT = bank()
                                    for kc in range(KC):
                                        mm(psg, bwv[:, 1, kc, :], xn_sb[:, kc, tsl], kc == 0, kc == KC - 1, [psgT], [bwT, xnT[tg]])
                                    if tg == 0:
                                        mset("dve", xbuf[i][:, 0:3], 0.0, [xbufT[i]])
                                    else:
                                        cp("dve", xbuf[i][:, 0:3], xh[:, c, :], [xbufT[i]], [carT])
                                    act(xbuf[i][:, 3:TG + 3], psx, AF.Copy, [xbufT[i]], [psxT])
                                    act(xb[i][:], psx, AF.Identity, [xbT[i]], [psxT, vecT],
                                        bias=vcol("bcb", l * 6 + c), scale=vec_sb[:, ocw + 18 + c:ocw + 18 + c + 1])
                                    cp("dve", xh[:, c, :], xbuf[i][:, TG:TG + 3], [carT], [xbufT[i]])
                                    for k in range(0, 3):
                                        stt(xb[i][:], xbuf[i][:, k:k + TG], vec_sb[:, ocw + k * 6 + c:ocw + k * 6 + c + 1], xb[i][:],
                                            ALU.mult, ALU.add, [xbT[i]], [xbufT[i], xbT[i], vecT])
                                    act(xbb[i][:], xb[i][:], AF.Copy, [xbbT[i]], [xbT[i]])
                                    psr, psrT = bank()
                                    mm(psr, wbd[:, 0, c, :], xbb[i][:], True, True, [psrT], [wbdT, xbbT[i]])
                                    psi, psiT = bank()
                                    mm(psi, wbd[:, 1, c, :], xbb[i][:], True, True, [psiT], [wbdT, xbbT[i]])
                                    st[c] = (psg, psgT, psr, psrT, psi, psiT)
                                for c in pair:
                                    i = c % 2
                                    psg, psgT, psr, psrT, psi, psiT = st[c]
                                    act(rbuf[i][:], psr, AF.Sigmoid, [rT_[i]], [psrT, vecT], bias=vcol("bba", l * 6 + c), scale=1.0)
                                    act(ibuf[i][:], psi, AF.Sigmoid, [iT_[i]], [psiT, vecT], bias=vcol("bbx", l * 6 + c), scale=1.0)
                                for c in pair:
                                    i = c % 2
                                    act(mbuf[i][:], rbuf[i][:], AF.Exp, [mT_[i]], [rT_[i], clT], scale=cl_sb[:, 1, l * 6 + c:l * 6 + c + 1])
                                    act(rbuf[i][:], rbuf[i][:], AF.Exp, [rT_[i]], [rT_[i], clT], scale=cl_sb[:, 0, l * 6 + c:l * 6 + c + 1])
                                    ts("pool", mbuf[i][:], mbuf[i][:], -1.0, 1.0, ALU.mult, ALU.add, [mT_[i]], [mT_[i]])
                                    tt("pool", ibuf[i][:], ibuf[i][:], xb[i][:], ALU.mult, [iT_[i]], [iT_[i], xbT[i]])
                                for c in pair:
                                    i = c % 2
                                    act(mbuf[i][:], mbuf[i][:], AF.Sqrt, [mT_[i]], [mT_[i]])
                                    if tg == 0:
                                        mset("dve", mbuf[i][:, 0:1], 1.0, [mT_[i]])
                                    tt("dve", ibuf[i][:], ibuf[i][:], mbuf[i][:], ALU.mult, [iT_[i]], [iT_[i], mT_[i]])
                                    init = 0.0 if tg == 0 else hc[:, c:c + 1]
                                    C.op("dve", lambda init=init, i=i: nc.vector.tensor_tensor_scan(mbuf[i][:], rbuf[i][:], ibuf[i][:], init, ALU.mult, ALU.add),
                                         [mT_[i]], [mT_[i], rT_[i], iT_[i], carT])
                                    cp("dve", hc[:, c:c + 1], mbuf[i][:, TG - 1:TG], [carT], [mT_[i]])
                                for c in pair:
                                    i = c % 2
                                    psg, psgT, psr, psrT, psi, psiT = st[c]
                                    act(xb[i][:], psg, AF.Gelu_apprx_tanh, [xbT[i]], [psgT])
                                    tt("pool", ob_sb[:, c, :], mbuf[i][:], xb[i][:], ALU.mult, [obT], [mT_[i], xbT[i]])
                        mark(f"s{s}l{l}.M{tg}")
                        with scope() as sm:
                            gate = sb(sm, [128, 3, TG], F32, "gate")
                            gateT = C.newT()
                            mg = sb(sm, [128, 8, TG], BF16, "mg")
                            mgT = C.newT()
                            t0 = sb(sm, [128, TG], F32, "t0")
                            t1 = sb(sm, [128, TG], F32, "t1")
                            t0T, t1T = C.newT(), C.newT()
                            for c in range(8):
                                gt, gtT = wpiece(l, f"gt{c}")
                                gtv = gt[:, 0:3072].rearrange("p (a k n) -> p a k n", a=3, k=8)
                                for X in range(3):
                                    pb, pt = bank()
                                    for kc in range(KC):
                                        mm(pb, gtv[:, X, kc, :], xn_sb[:, kc, tsl], kc == 0, kc == KC - 1, [pt], [gtT, xnT[tg]])
                                    act(gate[:, X, :], pb, AF.Sigmoid, [gateT], [pt])
                                pw, pwT = wpiece(l, f"pp{c}")
                                pwv = pw[:, 0:2304].rearrange("p (a k n) -> p a k n", a=3, k=6)
                                srcs = ((lambda k: oa_sb[:, k, :], oaT), (lambda k: ob_sb[:, k, :], obT),
                                        (lambda k: oc_sb[:, k, tsl], ocT))
                                for X in range(3):
                                    pb, pt = bank()
                                    fsrc, sT_ = srcs[X]
                                    for k in range(6):
                                        mm(pb, pwv[:, X, k, :], fsrc(k), k == 0, k == 5, [pt], [pwT, sT_])
                                    if X == 0:
                                        tt("dve", t0[:], pb, gate[:, 0, :], ALU.mult, [t0T], [pt, gateT])
                                    elif X == 1:
                                        tt("dve", t1[:], pb, gate[:, 1, :], ALU.mult, [t1T], [pt, gateT])
                                        tt("pool", t0[:], t0[:], t1[:], ALU.add, [t0T], [t0T, t1T])
                                    else:
                                        tt("dve", t1[:], pb, gate[:, 2, :], ALU.mult, [t1T], [pt, gateT])
                                        tt("pool", mg[:, c, :], t0[:], t1[:], ALU.add, [mgT], [t0T, t1T])
                            c = 0
                            for i, n in enumerate((384, 384, 256)):
                                wo, woT = wpiece(l, f"wo{i}")
                                wov = wo[:, 0:8 * n].rearrange("p (k n) -> p k n", k=8)
                                for cc in range(n // 128):
                                    pb, pt = bank()
                                    for k in range(8):
                                        mm(pb, wov[:, k, cc * 128:(cc + 1) * 128], mg[:, k, :], k == 0, k == 7, [pt], [woT, mgT])
                                    stt(x_sb[:, c, tsl], pb, adac(2, c), x_sb[:, c, tsl], ALU.mult, ALU.add,
                                        [xT[tg]], [pt, xT[tg], adaT])
                                    c += 1
                dump_x(s) if l == 0 else None

                maybe_precast(5)
                maybe_precast(6)
                mark(f"s{s}l{l}.P4")
                with scope() as s4:
                    xn2b = [sb(s4, [128, KC, TG], BF16, "xn2") for _ in range(2)]
                    xn2Tb = [C.newT() for _ in range(2)]
                    gbuf = [sb(s4, [128, TG + 2], F32, "gbuf") for _ in range(2)]
                    gbufT = [C.newT() for _ in range(2)]
                    tb = [sb(s4, [128, TG], F32, "tb") for _ in range(2)]
                    tbT = [C.newT() for _ in range(2)]
                    ub = [sb(s4, [128, TG], BF16, "ub") for _ in range(2)]
                    ubT = [C.newT() for _ in range(2)]
                    hbuf = sb(s4, [128, NF, TG], BF16, "hbuf")
                    hbufT = [C.newT() for _ in range(NF)]
                    fh = sb(s4, [128, NF, 2], F32, "fh")
                    fhT = C.newT()
                    def next_norm(tgn, stepped):
                        fn = norm_steps if stepped else norm_tg
                        if l + 1 < L:
                            return fn(tgn, lambda c: A_sb[:, l + 1, 0, c, s:s + 1],
                                      lambda c: ada_sb[:, l + 1, c, s:s + 1],
                                      lambda c: xn_sb[:, c, tgn * TG:(tgn + 1) * TG], xnT[tgn])
                        return fn(tgn, lambda c: vcol("final", c), None, None, None, final=True)

                    bank_set[0] = list(range(7))

                    for tg in range(NTG):
                        tsl = slice(tg * TG, (tg + 1) * TG)
                        xn2, xn2T = xn2b[tg % 2], xn2Tb[tg % 2]
                        if tg == 0:
                            norm_tg(0, lambda c: A_sb[:, l, 1, c, s:s + 1], lambda c: adac(3, c),
                                    lambda c: xn2b[0][:, c, :], xn2Tb[0])
                        sched = {}
                        if tg + 1 < NTG:
                            n2 = norm_steps(tg + 1, lambda c: A_sb[:, l, 1, c, s:s + 1], lambda c: adac(3, c),
                                            lambda c, nb=(tg + 1) % 2: xn2b[nb][:, c, :], xn2Tb[(tg + 1) % 2])
                            for k_, st_ in enumerate(n2):
                                sched.setdefault(k_, []).append(st_)
                        if tg > 0:
                            nn = next_norm(tg - 1, True)
                            for k_, st_ in enumerate(nn):
                                sched.setdefault(10 + k_, []).append(st_)
                        for f in range(NF):
                            i = f % 2
                            for st_ in sched.get(f, []):
                                st_()
                            gu, guT = wpiece(l, f"gu{f}")
                            guv = gu[:, 0:2048].rearrange("p (a k n) -> p a k n", a=2, k=8)
                            psg, psgT = bank()
                            for kc in range(KC):
                                mm(psg, guv[:, 0, kc, :], xn2[:, kc, :], kc == 0, kc == KC - 1, [psgT], [guT, xn2T])
                            psu, psuT = bank()
                            for kc in range(KC):
                                mm(psu, guv[:, 1, kc, :], xn2[:, kc, :], kc == 0, kc == KC - 1, [psuT], [guT, xn2T])
                            if tg == 0:
                                mset("dve", gbuf[i][:, 0:2], 0.0, [gbufT[i]])
                            else:
                                cp("dve", gbuf[i][:, 0:2], fh[:, f, :], [gbufT[i]], [fhT])
                            o = vtab["fcw"] + l * 3 * NF
                            act(gbuf[i][:, 2:TG + 2], psg, AF.Copy, [gbufT[i]], [psgT])
                            act(tb[i][:], psg, AF.Identity, [tbT[i]], [psgT, vecT],
                                bias=vcol("fcb", l * NF + f), scale=vec_sb[:, o + 2 * NF + f:o + 2 * NF + f + 1])
                            cp("dve", fh[:, f, :], gbuf[i][:, TG:TG + 2], [fhT], [gbufT[i]])
                            for k in range(0, 2):
                                stt(tb[i][:], gbuf[i][:, k:k + TG], vec_sb[:, o + k * NF + f:o + k * NF + f + 1], tb[i][:],
                                    ALU.mult, ALU.add, [tbT[i]], [gbufT[i], tbT[i], vecT])
                            act(tb[i][:], tb[i][:], AF.Gelu_apprx_tanh, [tbT[i]], [tbT[i]])
                            act(ub[i][:], psu, AF.Copy, [ubT[i]], [psuT])
                            tt("pool", hbuf[:, f, :], tb[i][:], ub[i][:], ALU.mult, [hbufT[f]], [ubT[i], tbT[i]])
                        for c in range(8):
                            dn, dnT = wpiece(l, f"dn{c}")
                            dnv = dn[:, 0:NF * 128].rearrange("p (k n) -> p k n", k=NF)
                            pb, pt = bank()
                            for f in range(NF):
                                mm(pb, dnv[:, f, :], hbuf[:, f, :], f == 0, f == NF - 1, [pt], [dnT, hbufT[f]])
                            stt(x_sb[:, c, tsl], pb, adac(5, c), x_sb[:, c, tsl], ALU.mult, ALU.add,
                                [xT[tg]], [pt, xT[tg], adaT])
                        if tg == NTG - 1:
                            next_norm(tg, False)
                    bank_set[0] = list(range(8))
                dump_x(s) if l == 0 else None

            C.dma("sp", outC, ofm[s], x_sb[:], w=[ofmT], r=xT)

        C._need("sp", outC.key, outC.count)
        mark("end")
        build_program.stats = dict(C.cnt, nwait=C.nwait)
        build_program.marks = marks
    return nc


def prepare_inputs(NS, L, ncores, x, c, w_ada, b_ada, norm1, w_in, a_ln, a_ws, a_bs, b_conv_w, b_conv_b, b_wa, b_ba,
                   b_wx, b_bx, b_lam, rel_table, p_a, p_b, p_c, w_out, norm2, f_wgate, f_wup,
                   f_conv_w, f_conv_b, f_wdown, final_norm):
    f32 = np.float32
    x = np.asarray(x, f32)
    c = np.asarray(c, f32)
    args = [np.asarray(a, f32) for a in (w_in, p_a, p_b, p_c, w_out, f_wgate, f_wup, f_wdown)]
    wflat = np.stack([host_wflat(l, *args) for l in range(L)], axis=0)
    w_ada = np.asarray(w_ada, f32)
    wada = np.ascontiguousarray(w_ada[:L].reshape(L, KC, 128, 24, 256).transpose(0, 3, 2, 1, 4))
    vecs = host_vecs(L, np.asarray(b_ada, f32), np.asarray(norm1, f32), np.asarray(norm2, f32),
                     np.asarray(final_norm, f32), np.asarray(b_conv_w, f32), np.asarray(b_conv_b, f32),
                     np.asarray(b_ba, f32), np.asarray(b_bx, f32), np.asarray(b_lam, f32),
                     np.asarray(f_conv_w, f32), np.asarray(f_conv_b, f32))
    a_ws = np.asarray(a_ws, f32)
    wsT = np.ascontiguousarray(a_ws[:L].transpose(0, 3, 1, 2))
    b_wa = np.asarray(b_wa, f32)
    b_wx = np.asarray(b_wx, f32)
    wbd = np.zeros((L, 128, 2, 6, 128), f32)
    for wi_, wsrc in enumerate((b_wa, b_wx)):
        for cch in range(6):
            for g2 in range(2):
                wbd[:, g2 * 64:(g2 + 1) * 64, wi_, cch, g2 * 64:(g2 + 1) * 64] = wsrc[:L, 2 * cch + g2]
    a_bs = np.asarray(a_bs, f32)
    bsb = np.ascontiguousarray(np.repeat(a_bs[:L].reshape(L, 6, 2, 1, 128), 64, axis=3).reshape(L, 6, 128, 128)
                               .transpose(0, 2, 1, 3))
    lnb = np.ascontiguousarray(np.broadcast_to(np.asarray(a_ln, f32)[:L, None, :], (L, 128, AW)))
    oh, triu = host_consts()
    rel = np.ascontiguousarray(np.asarray(rel_table, f32))
    in_maps = []
    for k in range(ncores):
        xs = x[k * NS:(k + 1) * NS]
        xfm = np.ascontiguousarray(xs.reshape(NS, NT, KC, 128).transpose(0, 3, 2, 1))
        cs = c[k * NS:(k + 1) * NS]
        cTk = np.ascontiguousarray(cs.reshape(NS, KC, 128).transpose(2, 1, 0))
        in_maps.append({"xfm": xfm, "cT": cTk, "wada": wada, "vecs": vecs, "wflat": wflat, "wsT": wsT,
                        "wbd": wbd, "bsb": bsb, "lnb": lnb, "rel": rel, "oh": oh, "triu": triu,
                        "ident": np.eye(8, dtype=np.float32)})
    return in_maps


def gather_output(res, NS, ncores):
    outs = []
    for k in range(ncores):
        o = np.asarray(res.results[k]["ofm"])
        outs.append(o.transpose(0, 3, 2, 1).reshape(NS, NT, D))
    return np.concatenate(outs, axis=0).astype(np.float32)


def kernel(**inputs):
    NS = BATCH // NCORES
    in_maps = prepare_inputs(NS, DEPTH, NCORES, **inputs)
    nc = build_program(NS, DEPTH)
    res = run_bass_kernel_spmd(nc, in_maps, core_ids=list(range(NCORES)))
    return gather_output(res, NS, NCORES)
```

```python
import numpy as np
from contextlib import ExitStack, contextmanager
import concourse.bass as bass
import concourse.mybir as mybir
from concourse.bass_utils import run_bass_kernel_spmd

F32 = mybir.dt.float32
BF16 = mybir.dt.bfloat16
AF = mybir.ActivationFunctionType
ALU = mybir.AluOpType

D = 1024
KC = 8
NT = 2048
TG = 512
NTG = 4
DFF = 2816
NF = 22
AW = 768
IN_COLS = 8448
EPS = 1e-6
DEPTH = 4
BATCH = 32
NCORES = 8
SLOT = 3072
DILS = (1, 4, 16)
NGW = 385


def piece_table():
    names = []
    for hp in range(6):
        names.append((f"qkv{hp}", 3 * 8 * 128))
    for i in range(2):
        names.append((f"au{i}", 8 * 384))
    for i in range(2):
        names.append((f"av{i}", 4 * 768))
    for c in range(6):
        names.append((f"b{c}", 2 * 8 * 128))
    for c in range(8):
        names.append((f"gt{c}", 3 * 8 * 128))
        names.append((f"pp{c}", 3 * 6 * 128))
    for i, n in enumerate((384, 384, 256)):
        names.append((f"wo{i}", 8 * n))
    for f in range(NF):
        names.append((f"gu{f}", 2 * 8 * 128))
    for c in range(8):
        names.append((f"dn{c}", NF * 128))
    tab = {}
    off = 0
    for n, e in names:
        tab[n] = (off, e)
        off += 128 * e
    return tab, off


def _kpn(w, kc):
    n = w.shape[1]
    return w.reshape(kc, 128, n).transpose(1, 0, 2)


def host_wflat(l, w_in, p_a, p_b, p_c, w_out, f_wgate, f_wup, f_wdown):
    tab, tot = piece_table()
    out = np.empty((tot,), np.float32)

    def put(name, arr):
        off, e = tab[name]
        out[off:off + 128 * e] = np.ascontiguousarray(arr, dtype=np.float32).reshape(-1)

    wi = w_in[l]
    for hp in range(6):
        a = np.stack([_kpn(wi[:, 3072 + wch * 768 + hp * 128: 3072 + wch * 768 + (hp + 1) * 128], 8)
                      for wch in range(3)], axis=1)
        put(f"qkv{hp}", a)
    for i in range(2):
        put(f"au{i}", _kpn(wi[:, i * 384:(i + 1) * 384], 8))
    for i in range(2):
        put(f"av{i}", _kpn(wi[i * 512:(i + 1) * 512, 768:1536], 4))
    for c in range(6):
        a = np.stack([_kpn(wi[:, 1536 + c * 128:1536 + (c + 1) * 128], 8),
                      _kpn(wi[:, 2304 + c * 128:2304 + (c + 1) * 128], 8)], axis=1)
        put(f"b{c}", a)
    pp = (p_a[l], p_b[l], p_c[l])
    for c in range(8):
        a = np.stack([_kpn(wi[:, 5376 + X * 1024 + c * 128:5376 + X * 1024 + (c + 1) * 128], 8)
                      for X in range(3)], axis=1)
        put(f"gt{c}", a)
        a = np.stack([_kpn(pp[X][:, c * 128:(c + 1) * 128], 6) for X in range(3)], axis=1)
        put(f"pp{c}", a)
    c0 = 0
    for i, n in enumerate((384, 384, 256)):
        put(f"wo{i}", _kpn(w_out[l][:, c0:c0 + n], 8))
        c0 += n
    for f in range(NF):
        a = np.stack([_kpn(f_wgate[l][:, f * 128:(f + 1) * 128], 8),
                      _kpn(f_wup[l][:, f * 128:(f + 1) * 128], 8)], axis=1)
        put(f"gu{f}", a)
    for c in range(8):
        put(f"dn{c}", _kpn(f_wdown[l][:, c * 128:(c + 1) * 128], NF))
    return out


def vec_table(L):
    tab = {}
    off = 0
    for name, n in (("b_ada", L * 48), ("norm1", L * 8), ("norm2", L * 8), ("final", 8),
                    ("bcw", L * 4 * 6), ("bcb", L * 6), ("bba", L * 6), ("bbx", L * 6), ("blam", L * 6),
                    ("fcw", L * 3 * NF), ("fcb", L * NF)):
        tab[name] = off
        off += n
    return tab, off


def host_vecs(L, b_ada, norm1, norm2, final_norm, b_conv_w, b_conv_b, b_ba, b_bx, b_lam, f_conv_w, f_conv_b):
    tab, nv = vec_table(L)
    v = np.zeros((128, nv), np.float32)

    def fm(a):
        a = np.asarray(a, np.float32)
        lead = a.shape[:-1]
        c = a.shape[-1] // 128
        return a.reshape(lead + (c, 128)).reshape(-1, 128).T

    def put(name, a):
        m = fm(a)
        v[:, tab[name]:tab[name] + m.shape[1]] = m

    put("b_ada", b_ada[:L])
    put("norm1", norm1[:L])
    put("norm2", norm2[:L])
    put("final", final_norm)
    put("bcw", b_conv_w[:L])
    put("bcb", b_conv_b[:L])
    put("bba", b_ba[:L])
    put("bbx", b_bx[:L])
    put("blam", b_lam[:L])
    put("fcw", f_conv_w[:L])
    put("fcb", f_conv_b[:L])
    return v


def t5_bucket(dist):
    n_buckets, max_distance = 32, 2048
    max_exact = n_buckets // 2
    d = np.maximum(dist, 1).astype(np.float32)
    large = max_exact + (np.log(d / max_exact) / np.log(max_distance / max_exact)
                         * (n_buckets - max_exact)).astype(np.int32)
    large = np.minimum(large, n_buckets - 1)
    return np.where(dist < max_exact, dist, large).astype(np.int32)


def host_consts():
    oh = np.zeros((32, 3 * 129), np.float32)
    for ci, dil in enumerate(DILS):
        b = t5_bucket(np.arange(129) * dil)
        oh[b, ci * 129 + np.arange(129)] = 1.0
    triu = np.triu(np.ones((128, 128), np.float32))
    return oh, triu


class T:
    __slots__ = ("wr", "rd", "excl")

    def __init__(self, floor=None, excl=False):
        self.wr = {}
        self.rd = dict(floor) if floor else {}
        self.excl = excl


class Chan:
    def __init__(self, key, sem):
        self.key = key
        self.sem = sem
        self.count = 0


class Ctx:
    def __init__(self, nc, es):
        self.nc = nc
        self.es = es
        self.E = {"pe": nc.tensor, "act": nc.scalar, "dve": nc.vector, "pool": nc.gpsimd, "sp": nc.sync}
        self.semh = {}
        self.cnt = {}
        self.seen = {}
        for k in self.E:
            self.semh[k] = es.enter_context(nc.semaphore("s_" + k))
            self.cnt[k] = 0
            self.seen[k] = {}
        self.floor = {}
        self.uid = 0
        self.nwait = 0

    def newT(self):
        return T(self.floor)

    def chan(self, name):
        sem = self.es.enter_context(self.nc.semaphore("c_" + name))
        key = "c_" + name
        self.semh[key] = sem
        return Chan(key, sem)

    def _need(self, eng, key, val):
        if val <= 0:
            return
        if self.seen[eng].get(key, 0) < val:
            self.E[eng].wait_ge(self.semh[key], val)
            self.seen[eng][key] = val
            self.nwait += 1

    def _deps(self, eng, w, r):
        for t in r:
            for k, v in t.wr.items():
                self._need(eng, k, v)
            if t.excl:
                for k, v in t.rd.items():
                    if k != eng:
                        self._need(eng, k, v)
        for t in w:
            for k, v in t.wr.items():
                if k != eng:
                    self._need(eng, k, v)
            for k, v in t.rd.items():
                if k != eng:
                    self._need(eng, k, v)

    def op(self, eng, fn, w=(), r=()):
        self._deps(eng, w, r)
        inst = fn()
        self.cnt[eng] += 1
        c = self.cnt[eng]
        inst.then_inc(self.semh[eng], 1)
        for t in r:
            t.rd[eng] = c
        for t in w:
            t.wr[eng] = c
        return inst

    def dma(self, q, ch, out, in_, w=(), r=()):
        self._deps(q, w, r)
        inst = self.E[q].dma_start(out=out, in_=in_)
        ch.count += 16
        inst.then_inc(ch.sem, 16)
        for t in r:
            t.rd[ch.key] = ch.count
        for t in w:
            t.wr[ch.key] = ch.count
        return inst

    def snapshot_floor(self):
        self.floor = dict(self.cnt)

    def fence(self, eng, ts_, dummy):
        self._deps(eng, ts_, ts_)
        self.op(eng, lambda: self.E[eng].memset(dummy, 0.0), (), ())


def bc_last(ap, n):
    return bass.AP(ap.tensor, ap.offset, [list(x) for x in ap.ap] + [[0, n]])


def build_program(NS, L, dbg=False):
    nc = bass.Bass("TRN2", target_bir_lowering=False)
    ptab, WTOT = piece_table()
    vtab, NV = vec_table(L)

    xfm = nc.dram_tensor("xfm", [NS, 128, KC, NT], F32, kind="ExternalInput").ap()
    cT = nc.dram_tensor("cT", [128, KC, NS], F32, kind="ExternalInput").ap()
    wada = nc.dram_tensor("wada", [L, 24, 128, KC, 256], F32, kind="ExternalInput").ap()
    ident_d = nc.dram_tensor("ident", [8, 8], F32, kind="ExternalInput").ap()
    vecs = nc.dram_tensor("vecs", [128, NV], F32, kind="ExternalInput").ap()
    wflat = nc.dram_tensor("wflat", [L, WTOT], F32, kind="ExternalInput").ap()
    wsT_d = nc.dram_tensor("wsT", [L, 128, 12, 128], F32, kind="ExternalInput").ap()
    wbd_d = nc.dram_tensor("wbd", [L, 128, 2, 6, 128], F32, kind="ExternalInput").ap()
    bsb_d = nc.dram_tensor("bsb", [L, 128, 6, 128], F32, kind="ExternalInput").ap()
    lnb_d = nc.dram_tensor("lnb", [L, 128, AW], F32, kind="ExternalInput").ap()
    rel_d = nc.dram_tensor("rel", [32, 12], F32, kind="ExternalInput").ap()
    oh_d = nc.dram_tensor("oh", [32, 387], F32, kind="ExternalInput").ap()
    triu_d = nc.dram_tensor("triu", [128, 128], F32, kind="ExternalInput").ap()
    ofm = nc.dram_tensor("ofm", [NS, 128, KC, NT], F32, kind="ExternalOutput").ap()
    wbf_t = nc.dram_tensor("wbf", [L, WTOT], BF16, kind="Internal")
    wbf = wbf_t.ap()
    GROW = 36 * NGW
    gscr_t = nc.dram_tensor("gscr", [128, GROW], BF16, kind="Internal")
    gscr = gscr_t.ap()
    if dbg:
        dbg_d = nc.dram_tensor("dbg", [8, 128, KC, NT], F32, kind="ExternalOutput").ap()
        dbg2_d = nc.dram_tensor("dbg2", [128, 4096], F32, kind="ExternalOutput").ap()

    with ExitStack() as es:
        C = Ctx(nc, es)
        uid = [0]

        def sb(stack, shape, dt, name="t"):
            uid[0] += 1
            return stack.enter_context(nc.sbuf_tensor(f"{name}_{uid[0]}", list(shape), dt))

        @contextmanager
        def scope():
            with ExitStack() as s2:
                yield s2
            C.snapshot_floor()

        def mm(out, lhsT, rhs, start, stop, w, r):
            return C.op("pe", lambda: nc.tensor.matmul(out, lhsT, rhs, start=start, stop=stop), w, r)

        def act(out, in_, func, w, r, **kw):
            return C.op("act", lambda: nc.scalar.activation(out=out, in_=in_, func=func, **kw), w, r)

        def tt(eng, out, in0, in1, op, w, r):
            return C.op(eng, lambda: C.E[eng].tensor_tensor(out=out, in0=in0, in1=in1, op=op), w, r)

        def ts(eng, out, in0, s1, s2, op0, op1, w, r):
            if op1 is None:
                return C.op(eng, lambda: C.E[eng].tensor_scalar(out=out, in0=in0, scalar1=s1, scalar2=None, op0=op0), w, r)
            return C.op(eng, lambda: C.E[eng].tensor_scalar(out=out, in0=in0, scalar1=s1, scalar2=s2, op0=op0, op1=op1), w, r)

        def stt(out, in0, scalar, in1, op0, op1, w, r):
            return C.op("dve", lambda: nc.vector.scalar_tensor_tensor(out=out, in0=in0, scalar=scalar, in1=in1, op0=op0, op1=op1), w, r)

        def cp(eng, out, in_, w, r):
            return C.op(eng, lambda: C.E[eng].tensor_copy(out, in_), w, r)

        def mset(eng, ap, val, w):
            return C.op(eng, lambda: C.E[eng].memset(ap, val), w, ())

        psum = es.enter_context(nc.psum_tensor("psum", [128, 8, 512], F32))
        psT = [T(excl=True) for _ in range(8)]
        bank_rr = [0]
        bank_set = [list(range(8))]

        def bank():
            single_banks = bank_set[0]
            b = single_banks[bank_rr[0] % len(single_banks)]
            bank_rr[0] += 1
            return psum[:, b, :], psT[b]

        xn_sb = sb(es, [128, KC, NT], BF16, "xn")
        xnT = [C.newT() for _ in range(NTG)]
        oc_sb = sb(es, [128, 6, NT], BF16, "oc")
        ocT = C.newT()
        NSLOT = 3
        ring = [sb(es, [128, SLOT], BF16, "ring") for _ in range(NSLOT)]
        ringT = [C.newT() for _ in range(NSLOT)]
        ringC = [C.chan(f"ring{i}") for i in range(NSLOT)]
        ring_rr = [0]
        mskC = [C.chan(f"msk{i}") for i in range(2)]
        vec_sb = sb(es, [128, NV], F32, "vecs")
        vecT = C.newT()
        ada_sb = sb(es, [128, L, 48, NS], F32, "ada")
        adaT = C.newT()
        A_sb = sb(es, [128, L, 2, 8, NS], F32, "Asb")
        AT = C.newT()
        cl_sb = sb(es, [128, 2, L * 6], F32, "cl")
        clT = C.newT()
        ones_bf = sb(es, [128, 128], BF16, "ones")
        onesm_bf = sb(es, [128, 128], BF16, "onesm")
        onesT = C.newT()
        triu_sb = sb(es, [128, 128], F32, "triu")
        triuT = C.newT()
        sq_sb = [sb(es, [128, TG], BF16, "sq") for _ in range(2)]
        sqT = [C.newT() for _ in range(2)]
        rstd_sb = sb(es, [128, TG], F32, "rstd")
        rstdT = C.newT()
        ntmp_sb = [sb(es, [128, TG], F32, "ntmp") for _ in range(2)]
        ntmpT = [C.newT() for _ in range(2)]

        dummy_sb = sb(es, [128, 1], F32, "dummy")
        ROWS = WTOT // 2048
        RCH = 2048
        NCH = (ROWS + RCH - 1) // RCH
        wbfT = [[C.newT() for _ in range(NCH)] for _ in range(L)]
        gscrT = C.newT()
        ofmT = C.newT()
        constC = C.chan("const")
        precastC = [[C.chan(f"pc{i}_{j}") for j in range(NCH)] for i in range(L)]
        outC = C.chan("out")
        gscrC = C.chan("gscr")
        smallC = [C.chan(f"small{i}") for i in range(4)]
        wadaC = [C.chan(f"wada{i}") for i in range(2)]

        def vcol(name, idx):
            o = vtab[name] + idx
            return vec_sb[:, o:o + 1]

        def wpiece(l, name):
            off, e = ptab[name]
            i = ring_rr[0] % NSLOT
            ring_rr[0] += 1
            src = bass.AP(wbf_t, l * WTOT + off, [[e, 128], [1, e]])
            c0_, c1_ = off // (RCH * 2048), (off + 128 * e - 1) // (RCH * 2048)
            C.dma("sp", ringC[i], ring[i][:, 0:e], src, w=[ringT[i]], r=[wbfT[l][ci] for ci in range(c0_, c1_ + 1)])
            return ring[i], ringT[i]

        def precast_chunk(l, ci):
            if True:
                r0 = ci * RCH
                rn = min(RCH, ROWS - r0)
                src = bass.AP(wflat.tensor, l * WTOT + r0 * 2048, [[2048, rn], [1, 2048]])
                dst = bass.AP(wbf_t, l * WTOT + r0 * 2048, [[2048, rn], [1, 2048]])
                C.dma("pool", precastC[l][ci], dst, src, w=[wbfT[l][ci]])

        x_sb = sb(es, [128, KC, NT], F32, "x")
        xT = [C.newT() for _ in range(NTG)]
        xloadC = C.chan("xload")
        for c in range(KC):
            C.dma("pool", xloadC, x_sb[:, c, :], xfm[0, :, c, :], w=xT)
        for ci0 in range(NCH):
            precast_chunk(0, ci0)

        C.dma("sp", constC, vec_sb[:], vecs[:, :], w=[vecT])
        C.dma("sp", constC, triu_sb[:], triu_d[:, :], w=[triuT])
        mset("dve", ones_bf[:], 1.0, [onesT])
        mset("dve", onesm_bf[:], 1.0 / 1024.0, [onesT])

        with scope() as s0:
            ct_sb = sb(s0, [128, KC, NS], F32, "ct")
            ctT = C.newT()
            cond_sb = sb(s0, [128, KC, NS], F32, "cond")
            condT = C.newT()
            rel_sb = sb(s0, [32, 12], F32, "rel")
            relb_sb = sb(s0, [32, 12, 128], F32, "relb")
            oh_sb = sb(s0, [32, 387], F32, "oh")
            relT = C.newT()
            Gb = sb(s0, [128, 36, NGW], BF16, "Gb")
            GbT = C.newT()
            wa_sl = [sb(s0, [128, KC, 256], F32, "wasl") for _ in range(2)]
            ident_sb = sb(s0, [8, 8], F32, "ident")
            atm = [sb(s0, [8, 256], F32, "atm") for _ in range(2)]
            atmT = [C.newT() for _ in range(2)]
            waT = [C.newT() for _ in range(2)]
            tmpv = sb(s0, [128, L * 6], F32, "tmpv")
            tmpvT = C.newT()
            tmpa = sb(s0, [128, 8, NS], F32, "tmpa")
            tmpaT = C.newT()

            C.dma("sp", constC, ct_sb[:], cT[:, :, :], w=[ctT])
            C.dma("sp", constC, rel_sb[:], rel_d[:, :], w=[relT])
            C.dma("sp", constC, oh_sb[:], oh_d[:, :], w=[relT])
            C.dma("sp", constC, ident_sb[:], ident_d[:, :], w=[relT])
            for t_ in (vecT, triuT, ctT, relT):
                t_.wr[constC.key] = constC.count

            act(cond_sb[:], ct_sb[:], AF.Silu, [condT], [ctT])
            for l in range(L):
                for pc in range(24):
                    i = (l * 24 + pc) % 2
                    C.dma("sp", wadaC[i], wa_sl[i][:], wada[l, pc], w=[waT[i]])
                    pb, pt = bank()
                    for kc in range(KC):
                        mm(pb[0:NS, 0:256], cond_sb[:, kc, :], wa_sl[i][:, kc, :],
                           kc == 0, kc == KC - 1, [pt], [waT[i], condT])
                    cp("dve", atm[i][0:NS, :], pb[0:NS, 0:256], [atmT[i]], [pt])
                    for jj in range(2):
                        j = pc * 2 + jj
                        pb2, pt2 = bank()
                        mm(pb2[:, 0:NS], atm[i][0:NS, jj * 128:(jj + 1) * 128], ident_sb[0:NS, 0:NS], True, True,
                           [pt2], [atmT[i], relT])
                        ts("dve", ada_sb[:, l, j, :], pb2[:, 0:NS], vcol("b_ada", l * 48 + j), None, ALU.add, None,
                           [adaT], [pt2, vecT])
            for l in range(L):
                for wh, (part, nm) in enumerate(((1, "norm1"), (4, "norm2"))):
                    ts("dve", tmpa[:], ada_sb[:, l, part * 8:(part + 1) * 8, :], 1.0, None, ALU.add, None,
                       [tmpaT], [adaT])
                    o = vtab[nm] + l * 8
                    tt("dve", A_sb[:, l, wh, :, :], tmpa[:], bc_last(vec_sb[:, o:o + 8], NS), ALU.mult,
                       [AT], [tmpaT, vecT])
            o = vtab["blam"]
            act(tmpv[:], vec_sb[:, o:o + L * 6], AF.Exp, [tmpvT], [vecT], scale=-1.0)
            ts("dve", tmpv[:], tmpv[:], 1.0, None, ALU.add, None, [tmpvT], [tmpvT])
            act(tmpv[:], tmpv[:], AF.Ln, [tmpvT], [tmpvT])
            ts("dve", cl_sb[:, 0, :], tmpv[:], -8.0, None, ALU.mult, None, [clT], [tmpvT])
            ts("dve", cl_sb[:, 1, :], tmpv[:], -16.0, None, ALU.mult, None, [clT], [tmpvT])

            mset("pool", Gb[:], 0.0, [GbT])
            cp("dve", relb_sb[:], bc_last(rel_sb[:, :], 128), [relT], [relT])
            for h in range(12):
                pb, pt = bank()
                mm(pb[:, 0:387], relb_sb[:, h, :], oh_sb[:, :], True, True, [pt], [relT])
                act(Gb[:, h * 3:(h + 1) * 3, 128:257], pb[:, 0:387].rearrange("p (c d) -> p c d", c=3), AF.Exp,
                    [GbT], [pt])
            C.dma("sp", gscrC, gscr[:, :], Gb[:].rearrange("p a b -> p (a b)"), w=[gscrT], r=[GbT])
            C.fence("dve", [GbT], dummy_sb[:])

        def norm_tg(tg, Avec, shvec, dst_fn, dstT, final=False):
            tsl = slice(tg * TG, (tg + 1) * TG)
            pb, pt = bank()
            for c in range(KC):
                i = c % 2
                act(sq_sb[i][:], x_sb[:, c, tsl], AF.Square, [sqT[i]], [xT[tg]])
                mm(pb, onesm_bf[:], sq_sb[i][:], c == 0, c == KC - 1, [pt], [sqT[i], onesT])
            act(rstd_sb[:], pb, AF.Sqrt, [rstdT], [pt], bias=EPS)
            C.op("dve", lambda: nc.vector.reciprocal(rstd_sb[:], rstd_sb[:]), [rstdT], [rstdT])
            if dbg and not dbg_r[0]:
                dbg_r[0] = 1
                C.dma("sp", outC, dbg2_d[:, 3584:4096], rstd_sb[:], w=[ofmT], r=[rstdT])
            for c in range(KC):
                if final:
                    stt(x_sb[:, c, tsl], x_sb[:, c, tsl], Avec(c), rstd_sb[:], ALU.mult, ALU.mult,
                        [xT[tg]], [xT[tg], rstdT, vecT, AT])
                else:
                    i = c % 2
                    stt(ntmp_sb[i][:], x_sb[:, c, tsl], Avec(c), rstd_sb[:], ALU.mult, ALU.mult,
                        [ntmpT[i]], [xT[tg], rstdT, vecT, AT])
                    act(dst_fn(c), ntmp_sb[i][:], AF.Identity, [dstT], [ntmpT[i], adaT], bias=shvec(c), scale=1.0)


        def norm_steps(tg, Avec, shvec, dst_fn, dstT, final=False, nbank=7):
            tsl = slice(tg * TG, (tg + 1) * TG)
            pb, pt = psum[:, nbank, :], psT[nbank]

            def sqr(c):
                i = c % 2
                act(sq_sb[i][:], x_sb[:, c, tsl], AF.Square, [sqT[i]], [xT[tg]])

            def mmc(c):
                i = c % 2
                mm(pb, onesm_bf[:], sq_sb[i][:], c == 0, c == KC - 1, [pt], [sqT[i], onesT])

            def sttc(c):
                if final:
                    stt(x_sb[:, c, tsl], x_sb[:, c, tsl], Avec(c), rstd_sb[:], ALU.mult, ALU.mult,
                        [xT[tg]], [xT[tg], rstdT, vecT, AT])
                else:
                    i = c % 2
                    stt(ntmp_sb[i][:], x_sb[:, c, tsl], Avec(c), rstd_sb[:], ALU.mult, ALU.mult,
                        [ntmpT[i]], [xT[tg], rstdT, vecT, AT])

            def idc(c):
                if not final:
                    i = c % 2
                    act(dst_fn(c), ntmp_sb[i][:], AF.Identity, [dstT], [ntmpT[i], adaT], bias=shvec(c), scale=1.0)

            steps = []
            for j in range(5):
                def st_(j=j):
                    if j >= 1:
                        mmc(2 * j - 2)
                        mmc(2 * j - 1)
                    if j < 4:
                        sqr(2 * j)
                        sqr(2 * j + 1)
                steps.append(st_)
            steps.append(lambda: act(rstd_sb[:], pb, AF.Sqrt, [rstdT], [pt], bias=EPS))
            steps.append(lambda: C.op("dve", lambda: nc.vector.reciprocal(rstd_sb[:], rstd_sb[:]), [rstdT], [rstdT]))
            for j in range(5):
                def st2_(j=j):
                    if j >= 1:
                        idc(2 * j - 2)
                        idc(2 * j - 1)
                    if j < 4:
                        sttc(2 * j)
                        sttc(2 * j + 1)
                steps.append(st2_)
            return steps

        def tok(cfg, tile):
            dil = DILS[cfg]
            if cfg == 0:
                return slice(tile * 128, (tile + 1) * 128)
            if cfg == 1:
                r, n = tile // 4, tile % 4
                s0_ = n * 512 + r
                return slice(s0_, s0_ + 127 * 4 + 1, 4)
            r = tile
            return slice(r, r + 127 * 16 + 1, 16)

        dbg_i = [0]
        dbg_r = [0]
        marks = []

        def mark(name):
            marks.append((name, C.cnt["pe"]))

        def dump_x(s):
            if dbg and s == 0 and dbg_i[0] < 8:
                C.dma("sp", outC, dbg_d[dbg_i[0]], x_sb[:], w=[ofmT], r=xT)
                dbg_i[0] += 1

        def dump_bf(src, nchunk, rT):
            if dbg and dbg_i[0] < 8:
                with scope() as sd:
                    tmp = sb(sd, [128, NT], F32, "dbgt")
                    tT = C.newT()
                    for ch in range(nchunk):
                        cp("dve", tmp[:], src[:, ch, :], [tT], rT)
                        C.dma("sp", outC, dbg_d[dbg_i[0], :, ch, :], tmp[:], w=[ofmT], r=[tT])
                    C.fence("dve", [tT], dummy_sb[:])
                dbg_i[0] += 1

        if dbg:
            n1 = L * 48 * NS
            C.dma("sp", outC, dbg2_d[:, 0:n1], ada_sb[:].rearrange("p a b c -> p (a b c)"), w=[ofmT], r=[adaT])
            n2 = L * 16 * NS
            C.dma("sp", outC, dbg2_d[:, 1024:1024 + n2], A_sb[:].rearrange("p a b c d -> p (a b c d)"), w=[ofmT], r=[AT])
            C.dma("sp", outC, dbg2_d[:, 2048:2048 + 12 * L], cl_sb[:].rearrange("p a b -> p (a b)"), w=[ofmT], r=[clT])
            C.dma("sp", outC, dbg2_d[:, 2560:2560 + NV], vec_sb[:], w=[ofmT], r=[vecT])
        for s in range(NS):
            if s > 0:
                for c in range(KC):
                    C.dma("pool", xloadC, x_sb[:, c, :], xfm[s, :, c, :], w=xT)

            if dbg and s == 0:
                C.dma("sp", outC, dbg_d[7], x_sb[:], w=[ofmT], r=xT)
            for l in range(L):
                def adac(part, c):
                    return ada_sb[:, l, part * 8 + c, s:s + 1]

                if l == 0:
                    for tg in range(NTG):
                        norm_tg(tg, lambda c: A_sb[:, l, 0, c, s:s + 1], lambda c: adac(0, c),
                                lambda c, tg=tg: xn_sb[:, c, tg * TG:(tg + 1) * TG], xnT[tg])
                if s == 0 and l == 0:
                    dump_bf(xn_sb[:], 8, xnT)

                mark(f"s{s}l{l}.P2")
                def maybe_precast(k):
                    if s == 0 and l + 1 < L and k < NCH:
                        precast_chunk(l + 1, k)

                maybe_precast(0)
                bank_set[0] = [6, 7]
                with scope() as s2:
                    q_sb = sb(s2, [128, NT], BF16, "q")
                    k_sb = sb(s2, [128, NT], BF16, "k")
                    qT_, kT_ = C.newT(), C.newT()
                    V_sb = sb(s2, [128, 3, 16, 192], BF16, "V")
                    VT = C.newT()
                    mset("pool", V_sb[:, :, :, 64:128], 0.0, [VT])
                    ones_pad = sb(s2, [128, 192], BF16, "onesp")
                    mset("pool", ones_pad[:, 0:64], 1.0, [VT])
                    mset("pool", ones_pad[:, 64:128], 0.0, [VT])
                    mset("pool", ones_pad[:, 128:192], 1.0, [VT])
                    acc = sb(s2, [128, 2, NT], F32, "acc")
                    accT = C.newT()
                    msk = [sb(s2, [128, 3, 2, 2, 128], BF16, "msk") for _ in range(2)]
                    mskT = [C.newT() for _ in range(2)]
                    pT = [sb(s2, [128, 2, 2, 128], BF16, "pT") for _ in range(3)]
                    pTT = [C.newT() for _ in range(3)]
                    for hp in range(6):
                        if hp == 2:
                            maybe_precast(1)
                        if hp == 4:
                            maybe_precast(2)
                        wq, wqT = wpiece(l, f"qkv{hp}")
                        wqv = wq[:, 0:3072].rearrange("p (a k n) -> p a k n", a=3, k=8)
                        mi = hp % 2
                        mo = (2 * hp) * 3 * NGW + 128
                        for h2m in range(2):
                            msrc = bass.AP(gscr_t, mo + h2m * 3 * NGW, [[GROW - 1, 128], [NGW, 3], [1, 256]])
                            C.dma("sp", mskC[mi], msk[mi][:, :, h2m, :, :].rearrange("p c j q -> p c (j q)"), msrc,
                                  w=[mskT[mi]], r=[gscrT])
                        for tg in range(NTG):
                            tsl = slice(tg * TG, (tg + 1) * TG)
                            pb, pt = bank()
                            for kc in range(KC):
                                mm(pb, wqv[:, 0, kc, :], xn_sb[:, kc, tsl], kc == 0, kc == KC - 1, [pt], [wqT, xnT[tg]])
                            act(q_sb[:, tsl], pb, AF.Copy, [qT_], [pt], scale=0.125)
                            pb, pt = bank()
                            for kc in range(KC):
                                mm(pb, wqv[:, 1, kc, :], xn_sb[:, kc, tsl], kc == 0, kc == KC - 1, [pt], [wqT, xnT[tg]])
                            act(k_sb[:, tsl], pb, AF.Copy, [kT_], [pt])
                        for cfg in range(3):
                            for grp in range(4):
                                pb, pt = bank()
                                pv = pb.rearrange("p (a n) -> p a n", a=4)
                                for ti in range(4):
                                    tile = grp * 4 + ti
                                    for kc in range(KC):
                                        mm(pv[:, ti, :], xn_sb[:, kc, tok(cfg, tile)], wqv[:, 2, kc, :],
                                           kc == 0, kc == KC - 1, [pt], [wqT] + xnT)
                                act(V_sb[:, cfg, grp * 4:(grp + 1) * 4, :].rearrange("p t (h e) -> p t h e", h=3)[:, :, 0:3:2, :],
                                    pv.rearrange("p t (h e) -> p t h e", h=2), AF.Copy, [VT], [pt])
                        blocks = []
                        for cfg in range(3):
                            for tile in range(16):
                                if cfg == 0:
                                    prev = tile - 1 if tile > 0 else None
                                elif cfg == 1:
                                    prev = tile - 1 if tile % 4 > 0 else None
                                else:
                                    prev = None
                                blocks.append((cfg, tile, prev))

                        def emit_S(bi):
                            cfg, tile, prev = blocks[bi]
                            nj = 2 if prev is not None else 1
                            pr = (bi % 3) * 2
                            pp2 = psum[:, pr:pr + 2, 0:256].rearrange("p h (j q) -> p h j q", j=2)
                            for h2 in range(2):
                                hs = slice(h2 * 64, (h2 + 1) * 64)
                                for j in range(nj):
                                    kt = tile if j == 0 else prev
                                    mm(pp2[:, h2, j, :], k_sb[hs, tok(cfg, kt)], q_sb[hs, tok(cfg, tile)], True, True,
                                       [psT[pr + h2]], [kT_, qT_])
                            pi = bi % 3
                            act(pT[pi][:, :, 0:nj, :], pp2[:, :, 0:nj, :], AF.Exp, [pTT[pi]], [psT[pr], psT[pr + 1]])
                            tt("pool" if bi % 2 == 1 else "dve", pT[pi][:, :, 0:nj, :], pT[pi][:, :, 0:nj, :],
                               msk[mi][:, cfg, :, 0:nj, :], ALU.mult, [pTT[pi]], [pTT[pi], mskT[mi]])

                        def emit_PV(bi):
                            cfg, tile, prev = blocks[bi]
                            nj = 2 if prev is not None else 1
                            pi = bi % 3
                            pb, pt = bank()
                            po = pb[:, 0:256].rearrange("p (a q) -> p a q", a=2)
                            n_mm = 2 * nj
                            k_ = 0
                            for h2 in range(2):
                                for j in range(nj):
                                    kt = tile if j == 0 else prev
                                    mm(po[:, 0, :], V_sb[:, cfg, kt, h2 * 64:h2 * 64 + 128], pT[pi][:, h2, j, :],
                                       k_ == 0, k_ == n_mm - 1, [pt], [VT, pTT[pi]])
                                    k_ += 1
                            k_ = 0
                            for h2 in range(2):
                                for j in range(nj):
                                    mm(po[:, 1, :], ones_pad[:, h2 * 64:h2 * 64 + 128], pT[pi][:, h2, j, :],
                                       k_ == 0, k_ == n_mm - 1, [pt], [VT, pTT[pi]])
                                    k_ += 1
                            dst = acc[:, :, tok(cfg, tile)]
                            if cfg == 0:
                                cp("dve", dst, po, [accT], [pt])
                            else:
                                tt("dve", dst, dst, po, ALU.add, [accT], [accT, pt])

                        emit_S(0)
                        emit_S(1)
                        for bi in range(len(blocks)):
                            if bi + 2 < len(blocks):
                                emit_S(bi + 2)
                            emit_PV(bi)
                        C.op("dve", lambda: nc.vector.reciprocal(acc[:, 1, :], acc[:, 1, :]), [accT], [accT])
                        tt("dve", oc_sb[:, hp, :], acc[:, 0, :], acc[:, 1, :], ALU.mult, [ocT], [accT])
                bank_set[0] = list(range(8))
                if s == 0 and l == 0:
                    dump_bf(oc_sb[:], 6, [ocT])

                mark(f"s{s}l{l}.P3")
                with scope() as s3:
                    oa_sb = sb(s3, [128, 6, TG], BF16, "oa")
                    ob_sb = sb(s3, [128, 6, TG], BF16, "ob")
                    oaT, obT = C.newT(), C.newT()
                    WmT = sb(s3, [128, 12, 128], BF16, "WmT")
                    WmTT = C.newT()
                    wbd = sb(s3, [128, 2, 6, 128], BF16, "wbd")
                    wbdT = C.newT()
                    bsb = sb(s3, [128, 6, 128], F32, "bsb")
                    bsbT = C.newT()
                    lnb = sb(s3, [128, AW], F32, "lnb")
                    lnbT = C.newT()
                    xh = sb(s3, [128, 6, 3], F32, "xh")
                    hc = sb(s3, [128, 6], F32, "hc")
                    carT = C.newT()
                    C.dma("pool", smallC[0], WmT[:], wsT_d[l], w=[WmTT])
                    C.dma("pool", smallC[1], wbd[:], wbd_d[l], w=[wbdT])
                    C.dma("sp", smallC[2], bsb[:], bsb_d[l], w=[bsbT])
                    C.dma("sp", smallC[3], lnb[:], lnb_d[l], w=[lnbT])
                    tt("pool", WmT[:], WmT[:], bass.AP(triu_sb[:].tensor, triu_sb[:].offset,
                                                       [list(triu_sb[:].ap[0]), [0, 12], [1, 128]]),
                       ALU.mult, [WmTT], [WmTT, triuT])
                    for tg in range(NTG):
                        tsl = slice(tg * TG, (tg + 1) * TG)
                        if tg == 0:
                            maybe_precast(3)
                        if tg == 2:
                            maybe_precast(4)
                        mark(f"s{s}l{l}.A{tg}")
                        with scope() as sa:
                            u_sb = sb(sa, [128, 6, TG], BF16, "u")
                            uT = C.newT()
                            gv = [sb(sa, [128, AW], F32, "gv") for _ in range(4)]
                            gvT = [C.newT() for _ in range(4)]
                            v_sb = [sb(sa, [128, AW], BF16, "v") for _ in range(2)]
                            vT = [C.newT() for _ in range(2)]
                            st = sb(sa, [128, 4, 2, 6], F32, "st")
                            mv = sb(sa, [128, 4, 2], F32, "mv")
                            rs = sb(sa, [128, 4], F32, "rs")
                            stT = C.newT()
                            atmp = [sb(sa, [128, 3, 128], F32, "atmp")] * 2
                            atmpT = [C.newT()] * 2
                            for i in range(2):
                                au, auT = wpiece(l, f"au{i}")
                                auv = au[:, 0:3072].rearrange("p (k n) -> p k n", k=8)
                                for cc in range(3):
                                    c = i * 3 + cc
                                    pb, pt = bank()
                                    for kc in range(KC):
                                        mm(pb, auv[:, kc, cc * 128:(cc + 1) * 128], xn_sb[:, kc, tsl], kc == 0, kc == KC - 1,
                                           [pt], [auT, xnT[tg]])
                                    act(u_sb[:, c, :], pb, AF.Gelu_apprx_tanh, [uT], [pt])
                            av0, av0T = wpiece(l, "av0")
                            av1, av1T = wpiece(l, "av1")
                            avv = [av0[:, 0:3072].rearrange("p (k n) -> p k n", k=4),
                                   av1[:, 0:3072].rearrange("p (k n) -> p k n", k=4)]
                            for ti in range(4):
                                tile = tg * 4 + ti
                                tks = slice(tile * 128, (tile + 1) * 128)
                                for half in range(2):
                                    pb, pt = bank()
                                    for kc in range(KC):
                                        mm(pb[:, 0:384], xn_sb[:, kc, tks], avv[kc // 4][:, kc % 4, half * 384:(half + 1) * 384],
                                           kc == 0, kc == KC - 1, [pt], [av0T, av1T, xnT[tg]])
                                    act(gv[ti][:, half * 384:(half + 1) * 384], pb[:, 0:384], AF.Gelu_apprx_tanh, [gvT[ti]], [pt])
                                    C.op("dve", lambda half=half, ti=ti: nc.vector.bn_stats(st[:, ti, half, :], gv[ti][:, half * 384:(half + 1) * 384]),
                                         [stT], [gvT[ti]])
                                C.op("dve", lambda ti=ti: nc.vector.bn_aggr(mv[:, ti, :], st[:, ti, :, :].rearrange("p a b -> p (a b)")), [stT], [stT])
                            act(rs[:], mv[:, :, 1], AF.Sqrt, [stT], [stT], bias=EPS)
                            C.op("dve", lambda: nc.vector.reciprocal(rs[:], rs[:]), [stT], [stT])
                            for ti in range(4):
                                b2 = ti % 2
                                ts("dve", gv[ti][:], gv[ti][:], mv[:, ti, 0:1], rs[:, ti:ti + 1], ALU.subtract, ALU.mult,
                                   [gvT[ti]], [gvT[ti], stT])
                                tt("dve", v_sb[b2][:], gv[ti][:], lnb[:], ALU.mult, [vT[b2]], [gvT[ti], lnbT])
                                for half in range(2):
                                    pb, pt = bank()
                                    for cc in range(3):
                                        c = half * 3 + cc
                                        for g2 in range(2):
                                            g = 2 * c + g2
                                            mm(pb[g2 * 64:(g2 + 1) * 64, cc * 128:(cc + 1) * 128], v_sb[b2][:, g * 64:(g + 1) * 64],
                                               WmT[:, g, :], True, True, [pt], [vT[b2], WmTT])
                                    tt("dve", atmp[half][:], pb[:, 0:384].rearrange("p (a t) -> p a t", a=3),
                                       bsb[:, half * 3:(half + 1) * 3, :], ALU.add, [atmpT[half]], [pt, bsbT])
                                    tt("pool", oa_sb[:, half * 3:(half + 1) * 3, ti * 128:(ti + 1) * 128], atmp[half][:],
                                       u_sb[:, half * 3:(half + 1) * 3, ti * 128:(ti + 1) * 128], ALU.mult,
                                       [oaT], [atmpT[half], uT])
                        mark(f"s{s}l{l}.B{tg}")
                        with scope() as sbb:
                            xbuf = [sb(sbb, [128, TG + 3], F32, "xbuf") for _ in range(2)]
                            xb = [sb(sbb, [128, TG], F32, "xb") for _ in range(2)]
                            xbb = [sb(sbb, [128, TG], BF16, "xbb") for _ in range(2)]
                            rbuf = [sb(sbb, [128, TG], F32, "rbuf") for _ in range(2)]
                            ibuf = [sb(sbb, [128, TG], F32, "ibuf") for _ in range(2)]
                            mbuf = [sb(sbb, [128, TG], F32, "mbuf") for _ in range(2)]
                            xbufT = [C.newT() for _ in range(2)]
                            xbT = [C.newT() for _ in range(2)]
                            xbbT = [C.newT() for _ in range(2)]
                            rT_ = [C.newT() for _ in range(2)]
                            iT_ = [C.newT() for _ in range(2)]
                            mT_ = [C.newT() for _ in range(2)]
                            ocw = vtab["bcw"] + l * 24
                            for cp_ in range(3):
                                pair = (2 * cp_, 2 * cp_ + 1)
                                st = {}
                                for c in pair:
                                    i = c % 2
                                    bw, bwT = wpiece(l, f"b{c}")
                                    bwv = bw[:, 0:2048].rearrange("p (a k n) -> p a k n", a=2, k=8)
                                    psx, psxT = bank()
                                    for kc in range(KC):
                                        mm(psx, bwv[:, 0, kc, :], xn_sb[:, kc, tsl], kc == 0, kc == KC - 1, [psxT], [bwT, xnT[tg]])
                                    psg, psgT = bank()
                                    for kc in range(KC):
                                        mm(psg, bwv[:, 1, kc, :], xn_sb[:, kc, tsl], kc == 0, kc == KC - 1, [psgT], [bwT, xnT[tg]])
                                    if tg == 0:
                                        mset("dve", xbuf[i][:, 0:3], 0.0, [xbufT[i]])
                                    else:
                                        cp("dve", xbuf[i][:, 0:3], xh[:, c, :], [xbufT[i]], [carT])
                                    act(xbuf[i][:, 3:TG + 3], psx, AF.Copy, [xbufT[i]], [psxT])
                                    act(xb[i][:], psx, AF.Identity, [xbT[i]], [psxT, vecT],
                                        bias=vcol("bcb", l * 6 + c), scale=vec_sb[:, ocw + 18 + c:ocw + 18 + c + 1])
                                    cp("dve", xh[:, c, :], xbuf[i][:, TG:TG + 3], [carT], [xbufT[i]])
                                    for k in range(0, 3):
                                        stt(xb[i][:], xbuf[i][:, k:k + TG], vec_sb[:, ocw + k * 6 + c:ocw + k * 6 + c + 1], xb[i][:],
                                            ALU.mult, ALU.add, [xbT[i]], [xbufT[i], xbT[i], vecT])
                                    act(xbb[i][:], xb[i][:], AF.Copy, [xbbT[i]], [xbT[i]])
                                    psr, psrT = bank()
                                    mm(psr, wbd[:, 0, c, :], xbb[i][:], True, True, [psrT], [wbdT, xbbT[i]])
                                    psi, psiT = bank()
                                    mm(psi, wbd[:, 1, c, :], xbb[i][:], True, True, [psiT], [wbdT, xbbT[i]])
                                    st[c] = (psg, psgT, psr, psrT, psi, psiT)
                                for c in pair:
                                    i = c % 2
                                    psg, psgT, psr, psrT, psi, psiT = st[c]
                                    act(rbuf[i][:], psr, AF.Sigmoid, [rT_[i]], [psrT, vecT], bias=vcol("bba", l * 6 + c), scale=1.0)
                                    act(ibuf[i][:], psi, AF.Sigmoid, [iT_[i]], [psiT, vecT], bias=vcol("bbx", l * 6 + c), scale=1.0)
                                for c in pair:
                                    i = c % 2
                                    act(mbuf[i][:], rbuf[i][:], AF.Exp, [mT_[i]], [rT_[i], clT], scale=cl_sb[:, 1, l * 6 + c:l * 6 + c + 1])
                                    act(rbuf[i][:], rbuf[i][:], AF.Exp, [rT_[i]], [rT_[i], clT], scale=cl_sb[:, 0, l * 6 + c:l * 6 + c + 1])
                                    ts("pool", mbuf[i][:], mbuf[i][:], -1.0, 1.0, ALU.mult, ALU.add, [mT_[i]], [mT_[i]])
                                    tt("pool", ibuf[i][:], ibuf[i][:], xb[i][:], ALU.mult, [iT_[i]], [iT_[i], xbT[i]])
                                for c in pair:
                                    i = c % 2
                                    act(mbuf[i][:], mbuf[i][:], AF.Sqrt, [mT_[i]], [mT_[i]])
                                    if tg == 0:
                                        mset("dve", mbuf[i][:, 0:1], 1.0, [mT_[i]])
                                    tt("dve", ibuf[i][:], ibuf[i][:], mbuf[i][:], ALU.mult, [iT_[i]], [iT_[i], mT_[i]])
                                    init = 0.0 if tg == 0 else hc[:, c:c + 1]
                                    C.op("dve", lambda init=init, i=i: nc.vector.tensor_tensor_scan(mbuf[i][:], rbuf[i][:], ibuf[i][:], init, ALU.mult, ALU.add),
                                         [mT_[i]], [mT_[i], rT_[i], iT_[i], carT])
                                    cp("dve", hc[:, c:c + 1], mbuf[i][:, TG - 1:TG], [carT], [mT_[i]])
                                for c in pair:
                                    i = c % 2
                                    psg, psgT, psr, psrT, psi, psiT = st[c]
                                    act(xb[i][:], psg, AF.Gelu_apprx_tanh, [xbT[i]], [psgT])
                                    tt("pool", ob_sb[:, c, :], mbuf[i][:], xb[i][:], ALU.mult, [obT], [mT_[i], xbT[i]])
                        mark(f"s{s}l{l}.M{tg}")
                        with scope() as sm:
                            gate = sb(sm, [128, 3, TG], F32, "gate")
                            gateT = C.newT()
                            mg = sb(sm, [128, 8, TG], BF16, "mg")
                            mgT = C.newT()
                            t0 = sb(sm, [128, TG], F32, "t0")
                            t1 = sb(sm, [128, TG], F32, "t1")
                            t0T, t1T = C.newT(), C.newT()
                            for c in range(8):
                                gt, gtT = wpiece(l, f"gt{c}")
                                gtv = gt[:, 0:3072].rearrange("p (a k n) -> p a k n", a=3, k=8)
                                for X in range(3):
                                    pb, pt = bank()
                                    for kc in range(KC):
                                        mm(pb, gtv[:, X, kc, :], xn_sb[:, kc, tsl], kc == 0, kc == KC - 1, [pt], [gtT, xnT[tg]])
                                    act(gate[:, X, :], pb, AF.Sigmoid, [gateT], [pt])
                                pw, pwT = wpiece(l, f"pp{c}")
                                pwv = pw[:, 0:2304].rearrange("p (a k n) -> p a k n", a=3, k=6)
                                srcs = ((lambda k: oa_sb[:, k, :], oaT), (lambda k: ob_sb[:, k, :], obT),
                                        (lambda k: oc_sb[:, k, tsl], ocT))
                                for X in range(3):
                                    pb, pt = bank()
                                    fsrc, sT_ = srcs[X]
                                    for k in range(6):
                                        mm(pb, pwv[:, X, k, :], fsrc(k), k == 0, k == 5, [pt], [pwT, sT_])
                                    if X == 0:
                                        tt("dve", t0[:], pb, gate[:, 0, :], ALU.mult, [t0T], [pt, gateT])
                                    elif X == 1:
                                        tt("dve", t1[:], pb, gate[:, 1, :], ALU.mult, [t1T], [pt, gateT])
                                        tt("pool", t0[:], t0[:], t1[:], ALU.add, [t0T], [t0T, t1T])
                                    else:
                                        tt("dve", t1[:], pb, gate[:, 2, :], ALU.mult, [t1T], [pt, gateT])
                                        tt("pool", mg[:, c, :], t0[:], t1[:], ALU.add, [mgT], [t0T, t1T])
                            c = 0
                            for i, n in enumerate((384, 384, 256)):
                                wo, woT = wpiece(l, f"wo{i}")
                                wov = wo[:, 0:8 * n].rearrange("p (k n) -> p k n", k=8)
                                for cc in range(n // 128):
                                    pb, pt = bank()
                                    for k in range(8):
                                        mm(pb, wov[:, k, cc * 128:(cc + 1) * 128], mg[:, k, :], k == 0, k == 7, [pt], [woT, mgT])
                                    stt(x_sb[:, c, tsl], pb, adac(2, c), x_sb[:, c, tsl], ALU.mult, ALU.add,
                                        [xT[tg]], [pt, xT[tg], adaT])
                                    c += 1
                dump_x(s) if l == 0 else None

                maybe_precast(5)
                maybe_precast(6)
                mark(f"s{s}l{l}.P4")
                with scope() as s4:
                    xn2b = [sb(s4, [128, KC, TG], BF16, "xn2") for _ in range(2)]
                    xn2Tb = [C.newT() for _ in range(2)]
                    gbuf = [sb(s4, [128, TG + 2], F32, "gbuf") for _ in range(2)]
                    gbufT = [C.newT() for _ in range(2)]
                    tb = [sb(s4, [128, TG], F32, "tb") for _ in range(2)]
                    tbT = [C.newT() for _ in range(2)]
                    ub = [sb(s4, [128, TG], BF16, "ub") for _ in range(2)]
                    ubT = [C.newT() for _ in range(2)]
                    hbuf = sb(s4, [128, NF, TG], BF16, "hbuf")
                    hbufT = [C.newT() for _ in range(NF)]
                    fh = sb(s4, [128, NF, 2], F32, "fh")
                    fhT = C.newT()
                    def next_norm(tgn, stepped):
                        fn = norm_steps if stepped else norm_tg
                        if l + 1 < L:
                            return fn(tgn, lambda c: A_sb[:, l + 1, 0, c, s:s + 1],
                                      lambda c: ada_sb[:, l + 1, c, s:s + 1],
                                      lambda c: xn_sb[:, c, tgn * TG:(tgn + 1) * TG], xnT[tgn])
                        return fn(tgn, lambda c: vcol("final", c), None, None, None, final=True)

                    bank_set[0] = list(range(7))

                    for tg in range(NTG):
                        tsl = slice(tg * TG, (tg + 1) * TG)
                        xn2, xn2T = xn2b[tg % 2], xn2Tb[tg % 2]
                        if tg == 0:
                            norm_tg(0, lambda c: A_sb[:, l, 1, c, s:s + 1], lambda c: adac(3, c),
                                    lambda c: xn2b[0][:, c, :], xn2Tb[0])
                        sched = {}
                        if tg + 1 < NTG:
                            n2 = norm_steps(tg + 1, lambda c: A_sb[:, l, 1, c, s:s + 1], lambda c: adac(3, c),
                                            lambda c, nb=(tg + 1) % 2: xn2b[nb][:, c, :], xn2Tb[(tg + 1) % 2])
                            for k_, st_ in enumerate(n2):
                                sched.setdefault(k_, []).append(st_)
                        if tg > 0:
                            nn = next_norm(tg - 1, True)
                            for k_, st_ in enumerate(nn):
                                sched.setdefault(10 + k_, []).append(st_)
                        for f in range(NF):
                            i = f % 2
                            for st_ in sched.get(f, []):
                                st_()
                            gu, guT = wpiece(l, f"gu{f}")
                            guv = gu[:, 0:2048].rearrange("p (a k n) -> p a k n", a=2, k=8)
                            psg, psgT = bank()
                            for kc in range(KC):
                                mm(psg, guv[:, 0, kc, :], xn2[:, kc, :], kc == 0, kc == KC - 1, [psgT], [guT, xn2T])
                            psu, psuT = bank()
                            for kc in range(KC):
                                mm(psu, guv[:, 1, kc, :], xn2[:, kc, :], kc == 0, kc == KC - 1, [psuT], [guT, xn2T])
                            if tg == 0:
                                mset("dve", gbuf[i][:, 0:2], 0.0, [gbufT[i]])
                            else:
                                cp("dve", gbuf[i][:, 0:2], fh[:, f, :], [gbufT[i]], [fhT])
                            o = vtab["fcw"] + l * 3 * NF
                            act(gbuf[i][:, 2:TG + 2], psg, AF.Copy, [gbufT[i]], [psgT])
                            act(tb[i][:], psg, AF.Identity, [tbT[i]], [psgT, vecT],
                                bias=vcol("fcb", l * NF + f), scale=vec_sb[:, o + 2 * NF + f:o + 2 * NF + f + 1])
                            cp("dve", fh[:, f, :], gbuf[i][:, TG:TG + 2], [fhT], [gbufT[i]])
                            for k in range(0, 2):
                                stt(tb[i][:], gbuf[i][:, k:k + TG], vec_sb[:, o + k * NF + f:o + k * NF + f + 1], tb[i][:],
                                    ALU.mult, ALU.add, [tbT[i]], [gbufT[i], tbT[i], vecT])
                            act(tb[i][:], tb[i][:], AF.Gelu_apprx_tanh, [tbT[i]], [tbT[i]])
                            act(ub[i][:], psu, AF.Copy, [ubT[i]], [psuT])
                            tt("pool", hbuf[:, f, :], tb[i][:], ub[i][:], ALU.mult, [hbufT[f]], [ubT[i], tbT[i]])
                        for c in range(8):
                            dn, dnT = wpiece(l, f"dn{c}")
                            dnv = dn[:, 0:NF * 128].rearrange("p (k n) -> p k n", k=NF)
                            pb, pt = bank()
                            for f in range(NF):
                                mm(pb, dnv[:, f, :], hbuf[:, f, :], f == 0, f == NF - 1, [pt], [dnT, hbufT[f]])
                            stt(x_sb[:, c, tsl], pb, adac(5, c), x_sb[:, c, tsl], ALU.mult, ALU.add,
                                [xT[tg]], [pt, xT[tg], adaT])
                        if tg == NTG - 1:
                            next_norm(tg, False)
                    bank_set[0] = list(range(8))
                dump_x(s) if l == 0 else None

            C.dma("sp", outC, ofm[s], x_sb[:], w=[ofmT], r=xT)

        C._need("sp", outC.key, outC.count)
        mark("end")
        build_program.stats = dict(C.cnt, nwait=C.nwait)
        build_program.marks = marks
    return nc


def prepare_inputs(NS, L, ncores, x, c, w_ada, b_ada, norm1, w_in, a_ln, a_ws, a_bs, b_conv_w, b_conv_b, b_wa, b_ba,
                   b_wx, b_bx, b_lam, rel_table, p_a, p_b, p_c, w_out, norm2, f_wgate, f_wup,
                   f_conv_w, f_conv_b, f_wdown, final_norm):
    f32 = np.float32
    x = np.asarray(x, f32)
    c = np.asarray(c, f32)
    args = [np.asarray(a, f32) for a in (w_in, p_a, p_b, p_c, w_out, f_wgate, f_wup, f_wdown)]
    wflat = np.stack([host_wflat(l, *args) for l in range(L)], axis=0)
    w_ada = np.asarray(w_ada, f32)
    wada = np.ascontiguousarray(w_ada[:L].reshape(L, KC, 128, 24, 256).transpose(0, 3, 2, 1, 4))
    vecs = host_vecs(L, np.asarray(b_ada, f32), np.asarray(norm1, f32), np.asarray(norm2, f32),
                     np.asarray(final_norm, f32), np.asarray(b_conv_w, f32), np.asarray(b_conv_b, f32),
                     np.asarray(b_ba, f32), np.asarray(b_bx, f32), np.asarray(b_lam, f32),
                     np.asarray(f_conv_w, f32), np.asarray(f_conv_b, f32))
    a_ws = np.asarray(a_ws, f32)
    wsT = np.ascontiguousarray(a_ws[:L].transpose(0, 3, 1, 2))
    b_wa = np.asarray(b_wa, f32)
    b_wx = np.asarray(b_wx, f32)
    wbd = np.zeros((L, 128, 2, 6, 128), f32)
    for wi_, wsrc in enumerate((b_wa, b_wx)):
        for cch in range(6):
            for g2 in range(2):
                wbd[:, g2 * 64:(g2 + 1) * 64, wi_, cch, g2 * 64:(g2 + 1) * 64] = wsrc[:L, 2 * cch + g2]
    a_bs = np.asarray(a_bs, f32)
    bsb = np.ascontiguousarray(np.repeat(a_bs[:L].reshape(L, 6, 2, 1, 128), 64, axis=3).reshape(L, 6, 128, 128)
                               .transpose(0, 2, 1, 3))
    lnb = np.ascontiguousarray(np.broadcast_to(np.asarray(a_ln, f32)[:L, None, :], (L, 128, AW)))
    oh, triu = host_consts()
    rel = np.ascontiguousarray(np.asarray(rel_table, f32))
    in_maps = []
    for k in range(ncores):
        xs = x[k * NS:(k + 1) * NS]
        xfm = np.ascontiguousarray(xs.reshape(NS, NT, KC, 128).transpose(0, 3, 2, 1))
        cs = c[k * NS:(k + 1) * NS]
        cTk = np.ascontiguousarray(cs.reshape(NS, KC, 128).transpose(2, 1, 0))
        in_maps.append({"xfm": xfm, "cT": cTk, "wada": wada, "vecs": vecs, "wflat": wflat, "wsT": wsT,
                        "wbd": wbd, "bsb": bsb, "lnb": lnb, "rel": rel, "oh": oh, "triu": triu,
                        "ident": np.eye(8, dtype=np.float32)})
    return in_maps


def gather_output(res, NS, ncores):
    outs = []
    for k in range(ncores):
        o = np.asarray(res.results[k]["ofm"])
        outs.append(o.transpose(0, 3, 2, 1).reshape(NS, NT, D))
    return np.concatenate(outs, axis=0).astype(np.float32)


def kernel(**inputs):
    NS = BATCH // NCORES
    in_maps = prepare_inputs(NS, DEPTH, NCORES, **inputs)
    nc = build_program(NS, DEPTH)
    res = run_bass_kernel_spmd(nc, in_maps, core_ids=list(range(NCORES)))
    return gather_output(res, NS, NCORES)
```

```python
import numpy as np
from contextlib import ExitStack, contextmanager
import concourse.bass as bass
import concourse.mybir as mybir
from concourse.bass_utils import run_bass_kernel_spmd

F32 = mybir.dt.float32
BF16 = mybir.dt.bfloat16
AF = mybir.ActivationFunctionType
ALU = mybir.AluOpType

D = 1024
KC = 8
NT = 2048
TG = 512
NTG = 4
DFF = 2816
NF = 22
AW = 768
IN_COLS = 8448
EPS = 1e-6
DEPTH = 4
BATCH = 32
NCORES = 8
SLOT = 3072
DILS = (1, 4, 16)
NGW = 385


def piece_table():
    names = []
    for hp in range(6):
        names.append((f"qkv{hp}", 3 * 8 * 128))
    for i in range(2):
        names.append((f"au{i}", 8 * 384))
    for i in range(2):
        names.append((f"av{i}", 4 * 768))
    for c in range(6):
        names.append((f"b{c}", 2 * 8 * 128))
    for c in range(8):
        names.append((f"gt{c}", 3 * 8 * 128))
        names.append((f"pp{c}", 3 * 6 * 128))
    for i, n in enumerate((384, 384, 256)):
        names.append((f"wo{i}", 8 * n))
    for f in range(NF):
        names.append((f"gu{f}", 2 * 8 * 128))
    for c in range(8):
        names.append((f"dn{c}", NF * 128))
    tab = {}
    off = 0
    for n, e in names:
        tab[n] = (off, e)
        off += 128 * e
    return tab, off


def _kpn(w, kc):
    n = w.shape[1]
    return w.reshape(kc, 128, n).transpose(1, 0, 2)


def host_wflat(l, w_in, p_a, p_b, p_c, w_out, f_wgate, f_wup, f_wdown):
    tab, tot = piece_table()
    out = np.empty((tot,), np.float32)

    def put(name, arr):
        off, e = tab[name]
        out[off:off + 128 * e] = np.ascontiguousarray(arr, dtype=np.float32).reshape(-1)

    wi = w_in[l]
    for hp in range(6):
        a = np.stack([_kpn(wi[:, 3072 + wch * 768 + hp * 128: 3072 + wch * 768 + (hp + 1) * 128], 8)
                      for wch in range(3)], axis=1)
        put(f"qkv{hp}", a)
    for i in range(2):
        put(f"au{i}", _kpn(wi[:, i * 384:(i + 1) * 384], 8))
    for i in range(2):
        put(f"av{i}", _kpn(wi[i * 512:(i + 1) * 512, 768:1536], 4))
    for c in range(6):
        a = np.stack([_kpn(wi[:, 1536 + c * 128:1536 + (c + 1) * 128], 8),
                      _kpn(wi[:, 2304 + c * 128:2304 + (c + 1) * 128], 8)], axis=1)
        put(f"b{c}", a)
    pp = (p_a[l], p_b[l], p_c[l])
    for c in range(8):
        a = np.stack([_kpn(wi[:, 5376 + X * 1024 + c * 128:5376 + X * 1024 + (c + 1) * 128], 8)
                      for X in range(3)], axis=1)
        put(f"gt{c}", a)
        a = np.stack([_kpn(pp[X][:, c * 128:(c + 1) * 128], 6) for X in range(3)], axis=1)
        put(f"pp{c}", a)
    c0 = 0
    for i, n in enumerate((384, 384, 256)):
        put(f"wo{i}", _kpn(w_out[l][:, c0:c0 + n], 8))
        c0 += n
    for f in range(NF):
        a = np.stack([_kpn(f_wgate[l][:, f * 128:(f + 1) * 128], 8),
                      _kpn(f_wup[l][:, f * 128:(f + 1) * 128], 8)], axis=1)
        put(f"gu{f}", a)
    for c in range(8):
        put(f"dn{c}", _kpn(f_wdown[l][:, c * 128:(c + 1) * 128], NF))
    return out


def vec_table(L):
    tab = {}
    off = 0
    for name, n in (("b_ada", L * 48), ("norm1", L * 8), ("norm2", L * 8), ("final", 8),
                    ("bcw", L * 4 * 6), ("bcb", L * 6), ("bba", L * 6), ("bbx", L * 6), ("blam", L * 6),
                    ("fcw", L * 3 * NF), ("fcb", L * NF)):
        tab[name] = off
        off += n
    return tab, off


def host_vecs(L, b_ada, norm1, norm2, final_norm, b_conv_w, b_conv_b, b_ba, b_bx, b_lam, f_conv_w, f_conv_b):
    tab, nv = vec_table(L)
    v = np.zeros((128, nv), np.float32)

    def fm(a):
        a = np.asarray(a, np.float32)
        lead = a.shape[:-1]
        c = a.shape[-1] // 128
        return a.reshape(lead + (c, 128)).reshape(-1, 128).T

    def put(name, a):
        m = fm(a)
        v[:, tab[name]:tab[name] + m.shape[1]] = m

    put("b_ada", b_ada[:L])
    put("norm1", norm1[:L])
    put("norm2", norm2[:L])
    put("final", final_norm)
    put("bcw", b_conv_w[:L])
    put("bcb", b_conv_b[:L])
    put("bba", b_ba[:L])
    put("bbx", b_bx[:L])
    put("blam", b_lam[:L])
    put("fcw", f_conv_w[:L])
    put("fcb", f_conv_b[:L])
    return v


def t5_bucket(dist):
    n_buckets, max_distance = 32, 2048
    max_exact = n_buckets // 2
    d = np.maximum(dist, 1).astype(np.float32)
    large = max_exact + (np.log(d / max_exact) / np.log(max_distance / max_exact)
                         * (n_buckets - max_exact)).astype(np.int32)
    large = np.minimum(large, n_buckets - 1)
    return np.where(dist < max_exact, dist, large).astype(np.int32)


def host_consts():
    oh = np.zeros((32, 3 * 129), np.float32)
    for ci, dil in enumerate(DILS):
        b = t5_bucket(np.arange(129) * dil)
        oh[b, ci * 129 + np.arange(129)] = 1.0
    triu = np.triu(np.ones((128, 128), np.float32))
    return oh, triu


class T:
    __slots__ = ("wr", "rd", "excl")

    def __init__(self, floor=None, excl=False):
        self.wr = {}
        self.rd = dict(floor) if floor else {}
        self.excl = excl


class Chan:
    def __init__(self, key, sem):
        self.key = key
        self.sem = sem
        self.count = 0


class Ctx:
    def __init__(self, nc, es):
        self.nc = nc
        self.es = es
        self.E = {"pe": nc.tensor, "act": nc.scalar, "dve": nc.vector, "pool": nc.gpsimd, "sp": nc.sync}
        self.semh = {}
        self.cnt = {}
        self.seen = {}
        for k in self.E:
            self.semh[k] = es.enter_context(nc.semaphore("s_" + k))
            self.cnt[k] = 0
            self.seen[k] = {}
        self.floor = {}
        self.uid = 0
        self.nwait = 0

    def newT(self):
        return T(self.floor)

    def chan(self, name):
        sem = self.es.enter_context(self.nc.semaphore("c_" + name))
        key = "c_" + name
        self.semh[key] = sem
        return Chan(key, sem)

    def _need(self, eng, key, val):
        if val <= 0:
            return
        if self.seen[eng].get(key, 0) < val:
            self.E[eng].wait_ge(self.semh[key], val)
            self.seen[eng][key] = val
            self.nwait += 1

    def _deps(self, eng, w, r):
        for t in r:
            for k, v in t.wr.items():
                self._need(eng, k, v)
            if t.excl:
                for k, v in t.rd.items():
                    if k != eng:
                        self._need(eng, k, v)
        for t in w:
            for k, v in t.wr.items():
                if k != eng:
                    self._need(eng, k, v)
            for k, v in t.rd.items():
                if k != eng:
                    self._need(eng, k, v)

    def op(self, eng, fn, w=(), r=()):
        self._deps(eng, w, r)
        inst = fn()
        self.cnt[eng] += 1
        c = self.cnt[eng]
        inst.then_inc(self.semh[eng], 1)
        for t in r:
            t.rd[eng] = c
        for t in w:
            t.wr[eng] = c
        return inst

    def dma(self, q, ch, out, in_, w=(), r=()):
        self._deps(q, w, r)
        inst = self.E[q].dma_start(out=out, in_=in_)
        ch.count += 16
        inst.then_inc(ch.sem, 16)
        for t in r:
            t.rd[ch.key] = ch.count
        for t in w:
            t.wr[ch.key] = ch.count
        return inst

    def snapshot_floor(self):
        self.floor = dict(self.cnt)

    def fence(self, eng, ts_, dummy):
        self._deps(eng, ts_, ts_)
        self.op(eng, lambda: self.E[eng].memset(dummy, 0.0), (), ())


def bc_last(ap, n):
    return bass.AP(ap.tensor, ap.offset, [list(x) for x in ap.ap] + [[0, n]])


def build_program(NS, L, dbg=False):
    nc = bass.Bass("TRN2", target_bir_lowering=False)
    ptab, WTOT = piece_table()
    vtab, NV = vec_table(L)

    xfm = nc.dram_tensor("xfm", [NS, 128, KC, NT], F32, kind="ExternalInput").ap()
    cT = nc.dram_tensor("cT", [128, KC, NS], F32, kind="ExternalInput").ap()
    wada = nc.dram_tensor("wada", [L, 24, 128, KC, 256], F32, kind="ExternalInput").ap()
    ident_d = nc.dram_tensor("ident", [8, 8], F32, kind="ExternalInput").ap()
    vecs = nc.dram_tensor("vecs", [128, NV], F32, kind="ExternalInput").ap()
    wflat = nc.dram_tensor("wflat", [L, WTOT], F32, kind="ExternalInput").ap()
    wsT_d = nc.dram_tensor("wsT", [L, 128, 12, 128], F32, kind="ExternalInput").ap()
    wbd_d = nc.dram_tensor("wbd", [L, 128, 2, 6, 128], F32, kind="ExternalInput").ap()
    bsb_d = nc.dram_tensor("bsb", [L, 128, 6, 128], F32, kind="ExternalInput").ap()
    lnb_d = nc.dram_tensor("lnb", [L, 128, AW], F32, kind="ExternalInput").ap()
    rel_d = nc.dram_tensor("rel", [32, 12], F32, kind="ExternalInput").ap()
    oh_d = nc.dram_tensor("oh", [32, 387], F32, kind="ExternalInput").ap()
    triu_d = nc.dram_tensor("triu", [128, 128], F32, kind="ExternalInput").ap()
    ofm = nc.dram_tensor("ofm", [NS, 128, KC, NT], F32, kind="ExternalOutput").ap()
    wbf_t = nc.dram_tensor("wbf", [L, WTOT], BF16, kind="Internal")
    wbf = wbf_t.ap()
    GROW = 36 * NGW
    gscr_t = nc.dram_tensor("gscr", [128, GROW], BF16, kind="Internal")
    gscr = gscr_t.ap()
    if dbg:
        dbg_d = nc.dram_tensor("dbg", [8, 128, KC, NT], F32, kind="ExternalOutput").ap()
        dbg2_d = nc.dram_tensor("dbg2", [128, 4096], F32, kind="ExternalOutput").ap()

    with ExitStack() as es:
        C = Ctx(nc, es)
        uid = [0]

        def sb(stack, shape, dt, name="t"):
            uid[0] += 1
            return stack.enter_context(nc.sbuf_tensor(f"{name}_{uid[0]}", list(shape), dt))

        @contextmanager
        def scope():
            with ExitStack() as s2:
                yield s2
            C.snapshot_floor()

        def mm(out, lhsT, rhs, start, stop, w, r):
            return C.op("pe", lambda: nc.tensor.matmul(out, lhsT, rhs, start=start, stop=stop), w, r)

        def act(out, in_, func, w, r, **kw):
            return C.op("act", lambda: nc.scalar.activation(out=out, in_=in_, func=func, **kw), w, r)

        def tt(eng, out, in0, in1, op, w, r):
            return C.op(eng, lambda: C.E[eng].tensor_tensor(out=out, in0=in0, in1=in1, op=op), w, r)

        def ts(eng, out, in0, s1, s2, op0, op1, w, r):
            if op1 is None:
                return C.op(eng, lambda: C.E[eng].tensor_scalar(out=out, in0=in0, scalar1=s1, scalar2=None, op0=op0), w, r)
            return C.op(eng, lambda: C.E[eng].tensor_scalar(out=out, in0=in0, scalar1=s1, scalar2=s2, op0=op0, op1=op1), w, r)

        def stt(out, in0, scalar, in1, op0, op1, w, r):
            return C.op("dve", lambda: nc.vector.scalar_tensor_tensor(out=out, in0=in0, scalar=scalar, in1=in1, op0=op0, op1=op1), w, r)

        def cp(eng, out, in_, w, r):
            return C.op(eng, lambda: C.E[eng].tensor_copy(out, in_), w, r)

        def mset(eng, ap, val, w):
            return C.op(eng, lambda: C.E[eng].memset(ap, val), w, ())

        psum = es.enter_context(nc.psum_tensor("psum", [128, 8, 512], F32))
        psT = [T(excl=True) for _ in range(8)]
        bank_rr = [0]
        bank_set = [list(range(8))]

        def bank():
            single_banks = bank_set[0]
            b = single_banks[bank_rr[0] % len(single_banks)]
            bank_rr[0] += 1
            return psum[:, b, :], psT[b]

        xn_sb = sb(es, [128, KC, NT], BF16, "xn")
        xnT = [C.newT() for _ in range(NTG)]
        oc_sb = sb(es, [128, 6, NT], BF16, "oc")
        ocT = C.newT()
        NSLOT = 3
        ring = [sb(es, [128, SLOT], BF16, "ring") for _ in range(NSLOT)]
        ringT = [C.newT() for _ in range(NSLOT)]
        ringC = [C.chan(f"ring{i}") for i in range(NSLOT)]
        ring_rr = [0]
        mskC = [C.chan(f"msk{i}") for i in range(2)]
        vec_sb = sb(es, [128, NV], F32, "vecs")
        vecT = C.newT()
        ada_sb = sb(es, [128, L, 48, NS], F32, "ada")
        adaT = C.newT()
        A_sb = sb(es, [128, L, 2, 8, NS], F32, "Asb")
        AT = C.newT()
        cl_sb = sb(es, [128, 2, L * 6], F32, "cl")
        clT = C.newT()
        ones_bf = sb(es, [128, 128], BF16, "ones")
        onesm_bf = sb(es, [128, 128], BF16, "onesm")
        onesT = C.newT()
        triu_sb = sb(es, [128, 128], F32, "triu")
        triuT = C.newT()
        sq_sb = [sb(es, [128, TG], BF16, "sq") for _ in range(2)]
        sqT = [C.newT() for _ in range(2)]
        rstd_sb = sb(es, [128, TG], F32, "rstd")
        rstdT = C.newT()
        ntmp_sb = [sb(es, [128, TG], F32, "ntmp") for _ in range(2)]
        ntmpT = [C.newT() for _ in range(2)]

        dummy_sb = sb(es, [128, 1], F32, "dummy")
        ROWS = WTOT // 2048
        RCH = 2048
        NCH = (ROWS + RCH - 1) // RCH
        wbfT = [[C.newT() for _ in range(NCH)] for _ in range(L)]
        gscrT = C.newT()
        ofmT = C.newT()
        constC = C.chan("const")
        precastC = [[C.chan(f"pc{i}_{j}") for j in range(NCH)] for i in range(L)]
        outC = C.chan("out")
        gscrC = C.chan("gscr")
        smallC = [C.chan(f"small{i}") for i in range(4)]
        wadaC = [C.chan(f"wada{i}") for i in range(2)]

        def vcol(name, idx):
            o = vtab[name] + idx
            return vec_sb[:, o:o + 1]

        def wpiece(l, name):
            off, e = ptab[name]
            i = ring_rr[0] % NSLOT
            ring_rr[0] += 1
            src = bass.AP(wbf_t, l * WTOT + off, [[e, 128], [1, e]])
            c0_, c1_ = off // (RCH * 2048), (off + 128 * e - 1) // (RCH * 2048)
            C.dma("sp", ringC[i], ring[i][:, 0:e], src, w=[ringT[i]], r=[wbfT[l][ci] for ci in range(c0_, c1_ + 1)])
            return ring[i], ringT[i]

        def precast_chunk(l, ci):
            if True:
                r0 = ci * RCH
                rn = min(RCH, ROWS - r0)
                src = bass.AP(wflat.tensor, l * WTOT + r0 * 2048, [[2048, rn], [1, 2048]])
                dst = bass.AP(wbf_t, l * WTOT + r0 * 2048, [[2048, rn], [1, 2048]])
                C.dma("pool", precastC[l][ci], dst, src, w=[wbfT[l][ci]])

        x_sb = sb(es, [128, KC, NT], F32, "x")
        xT = [C.newT() for _ in range(NTG)]
        xloadC = C.chan("xload")
        for c in range(KC):
            C.dma("pool", xloadC, x_sb[:, c, :], xfm[0, :, c, :], w=xT)
        for ci0 in range(NCH):
            precast_chunk(0, ci0)

        C.dma("sp", constC, vec_sb[:], vecs[:, :], w=[vecT])
        C.dma("sp", constC, triu_sb[:], triu_d[:, :], w=[triuT])
        mset("dve", ones_bf[:], 1.0, [onesT])
        mset("dve", onesm_bf[:], 1.0 / 1024.0, [onesT])

        with scope() as s0:
            ct_sb = sb(s0, [128, KC, NS], F32, "ct")
            ctT = C.newT()
            cond_sb = sb(s0, [128, KC, NS], F32, "cond")
            condT = C.newT()
            rel_sb = sb(s0, [32, 12], F32, "rel")
            relb_sb = sb(s0, [32, 12, 128], F32, "relb")
            oh_sb = sb(s0, [32, 387], F32, "oh")
            relT = C.newT()
            Gb = sb(s0, [128, 36, NGW], BF16, "Gb")
            GbT = C.newT()
            wa_sl = [sb(s0, [128, KC, 256], F32, "wasl") for _ in range(2)]
            ident_sb = sb(s0, [8, 8], F32, "ident")
            atm = [sb(s0, [8, 256], F32, "atm") for _ in range(2)]
            atmT = [C.newT() for _ in range(2)]
            waT = [C.newT() for _ in range(2)]
            tmpv = sb(s0, [128, L * 6], F32, "tmpv")
            tmpvT = C.newT()
            tmpa = sb(s0, [128, 8, NS], F32, "tmpa")
            tmpaT = C.newT()

            C.dma("sp", constC, ct_sb[:], cT[:, :, :], w=[ctT])
            C.dma("sp", constC, rel_sb[:], rel_d[:, :], w=[relT])
            C.dma("sp", constC, oh_sb[:], oh_d[:, :], w=[relT])
            C.dma("sp", constC, ident_sb[:], ident_d[:, :], w=[relT])
            for t_ in (vecT, triuT, ctT, relT):
                t_.wr[constC.key] = constC.count

            act(cond_sb[:], ct_sb[:], AF.Silu, [condT], [ctT])
            for l in range(L):
                for pc in range(24):
                    i = (l * 24 + pc) % 2
                    C.dma("sp", wadaC[i], wa_sl[i][:], wada[l, pc], w=[waT[i]])
                    pb, pt = bank()
                    for kc in range(KC):
                        mm(pb[0:NS, 0:256], cond_sb[:, kc, :], wa_sl[i][:, kc, :],
                           kc == 0, kc == KC - 1, [pt], [waT[i], condT])
                    cp("dve", atm[i][0:NS, :], pb[0:NS, 0:256], [atmT[i]], [pt])
                    for jj in range(2):
                        j = pc * 2 + jj
                        pb2, pt2 = bank()
                        mm(pb2[:, 0:NS], atm[i][0:NS, jj * 128:(jj + 1) * 128], ident_sb[0:NS, 0:NS], True, True,
                           [pt2], [atmT[i], relT])
                        ts("dve", ada_sb[:, l, j, :], pb2[:, 0:NS], vcol("b_ada", l * 48 + j), None, ALU.add, None,
                           [adaT], [pt2, vecT])
            for l in range(L):
                for wh, (part, nm) in enumerate(((1, "norm1"), (4, "norm2"))):
                    ts("dve", tmpa[:], ada_sb[:, l, part * 8:(part + 1) * 8, :], 1.0, None, ALU.add, None,
                       [tmpaT], [adaT])
                    o = vtab[nm] + l * 8
                    tt("dve", A_sb[:, l, wh, :, :], tmpa[:], bc_last(vec_sb[:, o:o + 8], NS), ALU.mult,
                       [AT], [tmpaT, vecT])
            o = vtab["blam"]
            act(tmpv[:], vec_sb[:, o:o + L * 6], AF.Exp, [tmpvT], [vecT], scale=-1.0)
            ts("dve", tmpv[:], tmpv[:], 1.0, None, ALU.add, None, [tmpvT], [tmpvT])
            act(tmpv[:], tmpv[:], AF.Ln, [tmpvT], [tmpvT])
            ts("dve", cl_sb[:, 0, :], tmpv[:], -8.0, None, ALU.mult, None, [clT], [tmpvT])
            ts("dve", cl_sb[:, 1, :], tmpv[:], -16.0, None, ALU.mult, None, [clT], [tmpvT])

            mset("pool", Gb[:], 0.0, [GbT])
            cp("dve", relb_sb[:], bc_last(rel_sb[:, :], 128), [relT], [relT])
            for h in range(12):
                pb, pt = bank()
                mm(pb[:, 0:387], relb_sb[:, h, :], oh_sb[:, :], True, True, [pt], [relT])
                act(Gb[:, h * 3:(h + 1) * 3, 128:257], pb[:, 0:387].rearrange("p (c d) -> p c d", c=3), AF.Exp,
                    [GbT], [pt])
            C.dma("sp", gscrC, gscr[:, :], Gb[:].rearrange("p a b -> p (a b)"), w=[gscrT], r=[GbT])
            C.fence("dve", [GbT], dummy_sb[:])

        def norm_tg(tg, Avec, shvec, dst_fn, dstT, final=False):
            tsl = slice(tg * TG, (tg + 1) * TG)
            pb, pt = bank()
            for c in range(KC):
                i = c % 2
                act(sq_sb[i][:], x_sb[:, c, tsl], AF.Square, [sqT[i]], [xT[tg]])
                mm(pb, onesm_bf[:], sq_sb[i][:], c == 0, c == KC - 1, [pt], [sqT[i], onesT])
            act(rstd_sb[:], pb, AF.Sqrt, [rstdT], [pt], bias=EPS)
            C.op("dve", lambda: nc.vector.reciprocal(rstd_sb[:], rstd_sb[:]), [rstdT], [rstdT])
            if dbg and not dbg_r[0]:
                dbg_r[0] = 1
                C.dma("sp", outC, dbg2_d[:, 3584:4096], rstd_sb[:], w=[ofmT], r=[rstdT])
            for c in range(KC):
                if final:
                    stt(x_sb[:, c, tsl], x_sb[:, c, tsl], Avec(c), rstd_sb[:], ALU.mult, ALU.mult,
                        [xT[tg]], [xT[tg], rstdT, vecT, AT])
                else:
                    i = c % 2
                    stt(ntmp_sb[i][:], x_sb[:, c, tsl], Avec(c), rstd_sb[:], ALU.mult, ALU.mult,
                        [ntmpT[i]], [xT[tg], rstdT, vecT, AT])
                    act(dst_fn(c), ntmp_sb[i][:], AF.Identity, [dstT], [ntmpT[i], adaT], bias=shvec(c), scale=1.0)


        def norm_steps(tg, Avec, shvec, dst_fn, dstT, final=False, nbank=7):
            tsl = slice(tg * TG, (tg + 1) * TG)
            pb, pt = psum[:, nbank, :], psT[nbank]

            def sqr(c):
                i = c % 2
                act(sq_sb[i][:], x_sb[:, c, tsl], AF.Square, [sqT[i]], [xT[tg]])

            def mmc(c):
                i = c % 2
                mm(pb, onesm_bf[:], sq_sb[i][:], c == 0, c == KC - 1, [pt], [sqT[i], onesT])

            def sttc(c):
                if final:
                    stt(x_sb[:, c, tsl], x_sb[:, c, tsl], Avec(c), rstd_sb[:], ALU.mult, ALU.mult,
                        [xT[tg]], [xT[tg], rstdT, vecT, AT])
                else:
                    i = c % 2
                    stt(ntmp_sb[i][:], x_sb[:, c, tsl], Avec(c), rstd_sb[:], ALU.mult, ALU.mult,
                        [ntmpT[i]], [xT[tg], rstdT, vecT, AT])

            def idc(c):
                if not final:
                    i = c % 2
                    act(dst_fn(c), ntmp_sb[i][:], AF.Identity, [dstT], [ntmpT[i], adaT], bias=shvec(c), scale=1.0)

            steps = []
            for j in range(5):
                def st_(j=j):
                    if j >= 1:
                        mmc(2 * j - 2)
                        mmc(2 * j - 1)
                    if j < 4:
                        sqr(2 * j)
                        sqr(2 * j + 1)
                steps.append(st_)
            steps.append(lambda: act(rstd_sb[:], pb, AF.Sqrt, [rstdT], [pt], bias=EPS))
            steps.append(lambda: C.op("dve", lambda: nc.vector.reciprocal(rstd_sb[:], rstd_sb[:]), [rstdT], [rstdT]))
            for j in range(5):
                def st2_(j=j):
                    if j >= 1:
                        idc(2 * j - 2)
                        idc(2 * j - 1)
                    if j < 4:
                        sttc(2 * j)
                        sttc(2 * j + 1)
                steps.append(st2_)
            return steps

        def tok(cfg, tile):
            dil = DILS[cfg]
            if cfg == 0:
                return slice(tile * 128, (tile + 1) * 128)
            if cfg == 1:
                r, n = tile // 4, tile % 4
                s0_ = n * 512 + r
                return slice(s0_, s0_ + 127 * 4 + 1, 4)
            r = tile
            return slice(r, r + 127 * 16 + 1, 16)

        dbg_i = [0]
        dbg_r = [0]
        marks = []

        def mark(name):
            marks.append((name, C.cnt["pe"]))

        def dump_x(s):
            if dbg and s == 0 and dbg_i[0] < 8:
                C.dma("sp", outC, dbg_d[dbg_i[0]], x_sb[:], w=[ofmT], r=xT)
                dbg_i[0] += 1

        def dump_bf(src, nchunk, rT):
            if dbg and dbg_i[0] < 8:
                with scope() as sd:
                    tmp = sb(sd, [128, NT], F32, "dbgt")
                    tT = C.newT()
                    for ch in range(nchunk):
                        cp("dve", tmp[:], src[:, ch, :], [tT], rT)
                        C.dma("sp", outC, dbg_d[dbg_i[0], :, ch, :], tmp[:], w=[ofmT], r=[tT])
                    C.fence("dve", [tT], dummy_sb[:])
                dbg_i[0] += 1

        if dbg:
            n1 = L * 48 * NS
            C.dma("sp", outC, dbg2_d[:, 0:n1], ada_sb[:].rearrange("p a b c -> p (a b c)"), w=[ofmT], r=[adaT])
            n2 = L * 16 * NS
            C.dma("sp", outC, dbg2_d[:, 1024:1024 + n2], A_sb[:].rearrange("p a b c d -> p (a b c d)"), w=[ofmT], r=[AT])
            C.dma("sp", outC, dbg2_d[:, 2048:2048 + 12 * L], cl_sb[:].rearrange("p a b -> p (a b)"), w=[ofmT], r=[clT])
            C.dma("sp", outC, dbg2_d[:, 2560:2560 + NV], vec_sb[:], w=[ofmT], r=[vecT])
        for s in range(NS):
            if s > 0:
                for c in range(KC):
                    C.dma("pool", xloadC, x_sb[:, c, :], xfm[s, :, c, :], w=xT)

            if dbg and s == 0:
                C.dma("sp", outC, dbg_d[7], x_sb[:], w=[ofmT], r=xT)
            for l in range(L):
                def adac(part, c):
                    return ada_sb[:, l, part * 8 + c, s:s + 1]

                if l == 0:
                    for tg in range(NTG):
                        norm_tg(tg, lambda c: A_sb[:, l, 0, c, s:s + 1], lambda c: adac(0, c),
                                lambda c, tg=tg: xn_sb[:, c, tg * TG:(tg + 1) * TG], xnT[tg])
                if s == 0 and l == 0:
                    dump_bf(xn_sb[:], 8, xnT)

                mark(f"s{s}l{l}.P2")
                def maybe_precast(k):
                    if s == 0 and l + 1 < L and k < NCH:
                        precast_chunk(l + 1, k)

                maybe_precast(0)
                bank_set[0] = [6, 7]
                with scope() as s2:
                    q_sb = sb(s2, [128, NT], BF16, "q")
                    k_sb = sb(s2, [128, NT], BF16, "k")
                    qT_, kT_ = C.newT(), C.newT()
                    V_sb = sb(s2, [128, 3, 16, 128], BF16, "V")
                    VT = C.newT()
                    acc = sb(s2, [128, 2, NT], F32, "acc")
                    accT = C.newT()
                    msk = [sb(s2, [128, 3, 2, 2, 128], BF16, "msk") for _ in range(2)]
                    mskT = [C.newT() for _ in range(2)]
                    pT = [sb(s2, [128, 2, 2, 128], BF16, "pT") for _ in range(3)]
                    pTT = [C.newT() for _ in range(3)]
                    for hp in range(6):
                        if hp == 2:
                            maybe_precast(1)
                        if hp == 4:
                            maybe_precast(2)
                        wq, wqT = wpiece(l, f"qkv{hp}")
                        wqv = wq[:, 0:3072].rearrange("p (a k n) -> p a k n", a=3, k=8)
                        mi = hp % 2
                        mo = (2 * hp) * 3 * NGW + 128
                        for h2m in range(2):
                            msrc = bass.AP(gscr_t, mo + h2m * 3 * NGW, [[GROW - 1, 128], [NGW, 3], [1, 256]])
                            C.dma("sp", mskC[mi], msk[mi][:, :, h2m, :, :].rearrange("p c j q -> p c (j q)"), msrc,
                                  w=[mskT[mi]], r=[gscrT])
                        for tg in range(NTG):
                            tsl = slice(tg * TG, (tg + 1) * TG)
                            pb, pt = bank()
                            for kc in range(KC):
                                mm(pb, wqv[:, 0, kc, :], xn_sb[:, kc, tsl], kc == 0, kc == KC - 1, [pt], [wqT, xnT[tg]])
                            act(q_sb[:, tsl], pb, AF.Copy, [qT_], [pt], scale=0.125)
                            pb, pt = bank()
                            for kc in range(KC):
                                mm(pb, wqv[:, 1, kc, :], xn_sb[:, kc, tsl], kc == 0, kc == KC - 1, [pt], [wqT, xnT[tg]])
                            act(k_sb[:, tsl], pb, AF.Copy, [kT_], [pt])
                        for cfg in range(3):
                            for grp in range(4):
                                pb, pt = bank()
                                pv = pb.rearrange("p (a n) -> p a n", a=4)
                                for ti in range(4):
                                    tile = grp * 4 + ti
                                    for kc in range(KC):
                                        mm(pv[:, ti, :], xn_sb[:, kc, tok(cfg, tile)], wqv[:, 2, kc, :],
                                           kc == 0, kc == KC - 1, [pt], [wqT] + xnT)
                                act(V_sb[:, cfg, grp * 4:(grp + 1) * 4, :], pv, AF.Copy, [VT], [pt])
                        blocks = []
                        for cfg in range(3):
                            for tile in range(16):
                                if cfg == 0:
                                    prev = tile - 1 if tile > 0 else None
                                elif cfg == 1:
                                    prev = tile - 1 if tile % 4 > 0 else None
                                else:
                                    prev = None
                                blocks.append((cfg, tile, prev))

                        def emit_S(bi):
                            cfg, tile, prev = blocks[bi]
                            nj = 2 if prev is not None else 1
                            pr = (bi % 3) * 2
                            pp2 = psum[:, pr:pr + 2, 0:256].rearrange("p h (j q) -> p h j q", j=2)
                            for h2 in range(2):
                                hs = slice(h2 * 64, (h2 + 1) * 64)
                                for j in range(nj):
                                    kt = tile if j == 0 else prev
                                    mm(pp2[:, h2, j, :], k_sb[hs, tok(cfg, kt)], q_sb[hs, tok(cfg, tile)], True, True,
                                       [psT[pr + h2]], [kT_, qT_])
                            pi = bi % 3
                            act(pT[pi][:, :, 0:nj, :], pp2[:, :, 0:nj, :], AF.Exp, [pTT[pi]], [psT[pr], psT[pr + 1]])
                            tt("pool" if bi % 2 == 1 else "dve", pT[pi][:, :, 0:nj, :], pT[pi][:, :, 0:nj, :],
                               msk[mi][:, cfg, :, 0:nj, :], ALU.mult, [pTT[pi]], [pTT[pi], mskT[mi]])

                        def emit_PV(bi):
                            cfg, tile, prev = blocks[bi]
                            nj = 2 if prev is not None else 1
                            pi = bi % 3
                            pb, pt = bank()
                            po = pb[:, 0:256].rearrange("p (a q) -> p a q", a=2)
                            for h2 in range(2):
                                hs = slice(h2 * 64, (h2 + 1) * 64)
                                for j in range(nj):
                                    kt = tile if j == 0 else prev
                                    mm(po[hs, 0, :], V_sb[:, cfg, kt, hs], pT[pi][:, h2, j, :], j == 0, j == nj - 1,
                                       [pt], [VT, pTT[pi]])
                                for j in range(nj):
                                    mm(po[hs, 1, :], ones_bf[:, 0:64], pT[pi][:, h2, j, :], j == 0, j == nj - 1,
                                       [pt], [onesT, pTT[pi]])
                            dst = acc[:, :, tok(cfg, tile)]
                            if cfg == 0:
                                cp("dve", dst, po, [accT], [pt])
                            else:
                                tt("dve", dst, dst, po, ALU.add, [accT], [accT, pt])

                        emit_S(0)
                        emit_S(1)
                        for bi in range(len(blocks)):
                            if bi + 2 < len(blocks):
                                emit_S(bi + 2)
                            emit_PV(bi)
                        C.op("dve", lambda: nc.vector.reciprocal(acc[:, 1, :], acc[:, 1, :]), [accT], [accT])
                        tt("dve", oc_sb[:, hp, :], acc[:, 0, :], acc[:, 1, :], ALU.mult, [ocT], [accT])
                bank_set[0] = list(range(8))
                if s == 0 and l == 0:
                    dump_bf(oc_sb[:], 6, [ocT])

                mark(f"s{s}l{l}.P3")
                with scope() as s3:
                    oa_sb = sb(s3, [128, 6, TG], BF16, "oa")
                    ob_sb = sb(s3, [128, 6, TG], BF16, "ob")
                    oaT, obT = C.newT(), C.newT()
                    WmT = sb(s3, [128, 12, 128], BF16, "WmT")
                    WmTT = C.newT()
                    wbd = sb(s3, [128, 2, 6, 128], BF16, "wbd")
                    wbdT = C.newT()
                    bsb = sb(s3, [128, 6, 128], F32, "bsb")
                    bsbT = C.newT()
                    lnb = sb(s3, [128, AW], F32, "lnb")
                    lnbT = C.newT()
                    xh = sb(s3, [128, 6, 3], F32, "xh")
                    hc = sb(s3, [128, 6], F32, "hc")
                    carT = C.newT()
                    C.dma("pool", smallC[0], WmT[:], wsT_d[l], w=[WmTT])
                    C.dma("pool", smallC[1], wbd[:], wbd_d[l], w=[wbdT])
                    C.dma("sp", smallC[2], bsb[:], bsb_d[l], w=[bsbT])
                    C.dma("sp", smallC[3], lnb[:], lnb_d[l], w=[lnbT])
                    tt("pool", WmT[:], WmT[:], bass.AP(triu_sb[:].tensor, triu_sb[:].offset,
                                                       [list(triu_sb[:].ap[0]), [0, 12], [1, 128]]),
                       ALU.mult, [WmTT], [WmTT, triuT])
                    for tg in range(NTG):
                        tsl = slice(tg * TG, (tg + 1) * TG)
                        if tg == 0:
                            maybe_precast(3)
                        if tg == 2:
                            maybe_precast(4)
                        mark(f"s{s}l{l}.A{tg}")
                        with scope() as sa:
                            u_sb = sb(sa, [128, 6, TG], BF16, "u")
                            uT = C.newT()
                            gv = [sb(sa, [128, AW], F32, "gv") for _ in range(4)]
                            gvT = [C.newT() for _ in range(4)]
                            v_sb = [sb(sa, [128, AW], BF16, "v") for _ in range(2)]
                            vT = [C.newT() for _ in range(2)]
                            st = sb(sa, [128, 4, 2, 6], F32, "st")
                            mv = sb(sa, [128, 4, 2], F32, "mv")
                            rs = sb(sa, [128, 4], F32, "rs")
                            stT = C.newT()
                            atmp = [sb(sa, [128, 3, 128], F32, "atmp")] * 2
                            atmpT = [C.newT()] * 2
                            for i in range(2):
                                au, auT = wpiece(l, f"au{i}")
                                auv = au[:, 0:3072].rearrange("p (k n) -> p k n", k=8)
                                for cc in range(3):
                                    c = i * 3 + cc
                                    pb, pt = bank()
                                    for kc in range(KC):
                                        mm(pb, auv[:, kc, cc * 128:(cc + 1) * 128], xn_sb[:, kc, tsl], kc == 0, kc == KC - 1,
                                           [pt], [auT, xnT[tg]])
                                    act(u_sb[:, c, :], pb, AF.Gelu_apprx_tanh, [uT], [pt])
                            av0, av0T = wpiece(l, "av0")
                            av1, av1T = wpiece(l, "av1")
                            avv = [av0[:, 0:3072].rearrange("p (k n) -> p k n", k=4),
                                   av1[:, 0:3072].rearrange("p (k n) -> p k n", k=4)]
                            for ti in range(4):
                                tile = tg * 4 + ti
                                tks = slice(tile * 128, (tile + 1) * 128)
                                for half in range(2):
                                    pb, pt = bank()
                                    for kc in range(KC):
                                        mm(pb[:, 0:384], xn_sb[:, kc, tks], avv[kc // 4][:, kc % 4, half * 384:(half + 1) * 384],
                                           kc == 0, kc == KC - 1, [pt], [av0T, av1T, xnT[tg]])
                                    act(gv[ti][:, half * 384:(half + 1) * 384], pb[:, 0:384], AF.Gelu_apprx_tanh, [gvT[ti]], [pt])
                                    C.op("dve", lambda half=half, ti=ti: nc.vector.bn_stats(st[:, ti, half, :], gv[ti][:, half * 384:(half + 1) * 384]),
                                         [stT], [gvT[ti]])
                                C.op("dve", lambda ti=ti: nc.vector.bn_aggr(mv[:, ti, :], st[:, ti, :, :].rearrange("p a b -> p (a b)")), [stT], [stT])
                            act(rs[:], mv[:, :, 1], AF.Sqrt, [stT], [stT], bias=EPS)
                            C.op("dve", lambda: nc.vector.reciprocal(rs[:], rs[:]), [stT], [stT])
                            for ti in range(4):
                                b2 = ti % 2
                                ts("dve", gv[ti][:], gv[ti][:], mv[:, ti, 0:1], rs[:, ti:ti + 1], ALU.subtract, ALU.mult,
                                   [gvT[ti]], [gvT[ti], stT])
                                tt("dve", v_sb[b2][:], gv[ti][:], lnb[:], ALU.mult, [vT[b2]], [gvT[ti], lnbT])
                                for half in range(2):
                                    pb, pt = bank()
                                    for cc in range(3):
                                        c = half * 3 + cc
                                        for g2 in range(2):
                                            g = 2 * c + g2
                                            mm(pb[g2 * 64:(g2 + 1) * 64, cc * 128:(cc + 1) * 128], v_sb[b2][:, g * 64:(g + 1) * 64],
                                               WmT[:, g, :], True, True, [pt], [vT[b2], WmTT])
                                    tt("dve", atmp[half][:], pb[:, 0:384].rearrange("p (a t) -> p a t", a=3),
                                       bsb[:, half * 3:(half + 1) * 3, :], ALU.add, [atmpT[half]], [pt, bsbT])
                                    tt("pool", oa_sb[:, half * 3:(half + 1) * 3, ti * 128:(ti + 1) * 128], atmp[half][:],
                                       u_sb[:, half * 3:(half + 1) * 3, ti * 128:(ti + 1) * 128], ALU.mult,
                                       [oaT], [atmpT[half], uT])
                        mark(f"s{s}l{l}.B{tg}")
                        with scope() as sbb:
                            xbuf = [sb(sbb, [128, TG + 3], F32, "xbuf") for _ in range(2)]
                            xb = [sb(sbb, [128, TG], F32, "xb") for _ in range(2)]
                            xbb = [sb(sbb, [128, TG], BF16, "xbb") for _ in range(2)]
                            rbuf = [sb(sbb, [128, TG], F32, "rbuf") for _ in range(2)]
                            ibuf = [sb(sbb, [128, TG], F32, "ibuf") for _ in range(2)]
                            mbuf = [sb(sbb, [128, TG], F32, "mbuf") for _ in range(2)]
                            xbufT = [C.newT() for _ in range(2)]
                            xbT = [C.newT() for _ in range(2)]
                            xbbT = [C.newT() for _ in range(2)]
                            rT_ = [C.newT() for _ in range(2)]
                            iT_ = [C.newT() for _ in range(2)]
                            mT_ = [C.newT() for _ in range(2)]
                            ocw = vtab["bcw"] + l * 24
                            for cp_ in range(3):
                                pair = (2 * cp_, 2 * cp_ + 1)
                                st = {}
                                for c in pair:
                                    i = c % 2
                                    bw, bwT = wpiece(l, f"b{c}")
                                    bwv = bw[:, 0:2048].rearrange("p (a k n) -> p a k n", a=2, k=8)
                                    psx, psxT = bank()
                                    for kc in range(KC):
                                        mm(psx, bwv[:, 0, kc, :], xn_sb[:, kc, tsl], kc == 0, kc == KC - 1, [psxT], [bwT, xnT[tg]])
                                    psg, psgT = bank()
                                    for kc in range(KC):
                                        mm(psg, bwv[:, 1, kc, :], xn_sb[:, kc, tsl], kc == 0, kc == KC - 1, [psgT], [bwT, xnT[tg]])
                                    if tg == 0:
                                        mset("dve", xbuf[i][:, 0:3], 0.0, [xbufT[i]])
                                    else:
                                        cp("dve", xbuf[i][:, 0:3], xh[:, c, :], [xbufT[i]], [carT])
                                    act(xbuf[i][:, 3:TG + 3], psx, AF.Copy, [xbufT[i]], [psxT])
                                    act(xb[i][:], psx, AF.Identity, [xbT[i]], [psxT, vecT],
                                        bias=vcol("bcb", l * 6 + c), scale=vec_sb[:, ocw + 18 + c:ocw + 18 + c + 1])
                                    cp("dve", xh[:, c, :], xbuf[i][:, TG:TG + 3], [carT], [xbufT[i]])
                                    for k in range(0, 3):
                                        stt(xb[i][:], xbuf[i][:, k:k + TG], vec_sb[:, ocw + k * 6 + c:ocw + k * 6 + c + 1], xb[i][:],
                                            ALU.mult, ALU.add, [xbT[i]], [xbufT[i], xbT[i], vecT])
                                    act(xbb[i][:], xb[i][:], AF.Copy, [xbbT[i]], [xbT[i]])
                                    psr, psrT = bank()
                                    mm(psr, wbd[:, 0, c, :], xbb[i][:], True, True, [psrT], [wbdT, xbbT[i]])
                                    psi, psiT = bank()
                                    mm(psi, wbd[:, 1, c, :], xbb[i][:], True, True, [psiT], [wbdT, xbbT[i]])
                                    st[c] = (psg, psgT, psr, psrT, psi, psiT)
                                for c in pair:
                                    i = c % 2
                                    psg, psgT, psr, psrT, psi, psiT = st[c]
                                    act(rbuf[i][:], psr, AF.Sigmoid, [rT_[i]], [psrT, vecT], bias=vcol("bba", l * 6 + c), scale=1.0)
                                    act(ibuf[i][:], psi, AF.Sigmoid, [iT_[i]], [psiT, vecT], bias=vcol("bbx", l * 6 + c), scale=1.0)
                                for c in pair:
                                    i = c % 2
                                    act(mbuf[i][:], rbuf[i][:], AF.Exp, [mT_[i]], [rT_[i], clT], scale=cl_sb[:, 1, l * 6 + c:l * 6 + c + 1])
                                    act(rbuf[i][:], rbuf[i][:], AF.Exp, [rT_[i]], [rT_[i], clT], scale=cl_sb[:, 0, l * 6 + c:l * 6 + c + 1])
                                    ts("pool", mbuf[i][:], mbuf[i][:], -1.0, 1.0, ALU.mult, ALU.add, [mT_[i]], [mT_[i]])
                                    tt("pool", ibuf[i][:], ibuf[i][:], xb[i][:], ALU.mult, [iT_[i]], [iT_[i], xbT[i]])
                                for c in pair:
                                    i = c % 2
                                    act(mbuf[i][:], mbuf[i][:], AF.Sqrt, [mT_[i]], [mT_[i]])
                                    if tg == 0:
                                        mset("dve", mbuf[i][:, 0:1], 1.0, [mT_[i]])
                                    tt("dve", ibuf[i][:], ibuf[i][:], mbuf[i][:], ALU.mult, [iT_[i]], [iT_[i], mT_[i]])
                                    init = 0.0 if tg == 0 else hc[:, c:c + 1]
                                    C.op("dve", lambda init=init, i=i: nc.vector.tensor_tensor_scan(mbuf[i][:], rbuf[i][:], ibuf[i][:], init, ALU.mult, ALU.add),
                                         [mT_[i]], [mT_[i], rT_[i], iT_[i], carT])
                                    cp("dve", hc[:, c:c + 1], mbuf[i][:, TG - 1:TG], [carT], [mT_[i]])
                                for c in pair:
                                    i = c % 2
                                    psg, psgT, psr, psrT, psi, psiT = st[c]
                                    act(xb[i][:], psg, AF.Gelu_apprx_tanh, [xbT[i]], [psgT])
                                    tt("pool", ob_sb[:, c, :], mbuf[i][:], xb[i][:], ALU.mult, [obT], [mT_[i], xbT[i]])
                        mark(f"s{s}l{l}.M{tg}")
                        with scope() as sm:
                            gate = sb(sm, [128, 3, TG], F32, "gate")
                            gateT = C.newT()
                            mg = sb(sm, [128, 8, TG], BF16, "mg")
                            mgT = [C.newT() for _ in range(8)]
                            t0 = sb(sm, [128, TG], F32, "t0")
                            t1 = sb(sm, [128, TG], F32, "t1")
                            t0T, t1T = C.newT(), C.newT()
                            for c in range(8):
                                gt, gtT = wpiece(l, f"gt{c}")
                                gtv = gt[:, 0:3072].rearrange("p (a k n) -> p a k n", a=3, k=8)
                                for X in range(3):
                                    pb, pt = bank()
                                    for kc in range(KC):
                                        mm(pb, gtv[:, X, kc, :], xn_sb[:, kc, tsl], kc == 0, kc == KC - 1, [pt], [gtT, xnT[tg]])
                                    act(gate[:, X, :], pb, AF.Sigmoid, [gateT], [pt])
                                pw, pwT = wpiece(l, f"pp{c}")
                                pwv = pw[:, 0:2304].rearrange("p (a k n) -> p a k n", a=3, k=6)
                                srcs = ((lambda k: oa_sb[:, k, :], oaT), (lambda k: ob_sb[:, k, :], obT),
                                        (lambda k: oc_sb[:, k, tsl], ocT))
                                for X in range(3):
                                    pb, pt = bank()
                                    fsrc, sT_ = srcs[X]
                                    for k in range(6):
                                        mm(pb, pwv[:, X, k, :], fsrc(k), k == 0, k == 5, [pt], [pwT, sT_])
                                    if X == 0:
                                        tt("dve", t0[:], pb, gate[:, 0, :], ALU.mult, [t0T], [pt, gateT])
                                    elif X == 1:
                                        tt("dve", t1[:], pb, gate[:, 1, :], ALU.mult, [t1T], [pt, gateT])
                                        tt("pool", t0[:], t0[:], t1[:], ALU.add, [t0T], [t0T, t1T])
                                    else:
                                        tt("dve", t1[:], pb, gate[:, 2, :], ALU.mult, [t1T], [pt, gateT])
                                        tt("pool", mg[:, c, :], t0[:], t1[:], ALU.add, [mgT[c]], [t0T, t1T])
                            c = 0
                            for i, n in enumerate((384, 384, 256)):
                                wo, woT = wpiece(l, f"wo{i}")
                                wov = wo[:, 0:8 * n].rearrange("p (k n) -> p k n", k=8)
                                for cc in range(n // 128):
                                    pb, pt = bank()
                                    for k in range(8):
                                        mm(pb, wov[:, k, cc * 128:(cc + 1) * 128], mg[:, k, :], k == 0, k == 7, [pt], [woT, mgT[k]])
                                    stt(x_sb[:, c, tsl], pb, adac(2, c), x_sb[:, c, tsl], ALU.mult, ALU.add,
                                        [xT[tg]], [pt, xT[tg], adaT])
                                    c += 1
                dump_x(s) if l == 0 else None

                maybe_precast(5)
                maybe_precast(6)
                mark(f"s{s}l{l}.P4")
                with scope() as s4:
                    xn2b = [sb(s4, [128, KC, TG], BF16, "xn2") for _ in range(2)]
                    xn2Tb = [C.newT() for _ in range(2)]
                    gbuf = [sb(s4, [128, TG + 2], F32, "gbuf") for _ in range(2)]
                    gbufT = [C.newT() for _ in range(2)]
                    tb = [sb(s4, [128, TG], F32, "tb") for _ in range(2)]
                    tbT = [C.newT() for _ in range(2)]
                    ub = [sb(s4, [128, TG], BF16, "ub") for _ in range(2)]
                    ubT = [C.newT() for _ in range(2)]
                    hbuf = sb(s4, [128, NF, TG], BF16, "hbuf")
                    hbufT = [C.newT() for _ in range(NF)]
                    fh = sb(s4, [128, NF, 2], F32, "fh")
                    fhT = C.newT()
                    def next_norm(tgn, stepped):
                        fn = norm_steps if stepped else norm_tg
                        if l + 1 < L:
                            return fn(tgn, lambda c: A_sb[:, l + 1, 0, c, s:s + 1],
                                      lambda c: ada_sb[:, l + 1, c, s:s + 1],
                                      lambda c: xn_sb[:, c, tgn * TG:(tgn + 1) * TG], xnT[tgn])
                        return fn(tgn, lambda c: vcol("final", c), None, None, None, final=True)

                    bank_set[0] = list(range(7))

                    for tg in range(NTG):
                        tsl = slice(tg * TG, (tg + 1) * TG)
                        xn2, xn2T = xn2b[tg % 2], xn2Tb[tg % 2]
                        if tg == 0:
                            norm_tg(0, lambda c: A_sb[:, l, 1, c, s:s + 1], lambda c: adac(3, c),
                                    lambda c: xn2b[0][:, c, :], xn2Tb[0])
                        sched = {}
                        if tg + 1 < NTG:
                            n2 = norm_steps(tg + 1, lambda c: A_sb[:, l, 1, c, s:s + 1], lambda c: adac(3, c),
                                            lambda c, nb=(tg + 1) % 2: xn2b[nb][:, c, :], xn2Tb[(tg + 1) % 2])
                            for k_, st_ in enumerate(n2):
                                sched.setdefault(k_, []).append(st_)
                        if tg > 0:
                            nn = next_norm(tg - 1, True)
                            for k_, st_ in enumerate(nn):
                                sched.setdefault(10 + k_, []).append(st_)
                        for f in range(NF):
                            i = f % 2
                            for st_ in sched.get(f, []):
                                st_()
                            gu, guT = wpiece(l, f"gu{f}")
                            guv = gu[:, 0:2048].rearrange("p (a k n) -> p a k n", a=2, k=8)
                            psg, psgT = bank()
                            for kc in range(KC):
                                mm(psg, guv[:, 0, kc, :], xn2[:, kc, :], kc == 0, kc == KC - 1, [psgT], [guT, xn2T])
                            psu, psuT = bank()
                            for kc in range(KC):
                                mm(psu, guv[:, 1, kc, :], xn2[:, kc, :], kc == 0, kc == KC - 1, [psuT], [guT, xn2T])
                            if tg == 0:
                                mset("dve", gbuf[i][:, 0:2], 0.0, [gbufT[i]])
                            else:
                                cp("dve", gbuf[i][:, 0:2], fh[:, f, :], [gbufT[i]], [fhT])
                            o = vtab["fcw"] + l * 3 * NF
                            act(gbuf[i][:, 2:TG + 2], psg, AF.Copy, [gbufT[i]], [psgT])
                            act(tb[i][:], psg, AF.Identity, [tbT[i]], [psgT, vecT],
                                bias=vcol("fcb", l * NF + f), scale=vec_sb[:, o + 2 * NF + f:o + 2 * NF + f + 1])
                            cp("dve", fh[:, f, :], gbuf[i][:, TG:TG + 2], [fhT], [gbufT[i]])
                            for k in range(0, 2):
                                stt(tb[i][:], gbuf[i][:, k:k + TG], vec_sb[:, o + k * NF + f:o + k * NF + f + 1], tb[i][:],
                                    ALU.mult, ALU.add, [tbT[i]], [gbufT[i], tbT[i], vecT])
                            act(tb[i][:], tb[i][:], AF.Gelu_apprx_tanh, [tbT[i]], [tbT[i]])
                            act(ub[i][:], psu, AF.Copy, [ubT[i]], [psuT])
                            tt("pool", hbuf[:, f, :], tb[i][:], ub[i][:], ALU.mult, [hbufT[f]], [ubT[i], tbT[i]])
                        for c in range(8):
                            dn, dnT = wpiece(l, f"dn{c}")
                            dnv = dn[:, 0:NF * 128].rearrange("p (k n) -> p k n", k=NF)
                            pb, pt = bank()
                            for f in range(NF):
                                mm(pb, dnv[:, f, :], hbuf[:, f, :], f == 0, f == NF - 1, [pt], [dnT, hbufT[f]])
                            stt(x_sb[:, c, tsl], pb, adac(5, c), x_sb[:, c, tsl], ALU.mult, ALU.add,
                                [xT[tg]], [pt, xT[tg], adaT])
                        if tg == NTG - 1:
                            next_norm(tg, False)
                    bank_set[0] = list(range(8))
                dump_x(s) if l == 0 else None

            C.dma("sp", outC, ofm[s], x_sb[:], w=[ofmT], r=xT)

        C._need("sp", outC.key, outC.count)
        mark("end")
        build_program.stats = dict(C.cnt, nwait=C.nwait)
        build_program.marks = marks
    return nc


def prepare_inputs(NS, L, ncores, x, c, w_ada, b_ada, norm1, w_in, a_ln, a_ws, a_bs, b_conv_w, b_conv_b, b_wa, b_ba,
                   b_wx, b_bx, b_lam, rel_table, p_a, p_b, p_c, w_out, norm2, f_wgate, f_wup,
                   f_conv_w, f_conv_b, f_wdown, final_norm):
    f32 = np.float32
    x = np.asarray(x, f32)
    c = np.asarray(c, f32)
    args = [np.asarray(a, f32) for a in (w_in, p_a, p_b, p_c, w_out, f_wgate, f_wup, f_wdown)]
    wflat = np.stack([host_wflat(l, *args) for l in range(L)], axis=0)
    w_ada = np.asarray(w_ada, f32)
    wada = np.ascontiguousarray(w_ada[:L].reshape(L, KC, 128, 24, 256).transpose(0, 3, 2, 1, 4))
    vecs = host_vecs(L, np.asarray(b_ada, f32), np.asarray(norm1, f32), np.asarray(norm2, f32),
                     np.asarray(final_norm, f32), np.asarray(b_conv_w, f32), np.asarray(b_conv_b, f32),
                     np.asarray(b_ba, f32), np.asarray(b_bx, f32), np.asarray(b_lam, f32),
                     np.asarray(f_conv_w, f32), np.asarray(f_conv_b, f32))
    a_ws = np.asarray(a_ws, f32)
    wsT = np.ascontiguousarray(a_ws[:L].transpose(0, 3, 1, 2))
    b_wa = np.asarray(b_wa, f32)
    b_wx = np.asarray(b_wx, f32)
    wbd = np.zeros((L, 128, 2, 6, 128), f32)
    for wi_, wsrc in enumerate((b_wa, b_wx)):
        for cch in range(6):
            for g2 in range(2):
                wbd[:, g2 * 64:(g2 + 1) * 64, wi_, cch, g2 * 64:(g2 + 1) * 64] = wsrc[:L, 2 * cch + g2]
    a_bs = np.asarray(a_bs, f32)
    bsb = np.ascontiguousarray(np.repeat(a_bs[:L].reshape(L, 6, 2, 1, 128), 64, axis=3).reshape(L, 6, 128, 128)
                               .transpose(0, 2, 1, 3))
    lnb = np.ascontiguousarray(np.broadcast_to(np.asarray(a_ln, f32)[:L, None, :], (L, 128, AW)))
    oh, triu = host_consts()
    rel = np.ascontiguousarray(np.asarray(rel_table, f32))
    in_maps = []
    for k in range(ncores):
        xs = x[k * NS:(k + 1) * NS]
        xfm = np.ascontiguousarray(xs.reshape(NS, NT, KC, 128).transpose(0, 3, 2, 1))
        cs = c[k * NS:(k + 1) * NS]
        cTk = np.ascontiguousarray(cs.reshape(NS, KC, 128).transpose(2, 1, 0))
        in_maps.append({"xfm": xfm, "cT": cTk, "wada": wada, "vecs": vecs, "wflat": wflat, "wsT": wsT,
                        "wbd": wbd, "bsb": bsb, "lnb": lnb, "rel": rel, "oh": oh, "triu": triu,
                        "ident": np.eye(8, dtype=np.float32)})
    return in_maps


def gather_output(res, NS, ncores):
    outs = []
    for k in range(ncores):
        o = np.asarray(res.results[k]["ofm"])
        outs.append(o.transpose(0, 3, 2, 1).reshape(NS, NT, D))
    return np.concatenate(outs, axis=0).astype(np.float32)


def kernel(**inputs):
    NS = BATCH // NCORES
    in_maps = prepare_inputs(NS, DEPTH, NCORES, **inputs)
    nc = build_program(NS, DEPTH)
    res = run_bass_kernel_spmd(nc, in_maps, core_ids=list(range(NCORES)))
    return gather_output(res, NS, NCORES)
```
